# Optimizing a Trainium2 kernel written in Bass

```python
import math
import jax
import jax.numpy as jnp
from jax import lax
import numpy as np

D_MODEL = 1024
BATCH = 16
SEQ = 2048
DEPTH = 2

GRID_W = 64
CTX_LEN = 256

DA_HEADS = 4
DA_QK_DIM = 64
DA_V_DIM = 2 * DA_QK_DIM
DA_WIDTH = DA_HEADS * DA_V_DIM
DA_QK_COLS = DA_HEADS * 2 * DA_QK_DIM
Q_BLOCK = 128
ROPE_BASE = 10000.0
DA_EPS = 1e-5

RW_HEAD = 64
RW_HEADS = 8
RW_WIDTH = RW_HEADS * RW_HEAD
RW_DECAY_LORA = 64
RW_AAA_LORA = 64
RW_GATE_LORA = 128
RW_COLS = 3 * RW_WIDTH + RW_DECAY_LORA + RW_AAA_LORA + RW_GATE_LORA
RW_SPLITS = (RW_WIDTH, 2 * RW_WIDTH, 3 * RW_WIDTH, 3 * RW_WIDTH + RW_DECAY_LORA,
             3 * RW_WIDTH + RW_DECAY_LORA + RW_AAA_LORA)
RW_GN_EPS = 64e-5

SG_CHUNK = 128
SG_GROUPS = 4
SG_WIDTH = 512
SG_GROUP_DIM = SG_WIDTH // SG_GROUPS

N_BRANCH = 3
MIX_WIDTH = DA_WIDTH + RW_WIDTH + SG_WIDTH
DA_Q0 = 0
DA_K0 = DA_Q0 + DA_QK_COLS
DA_V0 = DA_K0 + DA_QK_COLS
RW_0 = DA_V0 + DA_WIDTH
SG_0 = RW_0 + RW_COLS
GATE_0 = SG_0 + 2 * SG_WIDTH
IN_COLS = GATE_0 + N_BRANCH * D_MODEL

N_EXPERTS = 16
EXPERT_FF = 2048
EC_CAPACITY = 2

LN_EPS = 1e-5
DN_ALPHA = (2 * DEPTH) ** 0.25
DN_BETA = (8 * DEPTH) ** -0.25

kernel_name = 'hybrid_diffattn_rwkv7_chunkgmlp_ecmoe_trunk'


def layer_norm(z, g, b, eps=LN_EPS):
    zf = z.astype(jnp.float32)
    mu = jnp.mean(zf, -1, keepdims=True)
    var = jnp.mean(jnp.square(zf - mu), -1, keepdims=True)
    return ((zf - mu) * lax.rsqrt(var + eps) * g + b).astype(z.dtype)


def rms_norm(z, g, eps):
    zf = z.astype(jnp.float32)
    return (zf * lax.rsqrt(jnp.mean(jnp.square(zf), -1, keepdims=True) + eps) * g).astype(z.dtype)


def axial_rope_tables(n_tokens):
    rows = n_tokens // GRID_W
    row = jnp.repeat(jnp.arange(rows, dtype=jnp.float32), GRID_W)
    col = jnp.tile(jnp.arange(GRID_W, dtype=jnp.float32), rows)
    half = DA_QK_DIM // 2
    inv_freq = ROPE_BASE ** (-jnp.arange(0, half, 2, dtype=jnp.float32) / half)
    ar = row[:, None] * inv_freq
    ac = col[:, None] * inv_freq
    ang = jnp.concatenate([ar, ar, ac, ac], axis=-1)
    return jnp.cos(ang), jnp.sin(ang)


def rotate_half(z):
    z1, z2 = jnp.split(z, 2, axis=-1)
    return jnp.concatenate([-z2, z1], axis=-1)


def apply_axial_rope(z, cos, sin):
    zr, zc = jnp.split(z, 2, axis=-1)
    rot = jnp.concatenate([rotate_half(zr), rotate_half(zc)], axis=-1)
    cb = cos[None, :, None, None, :]
    sb = sin[None, :, None, None, :]
    return (z * cb + rot * sb).astype(z.dtype)


def diff_lambda(lam, lam_init):
    lf = lam.astype(jnp.float32)
    return jnp.exp(jnp.sum(lf[0] * lf[1])) - jnp.exp(jnp.sum(lf[2] * lf[3])) + lam_init


def diff_attn_probs(q, k, lam):
    s = jnp.einsum('bqhmd,bkhmd->bhmqk', q, k).astype(jnp.float32)
    p = jax.nn.softmax(s, axis=-1)
    return p[:, :, 0] - lam * p[:, :, 1]


def diff_attn_finish(o, norm_g, lam_init):
    B, T = o.shape[:2]
    return (rms_norm(o, norm_g, DA_EPS) * (1.0 - lam_init)).reshape(B, T, DA_WIDTH)


def diff_attention_latent(q, k, v, kc, vc, lam, norm_g, lam_init):
    B, T = q.shape[:2]
    k_all = jnp.concatenate([kc, k], axis=1)
    v_all = jnp.concatenate([vc, v], axis=1)
    nb = T // Q_BLOCK
    qb = q.reshape(B, nb, Q_BLOCK, DA_HEADS, 2, DA_QK_DIM).transpose(1, 0, 2, 3, 4, 5)

    def block(q_blk):
        a = diff_attn_probs(q_blk, k_all, lam)
        return jnp.einsum('bhqk,bkhe->bqhe', a.astype(v_all.dtype), v_all)

    o = lax.map(block, qb)
    o = o.transpose(1, 0, 2, 3, 4).reshape(B, T, DA_HEADS, DA_V_DIM)
    return diff_attn_finish(o, norm_g, lam_init)


def diff_attention_context(qc, kc, vc, lam, norm_g, lam_init):
    a = diff_attn_probs(qc, kc, lam)
    o = jnp.einsum('bhqk,bkhe->bqhe', a.astype(vc.dtype), vc)
    return diff_attn_finish(o, norm_g, lam_init)


def centred_token_shift(p, mu):
    prev = jnp.pad(p[:, :-1], ((0, 0), (1, 0), (0, 0)))
    nxt = jnp.pad(p[:, 1:], ((0, 0), (0, 1), (0, 0)))
    return p + mu[0] * (prev - p) + mu[1] * (nxt - p)


def heads_l2_normalize(z):
    B, T, C = z.shape
    zh = z.reshape(B, T, RW_HEADS, RW_HEAD).astype(jnp.float32)
    n = jnp.sqrt(jnp.sum(jnp.square(zh), -1, keepdims=True))
    return (zh / jnp.maximum(n, 1e-12)).reshape(B, T, C)


def rwkv7_features(p, mu, w0, w2, a0, a2, k_k, k_a):
    p = centred_token_shift(p, mu)
    r, k, v, xw, xa, xg = jnp.split(p, RW_SPLITS, axis=-1)
    w_pre = w0[:, None, None, :] + jnp.einsum('btr,zrc->zbtc', jnp.tanh(xw), w2)
    w_log = -jax.nn.softplus(-w_pre.astype(jnp.float32)) - 0.5
    decay = jnp.exp(-jnp.exp(w_log))
    a = jax.nn.sigmoid((a0[:, None, None, :] + jnp.einsum('btr,zrc->zbtc', xa, a2)).astype(jnp.float32))
    kk = heads_l2_normalize(k * k_k)
    k_dir = k[None].astype(jnp.float32) * (1.0 + (a - 1.0) * k_a)
    return r, k_dir, v, xg, decay, a, kk


def to_heads_tm(z):
    B, T, _ = z.shape
    return z.reshape(B, T, RW_HEADS, RW_HEAD).transpose(1, 0, 2, 3).astype(jnp.float32)


def from_heads_tm(z):
    T, B = z.shape[:2]
    return z.transpose(1, 0, 2, 3).reshape(B, T, RW_WIDTH)


def wkv7_scan(s0, xs):
    def step(S, inp):
        w_t, k_t, v_t, kk_t, a_t = inp[:5]
        sa = jnp.einsum('bhij,bhj->bhi', S, -kk_t)
        S = (S * w_t[:, :, None, :] + sa[..., None] * (kk_t * a_t)[:, :, None, :]
             + v_t[..., None] * k_t[:, :, None, :])
        y = jnp.einsum('bhij,bhj->bhi', S, inp[5]) if len(inp) == 6 else None
        return S, y
    return lax.scan(step, s0, xs)


def rwkv7_direction(s0, feats, z, reverse, emit):
    r, k_dir, v, xg, decay, a, kk = feats
    xs = [decay[z], k_dir[z], v, kk, a[z]] + ([r] if emit else [])
    xs = [to_heads_tm(t) for t in xs]
    if reverse:
        xs = [jnp.flip(t, axis=0) for t in xs]
    s, ys = wkv7_scan(s0, tuple(xs))
    if not emit:
        return s, None
    if reverse:
        ys = jnp.flip(ys, axis=0)
    return s, from_heads_tm(ys)


def rwkv7_output(y, feats, ln_g, ln_b, r_k, g2, dtype):
    r, k_dir, v, xg, decay, a, kk = feats
    B, T, C = y.shape
    yh = y.reshape(B, T, RW_HEADS, RW_HEAD)
    mu = jnp.mean(yh, -1, keepdims=True)
    var = jnp.mean(jnp.square(yh - mu), -1, keepdims=True)
    gn = ((yh - mu) * lax.rsqrt(var + RW_GN_EPS)).reshape(B, T, C) * ln_g + ln_b
    k_b = 0.5 * (k_dir[0] + k_dir[1])
    bonus = (jnp.sum((r * k_b).reshape(B, T, RW_HEADS, RW_HEAD) * r_k, -1, keepdims=True)
             * v.reshape(B, T, RW_HEADS, RW_HEAD))
    g = jax.nn.sigmoid(xg) @ g2
    return ((gn + bonus.reshape(B, T, C)) * g).astype(dtype)


def rwkv7_mixer(p_lat, p_ctx, mu, w0, w2, a0, a2, k_k, k_a, r_k, ln_g, ln_b, g2, emit_ctx):
    B = p_lat.shape[0]
    f_lat = rwkv7_features(p_lat, mu, w0, w2, a0, a2, k_k, k_a)
    f_ctx = rwkv7_features(p_ctx, mu, w0, w2, a0, a2, k_k, k_a)
    s_zero = jnp.zeros((B, RW_HEADS, RW_HEAD, RW_HEAD), jnp.float32)
    y_lat, y_ctx = [], []
    for z, reverse in ((0, False), (1, True)):
        s_ctx, yc = rwkv7_direction(s_zero, f_ctx, z, reverse, emit_ctx)
        _, yl = rwkv7_direction(s_ctx, f_lat, z, reverse, True)
        y_lat.append(yl)
        y_ctx.append(yc)
    out_lat = rwkv7_output(y_lat[0] + y_lat[1], f_lat, ln_g, ln_b, r_k, g2, p_lat.dtype)
    out_ctx = rwkv7_output(y_ctx[0] + y_ctx[1], f_ctx, ln_g, ln_b, r_k, g2, p_ctx.dtype) if emit_ctx else None
    return out_lat, out_ctx


def spatial_gating(p, norm_g, norm_b, w_s, b_s):
    B, T, _ = p.shape
    u, v = jnp.split(jax.nn.gelu(p), 2, axis=-1)
    v = layer_norm(v, norm_g, norm_b)
    vc = v.reshape(B, T // SG_CHUNK, SG_CHUNK, SG_GROUPS, SG_GROUP_DIM)
    vm = jnp.einsum('gpq,bnqgc->bnpgc', w_s, vc) + b_s.T[None, None, :, :, None]
    return u * vm.reshape(B, T, SG_WIDTH)


def gated_merge(y_da, y_rw, y_sg, gate_logits, w_branch, w_out):
    g_da, g_rw, g_sg = jnp.split(jax.nn.sigmoid(gate_logits), N_BRANCH, axis=-1)
    wb_da, wb_rw, wb_sg = jnp.split(w_branch, (DA_WIDTH, DA_WIDTH + RW_WIDTH), axis=0)
    m = g_da * (y_da @ wb_da) + g_rw * (y_rw @ wb_rw) + g_sg * (y_sg @ wb_sg)
    return m @ w_out


def expert_choice_ffn(h, w_router, w_gate, w_up, w_down):
    B, T, D = h.shape
    cap = EC_CAPACITY * T // N_EXPERTS
    aff = jax.nn.softmax(jnp.einsum('btd,de->bte', h, w_router).astype(jnp.float32), axis=-1)
    gate, idx = lax.top_k(jnp.swapaxes(aff, 1, 2), cap)
    xe = jax.vmap(lambda hb, ib: hb[ib])(h, idx)
    hid = jax.nn.silu(jnp.einsum('becd,edf->becf', xe, w_gate)) * jnp.einsum('becd,edf->becf', xe, w_up)
    ye = jnp.einsum('becf,efd->becd', hid, w_down) * gate[..., None].astype(h.dtype)

    def scatter(ib, yb):
        return jnp.zeros((T, D), yb.dtype).at[ib.reshape(-1)].add(yb.reshape(-1, D))

    return jax.vmap(scatter)(idx, ye)


def setup_inputs(seed: int = 0) -> dict:
    key = jax.random.key(seed)
    ks = iter(jax.random.split(key, 40))
    L, D = DEPTH, D_MODEL

    def nrm(shape, scale):
        return jax.random.normal(next(ks), shape, jnp.float32) * scale

    def uni(shape, lo, hi):
        return jax.random.uniform(next(ks), shape, jnp.float32, lo, hi)

    return {
        'x': nrm((BATCH, SEQ, D), 1.0),
        'c': nrm((BATCH, D), 1.0),
        'ctx': nrm((BATCH, CTX_LEN, D), 1.0),
        'c_ctx': nrm((D,), 1.0),
        'w_mod': nrm((L, D, 6 * D), 0.5 * D ** -0.5),
        'b_mod': nrm((L, 6 * D), 0.02),
        'w_in': nrm((L, D, IN_COLS), D ** -0.5),
        'da_lambda': nrm((L, 4, DA_QK_DIM), 0.1),
        'da_norm_g': 1.0 + nrm((L, DA_V_DIM), 0.05),
        'rw_shift_mu': uni((L, 2, RW_COLS), 0.0, 0.5),
        'rw_w0': uni((L, 2, RW_WIDTH), -6.5, -1.5),
        'rw_w2': nrm((L, 2, RW_DECAY_LORA, RW_WIDTH), 0.1 * RW_DECAY_LORA ** -0.5),
        'rw_a0': nrm((L, 2, RW_WIDTH), 0.1),
        'rw_a2': nrm((L, 2, RW_AAA_LORA, RW_WIDTH), 0.5 * RW_AAA_LORA ** -0.5),
        'rw_k_k': 0.85 + nrm((L, RW_WIDTH), 0.02),
        'rw_k_a': 1.0 + nrm((L, RW_WIDTH), 0.02),
        'rw_r_k': nrm((L, RW_HEADS, RW_HEAD), 0.1),
        'rw_ln_g': 1.0 + nrm((L, RW_WIDTH), 0.05),
        'rw_ln_b': nrm((L, RW_WIDTH), 0.02),
        'rw_g2': nrm((L, RW_GATE_LORA, RW_WIDTH), RW_GATE_LORA ** -0.5),
        'sg_norm_g': 1.0 + nrm((L, SG_WIDTH), 0.05),
        'sg_norm_b': nrm((L, SG_WIDTH), 0.02),
        'sg_w': nrm((L, SG_GROUPS, SG_CHUNK, SG_CHUNK), 0.5 * SG_CHUNK ** -0.5),
        'sg_b': 1.0 + nrm((L, SG_GROUPS, SG_CHUNK), 0.1),
        'w_branch': nrm((L, MIX_WIDTH, D), DA_WIDTH ** -0.5),
        'w_out': nrm((L, D, D), DN_BETA * D ** -0.5),
        'ln1_g': 1.0 + nrm((L, D), 0.05),
        'ln1_b': nrm((L, D), 0.02),
        'w_router': nrm((L, D, N_EXPERTS), D ** -0.5),
        'w_e_gate': nrm((L, N_EXPERTS, D, EXPERT_FF), D ** -0.5),
        'w_e_up': nrm((L, N_EXPERTS, D, EXPERT_FF), D ** -0.5),
        'w_e_down': nrm((L, N_EXPERTS, EXPERT_FF, D), DN_BETA * EXPERT_FF ** -0.5),
        'ln2_g': 1.0 + nrm((L, D), 0.05),
        'ln2_b': nrm((L, D), 0.02),
    }


def reference(x, c, ctx, c_ctx, w_mod, b_mod, w_in, da_lambda, da_norm_g,
              rw_shift_mu, rw_w0, rw_w2, rw_a0, rw_a2, rw_k_k, rw_k_a, rw_r_k,
              rw_ln_g, rw_ln_b, rw_g2, sg_norm_g, sg_norm_b, sg_w, sg_b,
              w_branch, w_out, ln1_g, ln1_b, w_router, w_e_gate, w_e_up, w_e_down,
              ln2_g, ln2_b):
    B, T, _ = x.shape
    cos, sin = axial_rope_tables(T)
    xc = ctx
    for l in range(DEPTH):
        last = l == DEPTH - 1
        Tc = xc.shape[1]
        lam_init = 0.8 - 0.6 * math.exp(-0.3 * l)
        lam = diff_lambda(da_lambda[l], lam_init)
        w_in_l = w_in[l]

        mod = (jax.nn.silu(c) @ w_mod[l] + b_mod[l])[:, None, :]
        sh1, sc1, g1, sh2, sc2, g2 = jnp.split(mod, 6, axis=-1)
        modc = jax.nn.silu(c_ctx) @ w_mod[l] + b_mod[l]
        sh1c, sc1c, g1c, sh2c, sc2c, g2c = jnp.split(modc, 6)

        h = x * (1.0 + sc1) + sh1
        hc = xc * (1.0 + sc1c) + sh1c
        p = h @ w_in_l

        q = apply_axial_rope(p[..., DA_Q0:DA_K0].reshape(B, T, DA_HEADS, 2, DA_QK_DIM) * DA_QK_DIM ** -0.5, cos, sin)
        k = apply_axial_rope(p[..., DA_K0:DA_V0].reshape(B, T, DA_HEADS, 2, DA_QK_DIM), cos, sin)
        v = p[..., DA_V0:RW_0].reshape(B, T, DA_HEADS, DA_V_DIM)
        pc_kv = hc @ w_in_l[:, DA_K0:RW_0]
        kc = pc_kv[..., :DA_QK_COLS].reshape(B, Tc, DA_HEADS, 2, DA_QK_DIM)
        vc = pc_kv[..., DA_QK_COLS:].reshape(B, Tc, DA_HEADS, DA_V_DIM)
        y_da = diff_attention_latent(q, k, v, kc, vc, lam, da_norm_g[l], lam_init)

        pc_rw = hc @ w_in_l[:, RW_0:SG_0]
        y_rw, y_rw_c = rwkv7_mixer(p[..., RW_0:SG_0], pc_rw, rw_shift_mu[l], rw_w0[l], rw_w2[l],
                                   rw_a0[l], rw_a2[l], rw_k_k[l], rw_k_a[l], rw_r_k[l],
                                   rw_ln_g[l], rw_ln_b[l], rw_g2[l], not last)

        y_sg = spatial_gating(p[..., SG_0:GATE_0], sg_norm_g[l], sg_norm_b[l], sg_w[l], sg_b[l])

        mix = gated_merge(y_da, y_rw, y_sg, p[..., GATE_0:], w_branch[l], w_out[l])
        x_mid = layer_norm(DN_ALPHA * x + g1 * mix, ln1_g[l], ln1_b[l])
        f = expert_choice_ffn(x_mid * (1.0 + sc2) + sh2, w_router[l], w_e_gate[l], w_e_up[l], w_e_down[l])
        x_new = layer_norm(DN_ALPHA * x_mid + g2 * f, ln2_g[l], ln2_b[l])

        if not last:
            qc = (hc @ w_in_l[:, DA_Q0:DA_K0]).reshape(B, Tc, DA_HEADS, 2, DA_QK_DIM) * DA_QK_DIM ** -0.5
            y_da_c = diff_attention_context(qc, kc, vc, lam, da_norm_g[l], lam_init)
            pc_rest = hc @ w_in_l[:, SG_0:]
            y_sg_c = spatial_gating(pc_rest[..., :2 * SG_WIDTH], sg_norm_g[l], sg_norm_b[l], sg_w[l], sg_b[l])
            mix_c = gated_merge(y_da_c, y_rw_c, y_sg_c, pc_rest[..., 2 * SG_WIDTH:], w_branch[l], w_out[l])
            xc_mid = layer_norm(DN_ALPHA * xc + g1c * mix_c, ln1_g[l], ln1_b[l])
            fc = expert_choice_ffn(xc_mid * (1.0 + sc2c) + sh2c, w_router[l], w_e_gate[l], w_e_up[l], w_e_down[l])
            xc = layer_norm(DN_ALPHA * xc_mid + g2c * fc, ln2_g[l], ln2_b[l])
        x = x_new
    return x
```

```python
import contextlib
import math
import numpy as np
import ml_dtypes
import concourse.bass as bass
import concourse.mybir as mybir
from concourse.bass_utils import run_bass_kernel_spmd

F32 = mybir.dt.float32
BF16 = mybir.dt.bfloat16
AF = mybir.ActivationFunctionType
ALU = mybir.AluOpType
AX = mybir.AxisListType

D = 1024
GRID_W = 64
DA_H = 4
RW_H = 8
NE = 16
FF = 2048
IN_COLS = 7424
C_Q, C_K, C_V, C_RW, C_SG, C_G = 0, 512, 1024, 1536, 3328, 4352
RW_COLS = 1792
LN_EPS = 1e-5
DA_EPS = 1e-5
RW_GN_EPS = 64e-5
DEPTH = 2
DN_ALPHA = (2 * DEPTH) ** 0.25
import os
DBG_RW = float(os.environ.get('DBG_RW', '99'))
DBG_AT = float(os.environ.get('DBG_AT', '99'))


class TB:
    __slots__ = ("w", "r", "dsem", "dcnt", "name", "space")

    def __init__(self, name, space):
        self.w = {}
        self.r = {}
        self.dsem = {}
        self.dcnt = {}
        self.name = name
        self.space = space


class V:
    __slots__ = ("ap", "tb")

    def __init__(self, ap, tb):
        self.ap = ap
        self.tb = tb

    def __getitem__(self, idx):
        return V(self.ap[idx], self.tb)

    def rearrange(self, *a, **k):
        return V(self.ap.rearrange(*a, **k), self.tb)

    def bitcast(self, dt):
        return V(self.ap.bitcast(dt), self.tb)

    def unsqueeze(self, i):
        return V(self.ap.unsqueeze(i), self.tb)

    def to_broadcast(self, shape):
        return V(self.ap.to_broadcast(shape), self.tb)

    def broadcast_to(self, shape):
        return V(self.ap.broadcast_to(shape), self.tb)

    @property
    def shape(self):
        return self.ap.shape


class Grp:
    def __init__(self, sem, cnt):
        self.sem = sem
        self.cnt = cnt


class Eng:
    def __init__(self, name, h, sem):
        self.name = name
        self.h = h
        self.sem = sem
        self.cnt = 0
        self.seen = {}


class K:
    def __init__(self, nc, es):
        self.nc = nc
        self.es = es
        self.E = {}
        for name, h in (("pe", nc.tensor), ("act", nc.scalar), ("dve", nc.vector),
                        ("pool", nc.gpsimd), ("sp", nc.sync)):
            self.E[name] = Eng(name, h, es.enter_context(nc.semaphore("sem_" + name)))
        self.n_inst = 0
        self.tbs = []
        self.sem_free = {"sw": [], "hw": []}
        self.phase_mark = 0
        self.groups = []

    def _tb(self, name, space):
        tb = TB(name, space)
        self.tbs.append(tb)
        return tb

    def sb(self, name, shape, dt=F32, es=None):
        self.uid = getattr(self, "uid", 0) + 1
        name = "%s_%d" % (name, self.uid)
        t = (es or self.es).enter_context(self.nc.sbuf_tensor(name, list(shape), dt))
        return V(t[:], self._tb(name, "sb"))

    def ps(self, name, shape, dt=F32, es=None):
        self.uid = getattr(self, "uid", 0) + 1
        name = "%s_%d" % (name, self.uid)
        t = (es or self.es).enter_context(self.nc.psum_tensor(name, list(shape), dt))
        return V(t[:], self._tb(name, "ps"))

    def sub(self, v, name):
        return V(v.ap, self._tb(name, v.tb.space))

    def dram(self, name, shape, dt=F32, kind="Internal"):
        t = self.nc.dram_tensor(name, list(shape), dt, kind=kind)
        return V(t.ap(), self._tb(name, "dram"))

    def _sem_get(self, name, qk):
        if self.sem_free[qk]:
            return self.sem_free[qk].pop()
        return [self.es.enter_context(self.nc.semaphore(name)), 0]

    def group(self, name):
        sem, cnt = self._sem_get("g_" + name, "hw")
        g = Grp(sem, cnt)
        self.groups.append(g)
        return g

    def end_phase(self):
        self.barrier()
        for tb in self.tbs:
            tb.w = {}
            tb.r = {}
        for tb in self.tbs[self.phase_mark:]:
            for qk, sc in tb.dsem.items():
                self.sem_free["sw" if qk == "pool" else "hw"].append(sc)
            tb.dsem = {}
        del self.tbs[self.phase_mark:]
        for g in self.groups:
            self.sem_free["hw"].append([g.sem, g.cnt])
        self.groups = []

    def begin_phase(self):
        self.phase_mark = len(self.tbs)

    def _wait(self, E, ev):
        sem, val = ev
        if isinstance(val, Grp):
            val = val.cnt
        key = id(sem)
        if E.seen.get(key, 0) >= val:
            return
        E.h.wait_ge(sem, val)
        E.seen[key] = val

    def _sync(self, E, reads, writes, skip_self=False):
        for tb in reads:
            for ev in tb.w.values():
                if not (skip_self and ev[0] is E.sem):
                    self._wait(E, ev)
        for tb in writes:
            for ev in list(tb.w.values()) + list(tb.r.values()):
                if not (skip_self and ev[0] is E.sem):
                    self._wait(E, ev)

    def _commit(self, ev, reads, writes):
        for tb in reads:
            tb.r[id(ev[0])] = ev
        for tb in writes:
            tb.w = {id(ev[0]): ev}
            tb.r = {}

    def op(self, eng, name, **kw):
        E = self.E[eng]
        reads, writes, args = [], [], {}
        for k, v in kw.items():
            if isinstance(v, V):
                (writes if k in ("out", "accum_out", "ap") else reads).append(v.tb)
                args[k] = v.ap
            else:
                args[k] = v
        self._sync(E, reads, writes, skip_self=(eng == "pe"))
        ins = getattr(E.h, name)(**args)
        E.cnt += 1
        ins.then_inc(E.sem, 1)
        self._commit((E.sem, E.cnt), reads, writes)
        self.n_inst += 1
        return ins

    def dma(self, q, out, in_, grp=None, **kw):
        E = self.E[q]
        dst, src = out.tb, in_.tb
        qk = q
        kind = "sw" if q == "pool" else "hw"
        if grp is not None:
            sem = grp.sem
        else:
            own = dst if dst.space == "sb" else (src if src.space == "sb" else dst)
            if qk not in own.dsem:
                own.dsem[qk] = self._sem_get("d%s_%s" % (qk, own.name), kind)
            sem = own.dsem[qk][0]
        for ev in src.w.values():
            self._wait(E, ev)
        if dst.space != "dram":
            for ev in dst.w.values():
                if ev[0] is not sem:
                    self._wait(E, ev)
        for ev in dst.r.values():
            self._wait(E, ev)
        E.h.dma_start(out=out.ap, in_=in_.ap, **kw).then_inc(sem, 16)
        if grp is not None:
            grp.cnt += 16
            ev = (sem, grp)
        else:
            own.dsem[qk][1] += 16
            ev = (sem, own.dsem[qk][1])
        src.r[id(sem)] = ev
        if dst.space == "dram":
            dst.w[id(sem)] = ev
        else:
            dst.w = {id(sem): ev}
        dst.r = {}
        self.n_inst += 1

    def barrier(self):
        evs = [(E.sem, E.cnt) for E in self.E.values() if E.cnt > 0]
        for tb in self.tbs:
            for qk, sc in tb.dsem.items():
                evs.append((sc[0], sc[1]))
        for g in self.groups:
            evs.append((g.sem, g.cnt))
        for E in self.E.values():
            for ev in evs:
                if ev[0] is E.sem:
                    continue
                self._wait(E, ev)

    def final_wait(self, tbs):
        E = self.E["sp"]
        for tb in tbs:
            for ev in tb.w.values():
                self._wait(E, ev)

    def mm(self, out, lhsT, rhs, start=True, stop=True):
        return self.op("pe", "matmul", out=out, lhsT=lhsT, rhs=rhs, start=start, stop=stop)

    def tr(self, out, in_, ident):
        return self.op("pe", "transpose", out=out, in_=in_, identity=ident)

    def act(self, out, in_, func, **kw):
        return self.op("act", "activation", out=out, in_=in_, func=func, **kw)

    def tt(self, out, in0, in1, op, eng="dve"):
        return self.op(eng, "tensor_tensor", out=out, in0=in0, in1=in1, op=op)

    def ts(self, out, in0, s1, op0, s2=None, op1=None, eng="dve", **kw):
        if op1 is None:
            return self.op(eng, "tensor_scalar", out=out, in0=in0, scalar1=s1, scalar2=None, op0=op0, **kw)
        return self.op(eng, "tensor_scalar", out=out, in0=in0, scalar1=s1, scalar2=s2, op0=op0, op1=op1, **kw)

    def stt(self, out, in0, scalar, in1, op0, op1):
        return self.op("dve", "scalar_tensor_tensor", out=out, in0=in0, scalar=scalar, in1=in1, op0=op0, op1=op1)

    def copy(self, out, in_, eng="dve"):
        if eng == "act":
            return self.op("act", "activation", out=out, in_=in_, func=AF.Identity)
        return self.op(eng, "tensor_copy", out=out, in_=in_)

    def memset(self, ap, val, eng="dve"):
        return self.op(eng, "memset", ap=ap, constant=val)


class Ring:
    def __init__(self, items):
        self.items = items
        self.i = 0

    def next(self):
        it = self.items[self.i % len(self.items)]
        self.i += 1
        return it


def blocks(total, size):
    out = []
    s = 0
    while s < total:
        n = min(size, total - s)
        out.append((s, n))
        s += n
    return out


class Cfg:
    def __init__(self, T=2048, Tc=256, NB=2, L=2, debug=False):
        self.T, self.Tc, self.NB, self.L, self.debug = T, Tc, NB, L, debug
        self.TT = T + Tc
        self.NT = self.TT // 128
        self.NTc = Tc // 128
        self.cap_lat = 2 * T // NE
        self.cap_ctx = 2 * Tc // NE
        self.RWROWS = self.TT + 3


def rw_row(cfg, t):
    return 1 + t if t < cfg.Tc else 2 + t


def host_consts(cfg):
    T, Tc, TT, NB = cfg.T, cfg.Tc, cfg.TT, cfg.NB
    c = {}
    c["ident_f"] = np.eye(128, dtype=np.float32)
    c["ident_b"] = np.eye(128).astype(ml_dtypes.bfloat16)
    rows = T // GRID_W
    row = np.repeat(np.arange(rows, dtype=np.float32), GRID_W)
    col = np.tile(np.arange(GRID_W, dtype=np.float32), rows)
    half = 32
    inv_freq = (10000.0 ** (-np.arange(0, half, 2, dtype=np.float32) / half)).astype(np.float32)
    ar = row[:, None] * inv_freq
    ac = col[:, None] * inv_freq
    ang = np.concatenate([ar, ar, ac, ac], axis=-1).astype(np.float32)
    cos = np.concatenate([np.ones((Tc, 64), np.float32), np.cos(ang)], 0)
    sin = np.concatenate([np.zeros((Tc, 64), np.float32), np.sin(ang)], 0)
    cosT = np.tile(np.concatenate([cos.T, cos.T], 0), (1, NB)).astype(np.float32)
    sinT = np.tile(np.concatenate([sin.T, sin.T], 0), (1, NB)).astype(np.float32)
    c["cosT"], c["sinT"] = np.ascontiguousarray(cosT), np.ascontiguousarray(sinT)
    R = np.zeros((64, 64), np.float32)
    for base in (0, 32):
        for i in range(16):
            R[base + i, base + 16 + i] = -1.0
            R[base + 16 + i, base + i] = 1.0
    R2 = np.zeros((128, 128), np.float32)
    R2[:64, :64] = R
    R2[64:, 64:] = R
    c["rotT"] = np.ascontiguousarray(R2.T)
    u = np.arange(128)[:, None]
    s = np.arange(128)[None, :]
    c["tri"] = np.stack([(u <= s), (u >= s)]).astype(np.float32)
    c["m_su"] = np.stack([(u < s), (u > s)]).astype(np.float32)
    c["m_iu"] = np.stack([(u <= s), (u >= s)]).astype(np.float32)
    c["m_nsu"] = -c["m_su"]
    c["m_nsl"] = -np.stack([(u > s), (u < s)]).astype(np.float32)
    c["iota"] = np.tile(np.arange(256, dtype=np.float32)[None, :], (128, 1))
    c["ones_f"] = np.ones((128, 128), np.float32)
    return c


WEIGHT_SPECS = {
    "w_mod": (D, 6 * D), "b_mod": (6 * D,), "w_in": (D, IN_COLS), "da_lambda": (4, 64), "da_norm_g": (128,),
    "rw_shift_mu": (2, RW_COLS), "rw_w0": (2, 512), "rw_w2": (2, 64, 512), "rw_a0": (2, 512),
    "rw_a2": (2, 64, 512), "rw_k_k": (512,), "rw_k_a": (512,), "rw_r_k": (8, 64), "rw_ln_g": (512,),
    "rw_ln_b": (512,), "rw_g2": (128, 512), "sg_norm_g": (512,), "sg_norm_b": (512,), "sg_w": (4, 128, 128),
    "sg_b": (4, 128), "w_branch": (1536, D), "w_out": (D, D), "ln1_g": (D,), "ln1_b": (D,),
    "w_router": (D, NE), "w_e_gate": (NE, D, FF), "w_e_up": (NE, D, FF), "w_e_down": (NE, FF, D),
    "ln2_g": (D,), "ln2_b": (D,),
}


def build(cfg, phases=None, dbg_out=(), dbg_in=()):
    nc = bass.Bass("TRN2", target_bir_lowering=False)
    T, Tc, TT, NB, NT, NTc = cfg.T, cfg.Tc, cfg.TT, cfg.NB, cfg.NT, cfg.NTc
    NS = NB * TT
    top = contextlib.ExitStack()
    k = K(nc, top)

    def din(name, shape, dt=F32):
        return V(nc.dram_tensor(name, list(shape), dt, kind="ExternalInput").ap(), k._tb(name, "dram"))

    X = din("x", [NB, T, D])
    CTX = din("ctx", [NB, Tc, D])
    C3 = din("c3", [NB + 1, D])
    W = {n: din(n, [cfg.L] + list(s)) for n, s in WEIGHT_SPECS.items()}
    hc = host_consts(cfg)
    CONST = {n: din("k_" + n, list(a.shape), BF16 if a.dtype == ml_dtypes.bfloat16 else F32) for n, a in hc.items()}
    OUT = V(nc.dram_tensor("out", [NB, T, D], F32, kind="ExternalOutput").ap(), k._tb("out", "dram"))

    S = {}

    def scratch(name, shape, dt=F32):
        kind = "ExternalOutput" if name in dbg_out else ("ExternalInput" if name in dbg_in else "Internal")
        S[name] = k.dram("s_" + name, shape, dt, kind=kind)
        return S[name]

    scratch("xcur", [NB, TT, D])
    scratch("mod", [cfg.L, 3, 6 * D])
    scratch("qk", [NB, 1024, TT], BF16)
    scratch("v", [NB, TT, 512], BF16)
    scratch("rw", [NB, cfg.RWROWS, RW_COLS])
    scratch("sg", [NB, TT, 1024])
    scratch("gT", [NB, 3072, TT], BF16)
    scratch("yT", [NB, 1536, TT], BF16)
    scratch("y0", [NB, TT, 512])
    scratch("rwaux", [NB, TT, 1024])
    scratch("xmid", [NB, TT, D])
    scratch("h2", [NB, TT, D], BF16)
    scratch("aff", [NB, NE, TT])
    scratch("slotT", [NB, TT, NE])
    scratch("GT", [NB, TT, NE])
    scratch("xe", [NB, 2, NE, D, cfg.cap_lat], BF16)
    scratch("ye", [NB, 2, NE * cfg.cap_lat, D], BF16)

    ident_f = k.sb("ident_f", [128, 128])
    ident_b = k.sb("ident_b", [128, 128], BF16)
    k.dma("sp", ident_f, CONST["ident_f"])
    k.dma("sp", ident_b, CONST["ident_b"])
    zrow = k.sb("zrow", [1, RW_COLS])
    k.memset(zrow, 0.0)
    for b in range(NB):
        for r in (0, Tc + 1, TT + 2):
            if "rw" not in dbg_in:
                k.dma("sp", S["rw"][b, r:r + 1, :], zrow)
    for b in range(NB):
        k.dma("sp", S["xcur"][b, 0:Tc, :], CTX[b])
        k.dma("sp", S["xcur"][b, Tc:TT, :], X[b])

    def stream_segs(t0, n):
        out = []
        s = t0
        while s < t0 + n:
            b = s // TT
            e = min(t0 + n, (b + 1) * TT)
            out.append((b, s - b * TT, s - t0, e - s))
            s = e
        return out

    def phase_mod(l, modT):
        k.begin_phase()
        with contextlib.ExitStack() as es:
            cT = k.sb("cT", [128, 8, 3], F32, es)
            for r in range(3):
                k.dma("sp", cT[:, :, r], C3[r].rearrange("(k p) -> p k", p=128), allow_slow_non_contiguous=True)
            scT = k.sb("scT", [128, 8, 3], BF16, es)
            k.act(scT, cT, AF.Silu)
            bm = k.sb("bm", [3, 6 * D], F32, es)
            k.dma("sp", bm, W["b_mod"][l].unsqueeze(0).broadcast_to([3, 6 * D]))
            modrow = k.sb("modrow", [3, 6 * D], F32, es)
            wr = Ring([k.sb(f"wm{i}", [128, 8, 512], BF16, es) for i in range(2)])
            pr = Ring([k.ps(f"pm{i}", [3, 512], F32, es) for i in range(2)])
            for (c0, n) in blocks(6 * D, 512):
                wb = wr.next()
                k.dma("pool", wb, W["w_mod"][l][:, c0:c0 + n].rearrange("(k p) c -> p k c", p=128))
                ps = pr.next()
                for kk in range(8):
                    k.mm(ps, scT[:, kk, :], wb[:, kk, :], start=(kk == 0), stop=(kk == 7))
                k.tt(modrow[:, c0:c0 + n], ps, bm[:, c0:c0 + n], ALU.add)
            k.dma("sp", S["mod"][l], modrow)
            for j in range(6):
                for r in range(3):
                    k.dma("sp", modT[:, j * 8:(j + 1) * 8, r],
                          S["mod"][l, r, j * D:(j + 1) * D].rearrange("(k p) -> p k", p=128),
                          allow_slow_non_contiguous=True)
            for j in (1, 4):
                k.ts(modT[:, j * 8:(j + 1) * 8, :], modT[:, j * 8:(j + 1) * 8, :], 1.0, ALU.add)
        k.end_phase()

    def phase_inproj(l, modT):
        k.begin_phase()
        with contextlib.ExitStack() as es:
            hT = k.sb("hT", [128, 8, NS], BF16, es)
            cosT = k.sb("cosT", [128, NS], F32, es)
            sinT = k.sb("sinT", [128, NS], F32, es)
            rotT = k.sb("rotT", [128, 128], F32, es)
            k.dma("sp", cosT, CONST["cosT"])
            k.dma("sp", sinT, CONST["sinT"])
            k.dma("sp", rotT, CONST["rotT"])
            xr = Ring([k.sb(f"xt{i}", [128, D], F32, es) for i in range(3)])
            ptr = Ring([k.ps(f"ptr{i}", [128, 4, 128], F32, es) for i in range(2)])
            for b in range(NB):
                for i in range(NT):
                    r = 2 if i < NTc else b
                    xt = xr.next()
                    k.dma("sp", xt, S["xcur"][b, i * 128:(i + 1) * 128, :])
                    for g in range(2):
                        pt = ptr.next()
                        for j in range(4):
                            kk = g * 4 + j
                            k.tr(pt[:, j, :], xt[:, kk * 128:(kk + 1) * 128], ident_f)
                        for j in range(4):
                            kk = g * 4 + j
                            s0 = b * TT + i * 128
                            k.act(hT[:, kk, s0:s0 + 128], pt[:, j, :], AF.Identity,
                                  scale=modT[:, 8 + kk, r:r + 1], bias=modT[:, kk, r:r + 1])
            wr = Ring([k.sb(f"wi{i}", [128, 8, 512], BF16, es) for i in range(3)])
            pmm = Ring([k.ps(f"pmm{i}", [128, 512], F32, es) for i in range(3)])
            prot = Ring([k.ps(f"prot{i}", [128, 512], F32, es) for i in range(2)])
            qs_r = Ring([k.sb(f"qs{i}", [128, 512], F32, es) for i in range(2)])
            t1_r = Ring([k.sb(f"t1{i}", [128, 512], F32, es) for i in range(2)])
            ob_r = Ring([k.sb(f"ob{i}", [128, 512], BF16, es) for i in range(3)])
            of_r = Ring([k.sb(f"of{i}", [128, 512], F32, es) for i in range(3)])
            sblocks = blocks(NS, 512)
            colblocks = []
            for (g0, g1) in ((C_Q, C_V), (C_V, C_RW), (C_RW, C_SG), (C_SG, C_G), (C_G, IN_COLS)):
                colblocks += [(g0 + a, n) for (a, n) in blocks(g1 - g0, 512)]
            for (c0, ncol) in colblocks:
                wb = wr.next()
                k.dma("pool", wb[:, :, 0:ncol], W["w_in"][l][:, c0:c0 + ncol].rearrange("(k p) c -> p k c", p=128))
                if c0 < C_V or c0 >= C_G:
                    for cc in range(ncol // 128):
                        col = c0 + cc * 128
                        for (t0, n) in sblocks:
                            ps = pmm.next()
                            for kk in range(8):
                                k.mm(ps[:, 0:n], wb[:, kk, cc * 128:(cc + 1) * 128], hT[:, kk, t0:t0 + n],
                                     start=(kk == 0), stop=(kk == 7))
                            ob = ob_r.next()
                            if col < C_V:
                                qs = qs_r.next()
                                k.act(qs[:, 0:n], ps[:, 0:n], AF.Identity, scale=(0.125 if col < C_K else 1.0))
                                pr_ = prot.next()
                                k.mm(pr_[:, 0:n], rotT, qs[:, 0:n])
                                t1 = t1_r.next()
                                k.tt(t1[:, 0:n], qs[:, 0:n], cosT[:, t0:t0 + n], ALU.mult, eng="pool")
                                k.tt(qs[:, 0:n], pr_[:, 0:n], sinT[:, t0:t0 + n], ALU.mult)
                                k.tt(ob[:, 0:n], qs[:, 0:n], t1[:, 0:n], ALU.add)
                                for (b, tb_, off, ln) in stream_segs(t0, n):
                                    k.dma("pool", S["qk"][b, col:col + 128, tb_:tb_ + ln], ob[:, off:off + ln])
                            else:
                                k.act(ob[:, 0:n], ps[:, 0:n], AF.Sigmoid)
                                gc = col - C_G
                                for (b, tb_, off, ln) in stream_segs(t0, n):
                                    k.dma("act", S["gT"][b, gc:gc + 128, tb_:tb_ + ln], ob[:, off:off + ln])
                else:
                    for b in range(NB):
                        for i in range(NT):
                            s0 = b * TT + i * 128
                            ps = pmm.next()
                            for kk in range(8):
                                k.mm(ps[:, 0:ncol], hT[:, kk, s0:s0 + 128], wb[:, kk, 0:ncol],
                                     start=(kk == 0), stop=(kk == 7))
                            if c0 < C_RW:
                                ob = ob_r.next()
                                k.copy(ob[:, 0:ncol], ps[:, 0:ncol], eng="act")
                                k.dma("act", S["v"][b, i * 128:(i + 1) * 128, c0 - C_V:c0 - C_V + ncol], ob[:, 0:ncol])
                            elif c0 < C_SG:
                                of = of_r.next()
                                k.copy(of[:, 0:ncol], ps[:, 0:ncol], eng="act")
                                r0 = rw_row(cfg, i * 128)
                                k.dma("act", S["rw"][b, r0:r0 + 128, c0 - C_RW:c0 - C_RW + ncol], of[:, 0:ncol])
                            else:
                                of = of_r.next()
                                k.act(of[:, 0:ncol], ps[:, 0:ncol], AF.Gelu_apprx_tanh)
                                k.dma("act", S["sg"][b, i * 128:(i + 1) * 128, c0 - C_SG:c0 - C_SG + ncol], of[:, 0:ncol])
        k.end_phase()

    def phase_attn(l):
        k.begin_phase()
        last = (l == DEPTH - 1)
        lam_init = 0.8 - 0.6 * math.exp(-0.3 * l)
        with contextlib.ExitStack() as es:
            lamt = k.sb("lamt", [128, 256], F32, es)
            k.dma("sp", lamt, W["da_lambda"][l].rearrange("a d -> (a d)").unsqueeze(0).broadcast_to([128, 256]))
            lv = lamt.rearrange("p (a d) -> p a d", a=4)
            prod = k.sb("lprod", [128, 2, 64], F32, es)
            k.tt(prod[:, 0, :], lv[:, 0, :], lv[:, 1, :], ALU.mult)
            k.tt(prod[:, 1, :], lv[:, 2, :], lv[:, 3, :], ALU.mult)
            lsum = k.sb("lsum", [128, 2], F32, es)
            k.op("dve", "tensor_reduce", out=lsum, in_=prod, axis=AX.X, op=ALU.add)
            lexp = k.sb("lexp", [128, 2], F32, es)
            k.act(lexp, lsum, AF.Exp)
            nlam = k.sb("nlam", [128, 1], F32, es)
            k.tt(nlam, lexp[:, 1:2], lexp[:, 0:1], ALU.subtract)
            k.ts(nlam, nlam, -lam_init, ALU.add)
            gb = k.sb("dag", [128, 128], F32, es)
            k.dma("sp", gb, W["da_norm_g"][l].unsqueeze(0).broadcast_to([128, 128]))
            k.ts(gb, gb, 1.0 - lam_init, ALU.mult)

            qr = Ring([k.sb(f"aq{i}", [128, TT], BF16, es) for i in range(2)])
            kr = Ring([k.sb(f"ak{i}", [128, TT], BF16, es) for i in range(2)])
            vr = Ring([k.sb(f"av{i}", [128, NT, 129], BF16, es) for i in range(2)])
            for vt in vr.items:
                k.memset(vt[:, :, 128:129], 1.0)
            yr = Ring([k.sb(f"ayT{i}", [128, TT], BF16, es) for i in range(2)])
            PT = [k.sb(f"aPT{m}", [128, NT, 512], BF16, es) for m in range(2)]
            pscore = Ring([k.ps(f"apsc{i}", [128, 512], F32, es) for i in range(2)])
            pacc = Ring([k.ps(f"apac{i}", [128, 512], F32, es) for i in range(4)])
            ptr_ = Ring([k.ps(f"aptr{i}", [128, 128], BF16, es) for i in range(2)])
            sm = Ring([k.sb(f"asm{i}", [128, 4], F32, es) for i in range(3)])
            y1r = Ring([k.sb(f"ay1{i}", [128, 128], F32, es) for i in range(2)])
            y2r = Ring([k.sb(f"ay2{i}", [128, 128], F32, es) for i in range(2)])
            jr = Ring([k.sb(f"ajk{i}", [128, 128], F32, es) for i in range(2)])
            ynr = Ring([k.sb(f"ayn{i}", [128, 128], BF16, es) for i in range(2)])
            groups = []
            if not last:
                groups += [(q0, n, NTc) for (q0, n) in blocks(Tc, 512)]
            groups += [(Tc + q0, n, NT) for (q0, n) in blocks(T, 512)]
            for b in range(NB):
                for h in range(DA_H):
                    qT, kT, Vh, yT = qr.next(), kr.next(), vr.next(), yr.next()
                    k.dma("sp", qT, S["qk"][b, h * 128:(h + 1) * 128, :])
                    k.dma("sp", kT, S["qk"][b, 512 + h * 128:512 + (h + 1) * 128, :])
                    k.dma("sp", Vh[:, :, 0:128], S["v"][b, :, h * 128:(h + 1) * 128].rearrange("(j p) e -> p j e", p=128))
                    for (q0, nq, nk) in groups:
                        if DBG_AT < 2:
                            continue
                        for m in range(2):
                            for j in range(nk):
                                ps = pscore.next()
                                k.mm(ps[:, 0:nq], kT[m * 64:(m + 1) * 64, j * 128:(j + 1) * 128],
                                     qT[m * 64:(m + 1) * 64, q0:q0 + nq])
                                k.act(PT[m][:, j, 0:nq], ps[:, 0:nq], AF.Exp)
                        for s_ in range(nq // 128):
                            if DBG_AT < 3:
                                continue
                            accs = []
                            for m in range(2):
                                acc = pacc.next()
                                for j in range(nk):
                                    k.mm(acc[:, 0:129], PT[m][:, j, s_ * 128:(s_ + 1) * 128], Vh[:, j, :],
                                         start=(j == 0), stop=(j == nk - 1))
                                accs.append(acc)
                            if DBG_AT < 4:
                                continue
                            t = sm.next()
                            k.op("dve", "reciprocal", out=t[:, 0:1], in_=accs[0][:, 128:129])
                            k.op("dve", "reciprocal", out=t[:, 1:2], in_=accs[1][:, 128:129])
                            k.tt(t[:, 1:2], t[:, 1:2], nlam, ALU.mult)
                            y1 = y1r.next()
                            k.ts(y1, accs[0][:, 0:128], t[:, 0:1], ALU.mult)
                            y2 = y2r.next()
                            k.stt(y2, accs[1][:, 0:128], t[:, 1:2], y1, ALU.mult, ALU.add)
                            jk = jr.next()
                            k.act(jk, y2, AF.Square, accum_out=t[:, 2:3])
                            k.ts(t[:, 2:3], t[:, 2:3], 1.0 / 128, ALU.mult, DA_EPS, ALU.add)
                            k.act(t[:, 2:3], t[:, 2:3], AF.Sqrt)
                            k.op("dve", "reciprocal", out=t[:, 3:4], in_=t[:, 2:3])
                            yn = ynr.next()
                            k.stt(yn, y2, t[:, 3:4], gb, ALU.mult, ALU.mult)
                            if DBG_AT < 5:
                                continue
                            pt = ptr_.next()
                            k.tr(pt, yn, ident_b)
                            c0 = q0 + s_ * 128
                            k.copy(yT[:, c0:c0 + 128], pt, eng="act")
                    lo = groups[0][0]
                    if DBG_AT >= 6:
                        k.dma("act", S["yT"][b, h * 128:(h + 1) * 128, lo:TT], yT[:, lo:TT])
        k.end_phase()

    def phase_gmlp(l):
        k.begin_phase()
        with contextlib.ExitStack() as es:
            wsT = k.sb("wsT", [128, 4, 128], BF16, es)
            with contextlib.ExitStack() as es2:
                ws_f = k.sb("ws_f", [128, 4, 128], F32, es2)
                k.dma("sp", ws_f, W["sg_w"][l].rearrange("g p q -> p g q"))
                pw = k.ps("sgpw", [128, 4, 128], F32, es2)
                for g in range(4):
                    k.tr(pw[:, g, :], ws_f[:, g, :], ident_f)
                k.copy(wsT, pw)
                k.barrier()
            bsT = k.sb("bsT", [128, 4], F32, es)
            k.dma("sp", bsT, W["sg_b"][l].rearrange("g p -> p g"), allow_slow_non_contiguous=True)
            ngb = k.sb("sgng", [128, 512], F32, es)
            nbb = k.sb("sgnb", [128, 512], F32, es)
            k.dma("sp", ngb, W["sg_norm_g"][l].unsqueeze(0).broadcast_to([128, 512]))
            k.dma("sp", nbb, W["sg_norm_b"][l].unsqueeze(0).broadcast_to([128, 512]))
            sgr = Ring([k.sb(f"sgt{i}", [128, 1024], F32, es) for i in range(3)])
            str_ = Ring([k.sb(f"sgst{i}", [128, 8], F32, es) for i in range(3)])
            vnr = Ring([k.sb(f"sgvn{i}", [128, 512], F32, es) for i in range(2)])
            vbr = Ring([k.sb(f"sgvb{i}", [128, 512], BF16, es) for i in range(2)])
            ysr = Ring([k.sb(f"sgy{i}", [128, 512], BF16, es) for i in range(2)])
            yTr = Ring([k.sb(f"sgyT{i}", [128, 4, 128], BF16, es) for i in range(2)])
            pmr = Ring([k.ps(f"sgpm{i}", [128, 512], F32, es) for i in range(2)])
            ptr_ = Ring([k.ps(f"sgpt{i}", [128, 4, 128], BF16, es) for i in range(2)])
            for b in range(NB):
                for i in range(NT):
                    if l == DEPTH - 1 and i < NTc:
                        continue
                    sg = sgr.next()
                    k.dma("sp", sg, S["sg"][b, i * 128:(i + 1) * 128, :])
                    st = str_.next()
                    k.op("dve", "bn_stats", out=st[:, 0:6], in_=sg[:, 512:1024])
                    k.op("dve", "bn_aggr", out=st[:, 6:8], in_=st[:, 0:6])
                    k.ts(st[:, 7:8], st[:, 7:8], LN_EPS, ALU.add)
                    k.act(st[:, 7:8], st[:, 7:8], AF.Sqrt)
                    k.op("dve", "reciprocal", out=st[:, 7:8], in_=st[:, 7:8])
                    vn = vnr.next()
                    k.ts(vn, sg[:, 512:1024], st[:, 6:7], ALU.subtract, st[:, 7:8], ALU.mult)
                    k.tt(vn, vn, ngb, ALU.mult, eng="pool")
                    vb = vbr.next()
                    k.tt(vb, vn, nbb, ALU.add)
                    pm = pmr.next()
                    for g in range(4):
                        k.mm(pm[:, g * 128:(g + 1) * 128], wsT[:, g, :], vb[:, g * 128:(g + 1) * 128])
                    ys = ysr.next()
                    for g in range(4):
                        k.stt(ys[:, g * 128:(g + 1) * 128], pm[:, g * 128:(g + 1) * 128], bsT[:, g:g + 1],
                              sg[:, g * 128:(g + 1) * 128], ALU.add, ALU.mult)
                    pt = ptr_.next()
                    for g in range(4):
                        k.tr(pt[:, g, :], ys[:, g * 128:(g + 1) * 128], ident_b)
                    yT = yTr.next()
                    k.copy(yT, pt, eng="act")
                    k.dma("act", S["yT"][b, 1024:1536, i * 128:(i + 1) * 128].rearrange("(g c) t -> c g t", c=128), yT)
        k.end_phase()

    def phase_rwkv(l):
        k.begin_phase()
        last = (l == DEPTH - 1)
        H = lambda v: v.rearrange("p (h c) -> p h c", h=8)
        with contextlib.ExitStack() as es:
            cg = k.group("rwc")
            def bcast(name, src, n):
                t = k.sb(name, [128, n], F32, es)
                k.dma("sp", t, src.unsqueeze(0).broadcast_to([128, n]), grp=cg)
                return t
            mu0b = bcast("rmu0", W["rw_shift_mu"][l, 0], RW_COLS)
            mu1b = bcast("rmu1", W["rw_shift_mu"][l, 1], RW_COLS)
            w0b = [bcast(f"rw0b{z}", W["rw_w0"][l, z], 512) for z in range(2)]
            a0b = [bcast(f"ra0b{z}", W["rw_a0"][l, z], 512) for z in range(2)]
            kkb = bcast("rkkb", W["rw_k_k"][l], 512)
            kab = bcast("rkab", W["rw_k_a"][l], 512)
            omkab = k.sb("romka", [128, 512], F32, es)
            rkb = bcast("rrkb", W["rw_r_k"][l].rearrange("h c -> (h c)"), 512)
            lngb = bcast("rlng", W["rw_ln_g"][l], 512)
            lnbb = bcast("rlnb", W["rw_ln_b"][l], 512)
            w2 = [k.sb(f"rw2_{z}", [64, 512], F32, es) for z in range(2)]
            a2 = [k.sb(f"ra2_{z}", [64, 512], F32, es) for z in range(2)]
            for z in range(2):
                k.dma("sp", w2[z], W["rw_w2"][l, z], grp=cg)
                k.dma("sp", a2[z], W["rw_a2"][l, z], grp=cg)
            g2 = k.sb("rg2", [128, 512], F32, es)
            k.dma("sp", g2, W["rw_g2"][l], grp=cg)
            ones = k.sb("rones", [128, 1], F32, es)
            tri, mc1, mc2, mnsl = [], [], [], []
            for z in range(2):
                t = k.sb(f"rtri{z}", [128, 128], F32, es); k.dma("sp", t, CONST["tri"][z], grp=cg); tri.append(t)
                t = k.sb(f"rmc1{z}", [128, 256], F32, es)
                k.dma("sp", t[:, 0:128], CONST["m_su"][z], grp=cg); k.dma("sp", t[:, 128:256], CONST["m_iu"][z], grp=cg); mc1.append(t)
                t = k.sb(f"rmc2{z}", [128, 256], F32, es)
                k.dma("sp", t[:, 0:128], CONST["m_nsu"][z], grp=cg); k.dma("sp", t[:, 128:256], CONST["m_iu"][z], grp=cg); mc2.append(t)
                t = k.sb(f"rmnsl{z}", [128, 128], F32, es); k.dma("sp", t, CONST["m_nsl"][z], grp=cg); mnsl.append(t)

            k.memset(ones, 1.0)
            k.ts(omkab, kab, -1.0, ALU.mult, 1.0, ALU.add)
            cur = k.sb("rcur", [128, RW_COLS], F32, es)
            prv = k.sb("rprv", [128, RW_COLS], F32, es)
            nxt = k.sb("rnxt", [128, RW_COLS], F32, es)
            psh = k.sb("rpsh", [128, RW_COLS], F32, es)
            sm3 = k.sb("rsm3", [128, 256], F32, es)
            smT = k.sb("rsmT", [128, 3, 128], F32, es)
            F = {n: k.sb("rf_" + n, [128, 512], F32, es) for n in
                 ("lw", "a", "kkp", "kk", "kd", "bb", "epos", "eneg", "eposx", "t0", "t1")}
            for n in ("kap", "rt", "kt", "bt", "vb"):
                F[n] = k.sb("rf_" + n, [128, 512], BF16, es)
            ss8 = k.sb("rss8", [128, 8], F32, es)
            gC = k.sb("rgC", [64, 8], F32, es)
            KR = k.sb("rKR", [128, 8, 2, 128], BF16, es)
            k.memset(KR, 0.0)
            KT = k.sb("rKT", [64, 8, 128], BF16, es)
            BT = k.sb("rBT", [64, 8, 128], BF16, es)
            A = [[k.sb(f"rA{b}_{h}", [128, 64], F32, es) for h in range(8)] for b in range(NB)]
            Ab = [[k.sb(f"rAb{b}_{h}", [128, 64], BF16, es) for h in range(8)] for b in range(NB)]
            ytile = k.sb("rytile", [128, 512], F32, es)
            y0t = k.sb("ry0t", [128, 512], F32, es)
            aux = k.sb("raux", [128, 1024], F32, es)
            gst = k.sb("rgst", [128, 16], F32, es)
            yb = k.sb("ryb", [128, 512], BF16, es)
            yTs = k.sb("ryTs", [128, 4, 128], BF16, es)
            pwa_b = k.ps("rBf0", [128, 512], F32, es)
            pwa = Ring([pwa_b])
            Bf1 = k.ps("rBf1", [128, 512], F32, es)
            ptS = Bf1[:, 0:384].rearrange("p (a t) -> p a t", a=3)
            pgc = Bf1[0:64, 384:392]
            ptF = Bf1
            Bb = Bf1.bitcast(BF16)[:, 0:512].rearrange("p (a t) -> p a t", a=4)
            lanes = []
            for ln in range(2):
                X = k.ps(f"rX{ln}", [128, 512], F32, es)
                Y = k.ps(f"rY{ln}", [128, 512], F32, es)
                Sb = k.ps(f"rS{ln}", [128, 512], F32, es)
                lanes.append(dict(
                    ps1=X[:, 0:256], ps2=X[:, 256:512], p5=Y[:, 0:256], p6=Y[:, 256:384],
                    pWT=Sb[0:64, 0:128], ps3=Sb[:, 128:256], ps4=Sb[:, 256:320], psz=Sb[:, 320:384],
                    psy=Sb[:, 384:448], psa=Sb[0:64, 448:512],
                    tmpA=k.sb(f"rtA{ln}", [64, 64], F32, es),
                    lmk=k.sb(f"rLMk{ln}", [128, 256], BF16, es),
                    mbt=k.sb(f"rMbT{ln}", [128, 128], BF16, es),
                    PT=Ring([k.sb(f"rPT{ln}_{i}", [128, 128], F32, es) for i in range(2)]),
                    PR=Ring([k.sb(f"rPR{ln}_{i}", [128, 256], F32, es) for i in range(2)]),
                    WT=k.sb(f"rWT{ln}", [64, 128], BF16, es),
                    Z=k.sb(f"rZ{ln}", [128, 64], BF16, es)))
            for b in range(NB):
                for h in range(8):
                    k.memset(A[b][h], 0.0)
                    k.memset(Ab[b][h], 0.0)

            def features(b, i, z):
                r0 = rw_row(cfg, i * 128)
                k.dma("sp", cur, S["rw"][b, r0:r0 + 128, :])
                k.dma("sp", prv, S["rw"][b, r0 - 1:r0 + 127, :])
                k.dma("sp", nxt, S["rw"][b, r0 + 1:r0 + 129, :])
                k.tt(prv, prv, cur, ALU.subtract, eng="pool")
                k.tt(prv, prv, mu0b, ALU.mult, eng="pool")
                k.tt(nxt, nxt, cur, ALU.subtract)
                k.tt(nxt, nxt, mu1b, ALU.mult)
                k.tt(psh, cur, prv, ALU.add, eng="pool")
                k.tt(psh, psh, nxt, ALU.add)
                r_, k_, v_ = psh[:, 0:512], psh[:, 512:1024], psh[:, 1024:1536]
                k.act(sm3[:, 0:64], psh[:, 1536:1600], AF.Tanh)
                k.act(sm3[:, 128:256], psh[:, 1664:1792], AF.Sigmoid)
                k.tr(ptS[0:64, 0, :], sm3[:, 0:64], ident_f)
                k.tr(ptS[0:64, 1, :], psh[:, 1600:1664], ident_f)
                k.tr(ptS[:, 2, :], sm3[:, 128:256], ident_f)
                k.copy(smT[0:64, 0:2, :], ptS[0:64, 0:2, :], eng="act")
                k.copy(smT[:, 2, :], ptS[:, 2, :], eng="act")
                lw, a_, kkp, kk, kd, bb = F["lw"], F["a"], F["kkp"], F["kk"], F["kd"], F["bb"]
                t0, t1 = F["t0"], F["t1"]
                pw = pwa.next()
                k.mm(pw, smT[0:64, 0, :], w2[z])
                k.tt(t0, pw, w0b[z], ALU.add)
                k.act(lw, t0, AF.Sigmoid)
                k.ts(lw, lw, -math.exp(-0.5), ALU.mult)
                pa = pwa.next()
                k.mm(pa, smT[0:64, 1, :], a2[z])
                k.tt(t0, pa, a0b[z], ALU.add)
                k.act(a_, t0, AF.Sigmoid)
                k.tt(kkp, k_, kkb, ALU.mult, eng="pool")
                k.tt(t1, kkp, kkp, ALU.mult, eng="pool")
                k.op("dve", "tensor_reduce", out=ss8, in_=H(t1), axis=AX.X, op=ALU.add)
                k.act(ss8, ss8, AF.Sqrt)
                k.ts(ss8, ss8, 1e-12, ALU.max)
                k.op("dve", "reciprocal", out=ss8, in_=ss8)
                k.tt(H(kk), H(kkp), ss8.unsqueeze(2).to_broadcast([128, 8, 64]), ALU.mult)
                k.tt(t0, a_, kab, ALU.mult)
                k.tt(t0, t0, omkab, ALU.add)
                k.tt(kd, k_, t0, ALU.mult)
                k.tt(bb, a_, kk, ALU.mult, eng="pool")
                pc = pwa.next()
                k.mm(pc, tri[z], lw)
                k.act(F["epos"], pc, AF.Exp)
                k.act(F["eneg"], pc, AF.Exp, scale=-1.0)
                k.tt(t0, pc, lw, ALU.subtract)
                k.act(F["eposx"], t0, AF.Exp)
                k.copy(F["vb"], v_, eng="act")
                k.tt(F["kap"], kk, F["eposx"], ALU.mult)
                k.tt(F["rt"], r_, F["epos"], ALU.mult, eng="pool")
                k.tt(F["kt"], kd, F["eneg"], ALU.mult)
                k.tt(F["bt"], bb, F["eneg"], ALU.mult, eng="pool")
                if DBG_RW < 2:
                    return v_
                for h in range(8):
                    k.mm(pgc[:, h:h + 1], lw[:, h * 64:(h + 1) * 64], ones)
                k.act(gC, pgc, AF.Exp)
                if DBG_RW < 3:
                    return v_
                if z == 0:
                    pa = pwa.next()
                    k.mm(pa, smT[0:64, 1, :], a2[1])
                    k.tt(t0, pa, a0b[1], ALU.add)
                    k.act(t1, t0, AF.Sigmoid)
                    k.tt(t1, t1, a_, ALU.add)
                    k.ts(t1, t1, 0.5, ALU.mult)
                    k.tt(t1, t1, kab, ALU.mult)
                    k.tt(t1, t1, omkab, ALU.add)
                    k.tt(t1, t1, k_, ALU.mult)
                    k.tt(t1, t1, r_, ALU.mult)
                    k.tt(t1, t1, rkb, ALU.mult)
                    k.op("dve", "tensor_reduce", out=ss8, in_=H(t1), axis=AX.X, op=ALU.add)
                    k.tt(H(aux[:, 0:512]), H(v_), ss8.unsqueeze(2).to_broadcast([128, 8, 64]), ALU.mult)
                    pg = pwa.next()
                    k.mm(pg, smT[:, 2, :], g2)
                    k.copy(aux[:, 512:1024], pg, eng="act")
                    k.dma("act", S["rwaux"][b, i * 128:(i + 1) * 128, :], aux)
                if DBG_RW < 4:
                    return v_
                ptFb = ptF.bitcast(BF16)
                ptFv = ptFb[0:64, 0:512].rearrange("p (a c t) -> p a c t", a=2, c=2)
                for hp in range(4):
                    for hh in range(2):
                        h = hp * 2 + hh
                        k.tr(ptFv[:, hh, 0, :], F["kap"][:, h * 64:(h + 1) * 64], ident_b)
                        k.tr(ptFv[:, hh, 1, :], F["rt"][:, h * 64:(h + 1) * 64], ident_b)
                    k.copy(KR[0:64, hp * 2:hp * 2 + 2, :, :], ptFv, eng="act")
                ptF4 = ptFb[0:64, 0:512].rearrange("p (a t) -> p a t", a=4)
                for (src, dstT) in ((F["kt"], KT), (F["bt"], BT)):
                    for hq in range(2):
                        for hh in range(4):
                            h = hq * 4 + hh
                            k.tr(ptF4[:, hh, :], src[:, h * 64:(h + 1) * 64], ident_b)
                        k.copy(dstT[:, hq * 4:(hq + 1) * 4, :], ptF4, eng="act")
                return F["vb"]

            def head_chunk(L, b, z, h, v_, emit):
                hs = slice(h * 64, (h + 1) * 64)
                Vh = v_[:, hs]
                KRh = KR[0:64, h, :, :].rearrange("p a t -> p (a t)")
                lmk, mbt = L["lmk"], L["mbt"]
                k.mm(L["ps1"], KT[:, h, :], KRh)
                k.mm(L["ps2"], BT[:, h, :], KRh)
                k.tt(lmk, L["ps1"], mc1[z], ALU.mult)
                PT = L["PT"].next()
                k.tt(PT, L["ps2"][:, 0:128], mc2[z][:, 0:128], ALU.mult)
                k.tt(mbt, L["ps2"][:, 128:256], mc2[z][:, 128:256], ALU.mult)
                yield
                if DBG_RW < 6:
                    return
                k.mm(L["ps3"], KR[0:64, h, 0, :], BT[:, h, :])
                k.mm(L["ps4"], lmk[:, 0:128], Vh)
                PR = L["PR"].next()
                k.tt(PR[:, 0:128], L["ps3"], mnsl[z], ALU.mult)
                k.copy(PR[:, 128:192], F["kap"][:, hs], eng="pool")
                k.copy(PR[:, 192:256], L["ps4"], eng="act")
                yield
                if DBG_RW < 7:
                    return
                for it in range(7):
                    k.mm(L["p5"], PT, PR)
                    PRn = L["PR"].next()
                    if it < 6:
                        k.copy(PRn[:, 0:128], L["p5"][:, 0:128], eng="act")
                    k.tt(PRn[:, 128:256], PR[:, 128:256], L["p5"][:, 128:256], ALU.add)
                    if it < 6:
                        k.tr(L["p6"], PRn[:, 0:128], ident_f)
                        PTn = L["PT"].next()
                        k.copy(PTn, L["p6"], eng="act")
                        PT = PTn
                    PR = PRn
                    yield
                if DBG_RW < 8.1:
                    return
                k.tr(L["pWT"], PR[:, 128:192], ident_f)
                k.copy(L["WT"], L["pWT"], eng="act")
                yield
                if DBG_RW < 8.2:
                    return
                Ah = A[b][h]
                Abh = Ab[b][h]
                k.mm(L["psz"], L["WT"], Abh[0:64, :])
                k.stt(L["Z"], PR[:, 192:256], -1.0, L["psz"], ALU.mult, ALU.subtract)
                yield
                if DBG_RW < 8.4:
                    return
                if DBG_RW < 8.6:
                    emit = False
                if emit and DBG_RW < 8.63:
                    k.mm(L["psy"], mbt, L["Z"], start=True, stop=False)
                    k.mm(L["psy"], lmk[:, 128:256], Vh, start=False, stop=True)
                elif emit:
                    k.mm(L["psy"], KR[:, h, 1, :], Abh, start=True, stop=False)
                    k.mm(L["psy"], mbt, L["Z"], start=False, stop=False)
                    k.mm(L["psy"], lmk[:, 128:256], Vh, start=False, stop=True)
                k.mm(L["psa"], F["bt"][:, hs], L["Z"], start=True, stop=False)
                k.mm(L["psa"], F["kt"][:, hs], Vh, start=False, stop=True)
                if emit and DBG_RW >= 8.65:
                    k.copy(ytile[:, hs], L["psy"], eng="dve")
                k.tt(L["tmpA"], L["psa"], Ah[0:64, :], ALU.add)
                k.ts(Ah[0:64, :], L["tmpA"], gC[:, h:h + 1], ALU.mult)
                k.copy(Abh[0:64, :], Ah[0:64, :], eng="act")
                yield

            def run_heads(b, z, v_, emit):
                for h0 in range(0, 8, 2):
                    gens = [head_chunk(lanes[j], b, z, h0 + j, v_, emit) for j in range(2)]
                    alive = [True, True]
                    while any(alive):
                        for j in range(2):
                            if alive[j]:
                                try:
                                    next(gens[j])
                                except StopIteration:
                                    alive[j] = False

            def finalize(b, i):
                k.dma("sp", y0t, S["y0"][b, i * 128:(i + 1) * 128, :])
                k.dma("sp", aux, S["rwaux"][b, i * 128:(i + 1) * 128, :])
                t0, t1 = F["t0"], F["t1"]
                k.tt(t0, ytile, y0t, ALU.add)
                k.op("dve", "tensor_reduce", out=gst[:, 0:8], in_=H(t0), axis=AX.X, op=ALU.add)
                k.ts(gst[:, 0:8], gst[:, 0:8], 1.0 / 64, ALU.mult)
                k.tt(H(t0), H(t0), gst[:, 0:8].unsqueeze(2).to_broadcast([128, 8, 64]), ALU.subtract)
                k.tt(t1, t0, t0, ALU.mult, eng="pool")
                k.op("dve", "tensor_reduce", out=gst[:, 8:16], in_=H(t1), axis=AX.X, op=ALU.add)
                k.ts(gst[:, 8:16], gst[:, 8:16], 1.0 / 64, ALU.mult, RW_GN_EPS, ALU.add)
                k.act(gst[:, 8:16], gst[:, 8:16], AF.Sqrt)
                k.op("dve", "reciprocal", out=gst[:, 8:16], in_=gst[:, 8:16])
                k.tt(H(t0), H(t0), gst[:, 8:16].unsqueeze(2).to_broadcast([128, 8, 64]), ALU.mult)
                k.tt(t0, t0, lngb, ALU.mult, eng="pool")
                k.tt(t0, t0, lnbb, ALU.add)
                k.tt(t0, t0, aux[:, 0:512], ALU.add, eng="pool")
                k.tt(yb, t0, aux[:, 512:1024], ALU.mult)
                for g in range(4):
                    k.tr(Bb[:, g, :], yb[:, g * 128:(g + 1) * 128], ident_b)
                k.copy(yTs, Bb, eng="act")
                k.dma("act", S["yT"][b, 512:1024, i * 128:(i + 1) * 128].rearrange("(g c) t -> c g t", c=128), yTs)

            for z in range(2):
                if z == 0:
                    order = list(range(NT))
                else:
                    order = list(range(NTc - 1, -1, -1)) + list(range(NT - 1, NTc - 1, -1))
                    for b in range(NB):
                        for h in range(8):
                            k.memset(A[b][h], 0.0)
                            k.memset(Ab[b][h], 0.0)
                for i in order:
                    emit = not (last and i < NTc)
                    for b in range(NB):
                        v_ = features(b, i, z)
                        if DBG_RW < 5:
                            continue
                        run_heads(b, z, v_, emit)
                        if DBG_RW < 10:
                            continue
                        if emit:
                            if z == 0:
                                k.dma("act", S["y0"][b, i * 128:(i + 1) * 128, :], ytile)
                            else:
                                finalize(b, i)
        k.end_phase()

    def phase_merge(l):
        k.begin_phase()
        last = (l == DEPTH - 1)
        lo = Tc if last else 0
        with contextlib.ExitStack() as es:
            wbr = k.sb("wbr", [128, 12, D], BF16, es)
            k.dma("pool", wbr, W["w_branch"][l].rearrange("(k p) d -> p k d", p=128))
            wo = k.sb("wo", [128, 8, D], BF16, es)
            k.dma("pool", wo, W["w_out"][l].rearrange("(k p) d -> p k d", p=128))
            wrt = k.sb("wrt", [128, 8, NE], F32, es)
            k.dma("sp", wrt, W["w_router"][l].rearrange("(k p) e -> p k e", p=128))
            lg = k.sb("ln1g", [128, D], F32, es)
            lb = k.sb("ln1b", [128, D], F32, es)
            k.dma("sp", lg, W["ln1_g"][l].unsqueeze(0).broadcast_to([128, D]))
            k.dma("sp", lb, W["ln1_b"][l].unsqueeze(0).broadcast_to([128, D]))
            bc = {}
            for nm, slot in (("g1", 2), ("sh2", 3), ("sc2", 4)):
                for which in range(2):
                    bc[nm, which] = k.sb(f"bc_{nm}{which}", [128, D], F32, es)
            affT = k.sb("affT", [NE, TT], F32, es)
            yTr = Ring([k.sb(f"myT{i}", [128, 12, 256], BF16, es) for i in range(2)])
            gTr = Ring([k.sb(f"mgT{i}", [128, 24, 256], BF16, es) for i in range(2)])
            mTr = Ring([k.sb(f"mmT{i}", [128, 8, 256], BF16, es) for i in range(2)])
            maccr = Ring([k.sb(f"macc{i}", [128, 256], F32, es) for i in range(2)])
            mtmpr = Ring([k.sb(f"mtmp{i}", [128, 256], F32, es) for i in range(2)])
            xr = Ring([k.sb(f"mx{i}", [128, D], F32, es) for i in range(2)])
            zr = Ring([k.sb(f"mz{i}", [128, D], F32, es) for i in range(2)])
            xmr = Ring([k.sb(f"mxm{i}", [128, D], F32, es) for i in range(2)])
            h2r = Ring([k.sb(f"mh2{i}", [128, D], F32, es) for i in range(2)])
            h2br = Ring([k.sb(f"mh2b{i}", [128, D], BF16, es) for i in range(2)])
            h2Tr = Ring([k.sb(f"mh2T{i}", [128, 8, 128], F32, es) for i in range(2)])
            str_ = Ring([k.sb(f"mst{i}", [128, 16], F32, es) for i in range(3)])
            smr = Ring([k.sb(f"msm{i}", [128, 4], F32, es) for i in range(3)])
            er = Ring([k.sb(f"mex{i}", [128, NE], F32, es) for i in range(2)])
            pmm = Ring([k.ps(f"mpm{i}", [128, 512], F32, es) for i in range(2)])
            pmix = Ring([k.ps(f"mpx{i}", [128, 512], F32, es) for i in range(2)])
            ptr_ = Ring([k.ps(f"mpt{i}", [128, 4, 128], F32, es) for i in range(2)])
            plq = k.ps("mplq", [128, 512], F32, es)
            plg = plq[:, 0:NE]
            paf = plq[0:NE, 128:256]
            for nm, slot in (("g1", 2), ("sh2", 3), ("sc2", 4)):
                k.dma("sp", bc[nm, 1], S["mod"][l, NB, slot * D:(slot + 1) * D].unsqueeze(0).broadcast_to([128, D]))
            k.ts(bc["sc2", 1], bc["sc2", 1], 1.0, ALU.add)
            for b in range(NB):
                for nm, slot in (("g1", 2), ("sh2", 3), ("sc2", 4)):
                    k.dma("sp", bc[nm, 0], S["mod"][l, b, slot * D:(slot + 1) * D].unsqueeze(0).broadcast_to([128, D]))
                k.ts(bc["sc2", 0], bc["sc2", 0], 1.0, ALU.add)
                for (tb0, n) in [(lo + a, n_) for (a, n_) in blocks(TT - lo, 256)]:
                    yTb, gTb, mT = yTr.next(), gTr.next(), mTr.next()
                    k.dma("sp", yTb[:, :, 0:n], S["yT"][b, :, tb0:tb0 + n].rearrange("(k p) t -> p k t", p=128))
                    k.dma("sp", gTb[:, :, 0:n], S["gT"][b, :, tb0:tb0 + n].rearrange("(k p) t -> p k t", p=128))
                    for dc in range(8):
                        macc = maccr.next()
                        for br in range(3):
                            ps = pmm.next()
                            for kc in range(4):
                                k.mm(ps[:, 0:n], wbr[:, br * 4 + kc, dc * 128:(dc + 1) * 128], yTb[:, br * 4 + kc, 0:n],
                                     start=(kc == 0), stop=(kc == 3))
                            if br == 0:
                                k.tt(macc[:, 0:n], ps[:, 0:n], gTb[:, dc, 0:n], ALU.mult)
                            else:
                                tmp = mtmpr.next()
                                k.tt(tmp[:, 0:n], ps[:, 0:n], gTb[:, br * 8 + dc, 0:n], ALU.mult)
                                if br == 1:
                                    k.tt(macc[:, 0:n], macc[:, 0:n], tmp[:, 0:n], ALU.add, eng="pool")
                                else:
                                    k.tt(mT[:, dc, 0:n], macc[:, 0:n], tmp[:, 0:n], ALU.add, eng="pool")
                    for s_ in range(n // 128):
                        t0 = tb0 + s_ * 128
                        which = 1 if t0 < Tc else 0
                        xt = xr.next()
                        k.dma("sp", xt, S["xcur"][b, t0:t0 + 128, :])
                        z = zr.next()
                        for half in range(2):
                            px = pmix.next()
                            for kk in range(8):
                                k.mm(px, mT[:, kk, s_ * 128:(s_ + 1) * 128], wo[:, kk, half * 512:(half + 1) * 512],
                                     start=(kk == 0), stop=(kk == 7))
                            k.tt(z[:, half * 512:(half + 1) * 512], px, bc["g1", which][:, half * 512:(half + 1) * 512], ALU.mult)
                        k.stt(z, xt, DN_ALPHA, z, ALU.mult, ALU.add)
                        st = str_.next()
                        k.op("dve", "bn_stats", out=st[:, 0:6], in_=z[:, 0:512])
                        k.op("dve", "bn_stats", out=st[:, 6:12], in_=z[:, 512:1024])
                        k.op("dve", "bn_aggr", out=st[:, 12:14], in_=st[:, 0:12])
                        k.ts(st[:, 13:14], st[:, 13:14], LN_EPS, ALU.add)
                        k.act(st[:, 13:14], st[:, 13:14], AF.Sqrt)
                        k.op("dve", "reciprocal", out=st[:, 13:14], in_=st[:, 13:14])
                        xm = xmr.next()
                        k.ts(xm, z, st[:, 12:13], ALU.subtract, st[:, 13:14], ALU.mult)
                        k.tt(xm, xm, lg, ALU.mult, eng="pool")
                        k.tt(xm, xm, lb, ALU.add)
                        k.dma("act", S["xmid"][b, t0:t0 + 128, :], xm)
                        h2 = h2r.next()
                        k.tt(h2, xm, bc["sc2", which], ALU.mult, eng="pool")
                        k.tt(h2, h2, bc["sh2", which], ALU.add)
                        h2b = h2br.next()
                        k.copy(h2b, h2, eng="act")
                        k.dma("act", S["h2"][b, t0:t0 + 128, :], h2b)
                        h2T = h2Tr.next()
                        for g in range(2):
                            pt = ptr_.next()
                            for j in range(4):
                                k.tr(pt[:, j, :], h2[:, (g * 4 + j) * 128:(g * 4 + j + 1) * 128], ident_f)
                            k.copy(h2T[:, g * 4:(g + 1) * 4, :], pt, eng="act")
                        for kk in range(8):
                            k.mm(plg, h2T[:, kk, :], wrt[:, kk, :], start=(kk == 0), stop=(kk == 7))
                        sm = smr.next()
                        k.op("dve", "tensor_reduce", out=sm[:, 0:1], in_=plg, axis=AX.X, op=ALU.max)
                        k.ts(sm[:, 0:1], sm[:, 0:1], -1.0, ALU.mult)
                        e = er.next()
                        k.act(e, plg, AF.Exp, bias=sm[:, 0:1], accum_out=sm[:, 1:2])
                        k.op("dve", "reciprocal", out=sm[:, 2:3], in_=sm[:, 1:2])
                        k.ts(e, e, sm[:, 2:3], ALU.mult)
                        k.tr(paf, e, ident_f)
                        k.copy(affT[:, t0:t0 + 128], paf, eng="act")
                k.dma("sp", S["aff"][b, :, lo:TT], affT[:, lo:TT])
        k.end_phase()

    def moe_segs(l):
        segs = []
        if l != DEPTH - 1:
            segs.append((0, 0, Tc, cfg.cap_ctx))
        segs.append((1, Tc, T, cfg.cap_lat))
        return segs

    def phase_route(l):
        k.begin_phase()
        with contextlib.ExitStack() as es:
            iota = k.sb("iota", [128, 256], F32, es)
            k.dma("sp", iota, CONST["iota"])
            zeros = k.sb("rzeros", [NE, T], F32, es)
            k.memset(zeros, 0.0)
            aff = k.sb("raff", [NE, T], F32, es)
            work = k.sb("rwork", [NE, T], F32, es)
            G = k.sb("rG", [NE, T], F32, es)
            mask = k.sb("rmask", [NE, T], F32, es)
            slot = k.sb("rslot", [NE, T], F32, es)
            mx8 = k.sb("rmx8", [NE, 8], F32, es)
            ntm = T // 128
            slotT = k.sb("rslotT", [128, ntm, NE], F32, es)
            GT = k.sb("rGT", [128, ntm, NE], F32, es)
            h2sb = k.sb("rh2", [128, ntm, D], BF16, es)
            selr = Ring([k.sb(f"rsel{i}", [128, ntm, cfg.cap_lat], BF16, es) for i in range(2)])
            xer = Ring([k.sb(f"rxe{i}", [128, 8, cfg.cap_lat], BF16, es) for i in range(2)])
            ptq = k.ps("rptq", [128, 2, NE], F32, es)
            pgr = Ring([k.ps(f"rpg{i}", [128, 512], F32, es) for i in range(4)])
            for b in range(NB):
                for (sid, tlo, Ts, cap) in moe_segs(l):
                    nt = Ts // 128
                    k.dma("sp", aff[:, 0:Ts], S["aff"][b, :, tlo:tlo + Ts])
                    k.dma("sp", h2sb[:, 0:nt, :], S["h2"][b, tlo:tlo + Ts, :].rearrange("(j p) d -> p j d", p=128))
                    k.copy(work[:, 0:Ts], aff[:, 0:Ts])
                    for _ in range(cap // 8):
                        k.op("dve", "max", out=mx8, in_=work[:, 0:Ts])
                        k.op("dve", "match_replace", out=work[:, 0:Ts], in_to_replace=mx8, in_values=work[:, 0:Ts], imm_value=0.0)
                    k.tt(G[:, 0:Ts], aff[:, 0:Ts], work[:, 0:Ts], ALU.subtract)
                    k.ts(mask[:, 0:Ts], G[:, 0:Ts], 0.0, ALU.is_gt)
                    k.op("dve", "tensor_tensor_scan", out=slot[:, 0:Ts], data0=mask[:, 0:Ts], data1=zeros[:, 0:Ts],
                         initial=0.0, op0=ALU.add, op1=ALU.add)
                    k.tt(slot[:, 0:Ts], slot[:, 0:Ts], mask[:, 0:Ts], ALU.mult)
                    k.ts(slot[:, 0:Ts], slot[:, 0:Ts], -1.0, ALU.add)
                    for i in range(nt):
                        k.tr(ptq[:, 0, :], slot[:, i * 128:(i + 1) * 128], ident_f[0:NE, 0:NE])
                        k.tr(ptq[:, 1, :], G[:, i * 128:(i + 1) * 128], ident_f[0:NE, 0:NE])
                        k.copy(slotT[:, i, :], ptq[:, 0, :], eng="act")
                        k.copy(GT[:, i, :], ptq[:, 1, :], eng="act")
                    k.dma("act", S["slotT"][b, tlo:tlo + Ts, :].rearrange("(j p) e -> p j e", p=128), slotT[:, 0:nt, :])
                    k.dma("act", S["GT"][b, tlo:tlo + Ts, :].rearrange("(j p) e -> p j e", p=128), GT[:, 0:nt, :])
                    for e in range(NE):
                        sel = selr.next()
                        for i in range(nt):
                            k.ts(sel[:, i, 0:cap], iota[:, 0:cap], slotT[:, i, e:e + 1], ALU.is_equal,
                                 eng=("pool" if i % 2 else "dve"))
                        xe = xer.next()
                        for kk in range(8):
                            pg = pgr.next()
                            for i in range(nt):
                                k.mm(pg[:, 0:cap], h2sb[:, i, kk * 128:(kk + 1) * 128], sel[:, i, 0:cap],
                                     start=(i == 0), stop=(i == nt - 1))
                            k.copy(xe[:, kk, 0:cap], pg[:, 0:cap], eng=("act" if kk % 2 else "dve"))
                        k.dma("act", S["xe"][b, sid, e][:, 0:cap].rearrange("(k p) c -> p k c", p=128), xe[:, :, 0:cap])
        k.end_phase()

    def phase_experts(l):
        k.begin_phase()
        segs = moe_segs(l)
        with contextlib.ExitStack() as es:
            wgr = Ring([k.sb(f"ewg{i}", [128, 8, 1024], BF16, es) for i in range(2)])
            wur = Ring([k.sb(f"ewu{i}", [128, 8, 1024], BF16, es) for i in range(2)])
            wdr = Ring([k.sb(f"ewd{i}", [128, 8, 1024], BF16, es) for i in range(2)])
            xer = Ring([k.sb(f"exe{i}", [128, 8, cfg.cap_lat], BF16, es) for i in range(2)])
            hidr = Ring([k.sb(f"ehid{i}", [128, 8, cfg.cap_lat], BF16, es) for i in range(2)])
            sgr = Ring([k.sb(f"esg{i}", [128, cfg.cap_lat], F32, es) for i in range(2)])
            yacc = {}
            for b in range(NB):
                for (sid, tlo, Ts, cap) in segs:
                    yacc[b, sid] = k.sb(f"eya{b}_{sid}", [128, (cap + 127) // 128, D], F32, es)
            yor = Ring([k.sb(f"eyo{i}", [128, 512], BF16, es) for i in range(3)])
            pgr = Ring([k.ps(f"epg{i}", [128, 512], F32, es) for i in range(2)])
            pur = Ring([k.ps(f"epu{i}", [128, 512], F32, es) for i in range(2)])
            pyr = Ring([k.ps(f"epy{i}", [128, 512], F32, es) for i in range(3)])
            for e in range(NE):
                for half in range(2):
                    wg, wu, wd = wgr.next(), wur.next(), wdr.next()
                    f0 = half * 1024
                    k.dma("pool", wg, W["w_e_gate"][l, e][:, f0:f0 + 1024].rearrange("(k p) f -> p k f", p=128))
                    k.dma("pool", wu, W["w_e_up"][l, e][:, f0:f0 + 1024].rearrange("(k p) f -> p k f", p=128))
                    k.dma("pool", wd, W["w_e_down"][l, e][f0:f0 + 1024, :].rearrange("(k p) d -> p k d", p=128))
                    for b in range(NB):
                        for (sid, tlo, Ts, cap) in segs:
                            xe = xer.next()
                            k.dma("sp", xe[:, :, 0:cap], S["xe"][b, sid, e][:, 0:cap].rearrange("(k p) c -> p k c", p=128))
                            hid = hidr.next()
                            for fc in range(8):
                                pg, pu = pgr.next(), pur.next()
                                for kk in range(8):
                                    k.mm(pg[:, 0:cap], wg[:, kk, fc * 128:(fc + 1) * 128], xe[:, kk, 0:cap],
                                         start=(kk == 0), stop=(kk == 7))
                                for kk in range(8):
                                    k.mm(pu[:, 0:cap], wu[:, kk, fc * 128:(fc + 1) * 128], xe[:, kk, 0:cap],
                                         start=(kk == 0), stop=(kk == 7))
                                sg = sgr.next()
                                k.act(sg[:, 0:cap], pg[:, 0:cap], AF.Silu)
                                k.tt(hid[:, fc, 0:cap], sg[:, 0:cap], pu[:, 0:cap], ALU.mult)
                            for ch in range((cap + 127) // 128):
                                M = min(128, cap - ch * 128)
                                for dh in range(2):
                                    py = pyr.next()
                                    for fc in range(8):
                                        k.mm(py[0:M, :], hid[:, fc, ch * 128:ch * 128 + M], wd[:, fc, dh * 512:(dh + 1) * 512],
                                             start=(fc == 0), stop=(fc == 7))
                                    ya = yacc[b, sid][0:M, ch, dh * 512:(dh + 1) * 512]
                                    if half == 0:
                                        k.copy(ya, py[0:M, :], eng="act")
                                    else:
                                        yo = yor.next()
                                        k.tt(yo[0:M, :], ya, py[0:M, :], ALU.add)
                                        r0 = e * cap + ch * 128
                                        k.dma("act", S["ye"][b, sid, r0:r0 + M, dh * 512:(dh + 1) * 512], yo[0:M, :])
        k.end_phase()

    def phase_scatter(l):
        k.begin_phase()
        last = (l == DEPTH - 1)
        with contextlib.ExitStack() as es:
            iota = k.sb("siota", [128, 256], F32, es)
            k.dma("sp", iota, CONST["iota"])
            lg = k.sb("ln2g", [128, D], F32, es)
            lb = k.sb("ln2b", [128, D], F32, es)
            k.dma("sp", lg, W["ln2_g"][l].unsqueeze(0).broadcast_to([128, D]))
            k.dma("sp", lb, W["ln2_b"][l].unsqueeze(0).broadcast_to([128, D]))
            g2b = [k.sb(f"g2b{i}", [128, D], F32, es) for i in range(2)]
            nKmax = NE * cfg.cap_lat // 128
            ye = k.sb("sye", [128, nKmax, D], BF16, es)
            ntm = T // 128
            slotT = k.sb("sslotT", [128, ntm, NE], F32, es)
            GT = k.sb("sGT", [128, ntm, NE], F32, es)
            selr = Ring([k.sb(f"ssel{i}", [128, NE * cfg.cap_lat], BF16, es) for i in range(2)])
            selTr = Ring([k.sb(f"sselT{i}", [128, nKmax, 128], BF16, es) for i in range(2)])
            xr = Ring([k.sb(f"sx{i}", [128, D], F32, es) for i in range(2)])
            zr = Ring([k.sb(f"sz{i}", [128, D], F32, es) for i in range(2)])
            xor_ = Ring([k.sb(f"sxo{i}", [128, D], F32, es) for i in range(2)])
            str_ = Ring([k.sb(f"sst{i}", [128, 16], F32, es) for i in range(3)])
            ptr_ = Ring([k.ps(f"spt{i}", [128, 8, 128], BF16, es) for i in range(2)])
            pfr = Ring([k.ps(f"spf{i}", [128, 512], F32, es) for i in range(4)])
            k.dma("sp", g2b[1], S["mod"][l, NB, 5 * D:6 * D].unsqueeze(0).broadcast_to([128, D]))
            for b in range(NB):
                k.dma("sp", g2b[0], S["mod"][l, b, 5 * D:6 * D].unsqueeze(0).broadcast_to([128, D]))
                for (sid, tlo, Ts, cap) in moe_segs(l):
                    nt = Ts // 128
                    nK = NE * cap // 128
                    which = 1 if sid == 0 else 0
                    k.dma("sp", ye[:, 0:nK, :], S["ye"][b, sid, 0:NE * cap, :].rearrange("(j p) d -> p j d", p=128))
                    k.dma("sp", slotT[:, 0:nt, :], S["slotT"][b, tlo:tlo + Ts, :].rearrange("(j p) e -> p j e", p=128))
                    k.dma("sp", GT[:, 0:nt, :], S["GT"][b, tlo:tlo + Ts, :].rearrange("(j p) e -> p j e", p=128))
                    for i in range(nt):
                        t0 = tlo + i * 128
                        sel = selr.next()
                        for e in range(NE):
                            k.ts(sel[:, e * cap:(e + 1) * cap], iota[:, 0:cap], slotT[:, i, e:e + 1], ALU.is_equal,
                                 GT[:, i, e:e + 1], ALU.mult, eng=("pool" if e % 2 else "dve"))
                        selT = selTr.next()
                        for j0 in range(0, nK, 8):
                            pt = ptr_.next()
                            nj = min(8, nK - j0)
                            for j in range(nj):
                                k.tr(pt[:, j, :], sel[:, (j0 + j) * 128:(j0 + j + 1) * 128], ident_b)
                            k.copy(selT[:, j0:j0 + nj, :], pt[:, 0:nj, :], eng="act")
                        xt = xr.next()
                        k.dma("sp", xt, S["xmid"][b, t0:t0 + 128, :])
                        z = zr.next()
                        for dh in range(2):
                            pf = pfr.next()
                            for j in range(nK):
                                k.mm(pf, selT[:, j, :], ye[:, j, dh * 512:(dh + 1) * 512], start=(j == 0), stop=(j == nK - 1))
                            k.tt(z[:, dh * 512:(dh + 1) * 512], pf, g2b[which][:, dh * 512:(dh + 1) * 512], ALU.mult)
                        k.stt(z, xt, DN_ALPHA, z, ALU.mult, ALU.add)
                        st = str_.next()
                        k.op("dve", "bn_stats", out=st[:, 0:6], in_=z[:, 0:512])
                        k.op("dve", "bn_stats", out=st[:, 6:12], in_=z[:, 512:1024])
                        k.op("dve", "bn_aggr", out=st[:, 12:14], in_=st[:, 0:12])
                        k.ts(st[:, 13:14], st[:, 13:14], LN_EPS, ALU.add)
                        k.act(st[:, 13:14], st[:, 13:14], AF.Sqrt)
                        k.op("dve", "reciprocal", out=st[:, 13:14], in_=st[:, 13:14])
                        xo = xor_.next()
                        k.ts(xo, z, st[:, 12:13], ALU.subtract, st[:, 13:14], ALU.mult)
                        k.tt(xo, xo, lg, ALU.mult, eng="pool")
                        k.tt(xo, xo, lb, ALU.add)
                        if last:
                            k.dma("act", OUT[b, t0 - Tc:t0 - Tc + 128, :], xo)
                        else:
                            k.dma("act", S["xcur"][b, t0:t0 + 128, :], xo)
        k.end_phase()

    PH = phases
    for l in range(cfg.L):
        with contextlib.ExitStack() as les:
            modT = k.sb("modT", [128, 48, 3], F32, les)
            if PH is None or "mod" in PH:
                phase_mod(l, modT)
            if PH is None or "inproj" in PH:
                phase_inproj(l, modT)
            if PH is None or "attn" in PH:
                phase_attn(l)
            if PH is None or "rwkv" in PH:
                phase_rwkv(l)
            if PH is None or "gmlp" in PH:
                phase_gmlp(l)
            if PH is None or "merge" in PH:
                phase_merge(l)
            if PH is None or "moe" in PH:
                phase_route(l)
                phase_experts(l)
                phase_scatter(l)
        k.end_phase()
    outs = [OUT] + [S[n] for n in dbg_out]
    k.final_wait([o.tb for o in outs])
    top.close()
    return nc, hc, k


_CACHE = {}


def _get_program(cfg_key):
    if cfg_key not in _CACHE:
        cfg = Cfg(*cfg_key)
        nc, hc, k = build(cfg)
        _CACHE[cfg_key] = (cfg, nc, hc)
    return _CACHE[cfg_key]


def kernel(**inputs):
    x = np.asarray(inputs["x"], np.float32)
    B, T, _ = x.shape
    ctx = np.asarray(inputs["ctx"], np.float32)
    Tc = ctx.shape[1]
    n_cores = 8
    NB = B // n_cores
    L = inputs["w_in"].shape[0]
    cfg, nc, hc = _get_program((T, Tc, NB, L))
    c = np.asarray(inputs["c"], np.float32)
    c_ctx = np.asarray(inputs["c_ctx"], np.float32)
    shared = {n: np.ascontiguousarray(np.asarray(inputs[n], np.float32)) for n in WEIGHT_SPECS}
    for n, a in hc.items():
        shared["k_" + n] = a
    in_maps = []
    for ci in range(n_cores):
        sl = slice(ci * NB, (ci + 1) * NB)
        m = dict(shared)
        perm = np.roll(np.arange(NE), -ci)
        m["w_router"] = np.ascontiguousarray(shared["w_router"][:, :, perm])
        for n in ("w_e_gate", "w_e_up", "w_e_down"):
            m[n] = np.ascontiguousarray(shared[n][:, perm])
        m["x"] = np.ascontiguousarray(x[sl])
        m["ctx"] = np.ascontiguousarray(ctx[sl])
        m["c3"] = np.ascontiguousarray(np.concatenate([c[sl], c_ctx[None]], 0))
        in_maps.append(m)
    res = run_bass_kernel_spmd(nc, in_maps, core_ids=list(range(n_cores)))
    return np.concatenate([np.asarray(r["out"], np.float32) for r in res.results], axis=0)
```

```python
import contextlib
import math
import numpy as np
import ml_dtypes
import concourse.bass as bass
import concourse.mybir as mybir
from concourse.bass_utils import run_bass_kernel_spmd

F32 = mybir.dt.float32
BF16 = mybir.dt.bfloat16
AF = mybir.ActivationFunctionType
ALU = mybir.AluOpType
AX = mybir.AxisListType

D = 1024
GRID_W = 64
DA_H = 4
RW_H = 8
NE = 16
FF = 2048
IN_COLS = 7424
C_Q, C_K, C_V, C_RW, C_SG, C_G = 0, 512, 1024, 1536, 3328, 4352
RW_COLS = 1792
LN_EPS = 1e-5
DA_EPS = 1e-5
RW_GN_EPS = 64e-5
DEPTH = 2
DN_ALPHA = (2 * DEPTH) ** 0.25
import os
DBG_RW = float(os.environ.get('DBG_RW', '99'))
DBG_AT = float(os.environ.get('DBG_AT', '99'))


class TB:
    __slots__ = ("w", "r", "dsem", "dcnt", "name", "space")

    def __init__(self, name, space):
        self.w = {}
        self.r = {}
        self.dsem = {}
        self.dcnt = {}
        self.name = name
        self.space = space


class V:
    __slots__ = ("ap", "tb")

    def __init__(self, ap, tb):
        self.ap = ap
        self.tb = tb

    def __getitem__(self, idx):
        return V(self.ap[idx], self.tb)

    def rearrange(self, *a, **k):
        return V(self.ap.rearrange(*a, **k), self.tb)

    def bitcast(self, dt):
        return V(self.ap.bitcast(dt), self.tb)

    def unsqueeze(self, i):
        return V(self.ap.unsqueeze(i), self.tb)

    def to_broadcast(self, shape):
        return V(self.ap.to_broadcast(shape), self.tb)

    def broadcast_to(self, shape):
        return V(self.ap.broadcast_to(shape), self.tb)

    @property
    def shape(self):
        return self.ap.shape


class Grp:
    def __init__(self, sem, cnt):
        self.sem = sem
        self.cnt = cnt


class Eng:
    def __init__(self, name, h, sem):
        self.name = name
        self.h = h
        self.sem = sem
        self.cnt = 0
        self.seen = {}


class K:
    def __init__(self, nc, es):
        self.nc = nc
        self.es = es
        self.E = {}
        for name, h in (("pe", nc.tensor), ("act", nc.scalar), ("dve", nc.vector),
                        ("pool", nc.gpsimd), ("sp", nc.sync)):
            self.E[name] = Eng(name, h, es.enter_context(nc.semaphore("sem_" + name)))
        self.n_inst = 0
        self.tbs = []
        self.sem_free = {"sw": [], "hw": []}
        self.phase_mark = 0
        self.groups = []

    def _tb(self, name, space):
        tb = TB(name, space)
        self.tbs.append(tb)
        return tb

    def sb(self, name, shape, dt=F32, es=None):
        self.uid = getattr(self, "uid", 0) + 1
        name = "%s_%d" % (name, self.uid)
        t = (es or self.es).enter_context(self.nc.sbuf_tensor(name, list(shape), dt))
        return V(t[:], self._tb(name, "sb"))

    def ps(self, name, shape, dt=F32, es=None):
        self.uid = getattr(self, "uid", 0) + 1
        name = "%s_%d" % (name, self.uid)
        t = (es or self.es).enter_context(self.nc.psum_tensor(name, list(shape), dt))
        return V(t[:], self._tb(name, "ps"))

    def sub(self, v, name):
        return V(v.ap, self._tb(name, v.tb.space))

    def dram(self, name, shape, dt=F32, kind="Internal"):
        t = self.nc.dram_tensor(name, list(shape), dt, kind=kind)
        return V(t.ap(), self._tb(name, "dram"))

    def _sem_get(self, name, qk):
        if self.sem_free[qk]:
            return self.sem_free[qk].pop()
        return [self.es.enter_context(self.nc.semaphore(name)), 0]

    def group(self, name):
        sem, cnt = self._sem_get("g_" + name, "hw")
        g = Grp(sem, cnt)
        self.groups.append(g)
        return g

    def end_phase(self):
        self.barrier()
        for tb in self.tbs:
            tb.w = {}
            tb.r = {}
        for tb in self.tbs[self.phase_mark:]:
            for qk, sc in tb.dsem.items():
                self.sem_free["sw" if qk == "pool" else "hw"].append(sc)
            tb.dsem = {}
        del self.tbs[self.phase_mark:]
        for g in self.groups:
            self.sem_free["hw"].append([g.sem, g.cnt])
        self.groups = []

    def begin_phase(self):
        self.phase_mark = len(self.tbs)

    def _wait(self, E, ev):
        sem, val = ev
        if isinstance(val, Grp):
            val = val.cnt
        key = id(sem)
        if E.seen.get(key, 0) >= val:
            return
        E.h.wait_ge(sem, val)
        E.seen[key] = val

    def _sync(self, E, reads, writes, skip_self=False):
        for tb in reads:
            for ev in tb.w.values():
                if not (skip_self and ev[0] is E.sem):
                    self._wait(E, ev)
        for tb in writes:
            for ev in list(tb.w.values()) + list(tb.r.values()):
                if not (skip_self and ev[0] is E.sem):
                    self._wait(E, ev)

    def _commit(self, ev, reads, writes):
        for tb in reads:
            tb.r[id(ev[0])] = ev
        for tb in writes:
            tb.w = {id(ev[0]): ev}
            tb.r = {}

    def op(self, eng, name, **kw):
        E = self.E[eng]
        reads, writes, args = [], [], {}
        for k, v in kw.items():
            if isinstance(v, V):
                (writes if k in ("out", "accum_out", "ap") else reads).append(v.tb)
                args[k] = v.ap
            else:
                args[k] = v
        self._sync(E, reads, writes, skip_self=(eng == "pe"))
        ins = getattr(E.h, name)(**args)
        E.cnt += 1
        ins.then_inc(E.sem, 1)
        self._commit((E.sem, E.cnt), reads, writes)
        self.n_inst += 1
        return ins

    def dma(self, q, out, in_, grp=None, **kw):
        E = self.E[q]
        dst, src = out.tb, in_.tb
        qk = q
        kind = "sw" if q == "pool" else "hw"
        if grp is not None:
            sem = grp.sem
        else:
            own = dst if dst.space == "sb" else (src if src.space == "sb" else dst)
            if qk not in own.dsem:
                own.dsem[qk] = self._sem_get("d%s_%s" % (qk, own.name), kind)
            sem = own.dsem[qk][0]
        for ev in src.w.values():
            self._wait(E, ev)
        if dst.space != "dram":
            for ev in dst.w.values():
                if ev[0] is not sem:
                    self._wait(E, ev)
        for ev in dst.r.values():
            self._wait(E, ev)
        E.h.dma_start(out=out.ap, in_=in_.ap, **kw).then_inc(sem, 16)
        if grp is not None:
            grp.cnt += 16
            ev = (sem, grp)
        else:
            own.dsem[qk][1] += 16
            ev = (sem, own.dsem[qk][1])
        src.r[id(sem)] = ev
        if dst.space == "dram":
            dst.w[id(sem)] = ev
        else:
            dst.w = {id(sem): ev}
        dst.r = {}
        self.n_inst += 1

    def barrier(self):
        evs = [(E.sem, E.cnt) for E in self.E.values() if E.cnt > 0]
        for tb in self.tbs:
            for qk, sc in tb.dsem.items():
                evs.append((sc[0], sc[1]))
        for g in self.groups:
            evs.append((g.sem, g.cnt))
        for E in self.E.values():
            for ev in evs:
                if ev[0] is E.sem:
                    continue
                self._wait(E, ev)

    def final_wait(self, tbs):
        E = self.E["sp"]
        for tb in tbs:
            for ev in tb.w.values():
                self._wait(E, ev)

    def mm(self, out, lhsT, rhs, start=True, stop=True):
        return self.op("pe", "matmul", out=out, lhsT=lhsT, rhs=rhs, start=start, stop=stop)

    def tr(self, out, in_, ident):
        return self.op("pe", "transpose", out=out, in_=in_, identity=ident)

    def act(self, out, in_, func, **kw):
        return self.op("act", "activation", out=out, in_=in_, func=func, **kw)

    def tt(self, out, in0, in1, op, eng="dve"):
        return self.op(eng, "tensor_tensor", out=out, in0=in0, in1=in1, op=op)

    def ts(self, out, in0, s1, op0, s2=None, op1=None, eng="dve", **kw):
        if op1 is None:
            return self.op(eng, "tensor_scalar", out=out, in0=in0, scalar1=s1, scalar2=None, op0=op0, **kw)
        return self.op(eng, "tensor_scalar", out=out, in0=in0, scalar1=s1, scalar2=s2, op0=op0, op1=op1, **kw)

    def stt(self, out, in0, scalar, in1, op0, op1):
        return self.op("dve", "scalar_tensor_tensor", out=out, in0=in0, scalar=scalar, in1=in1, op0=op0, op1=op1)

    def copy(self, out, in_, eng="dve"):
        if eng == "act":
            return self.op("act", "activation", out=out, in_=in_, func=AF.Identity)
        return self.op(eng, "tensor_copy", out=out, in_=in_)

    def memset(self, ap, val, eng="dve"):
        return self.op(eng, "memset", ap=ap, constant=val)


class Ring:
    def __init__(self, items):
        self.items = items
        self.i = 0

    def next(self):
        it = self.items[self.i % len(self.items)]
        self.i += 1
        return it


def blocks(total, size):
    out = []
    s = 0
    while s < total:
        n = min(size, total - s)
        out.append((s, n))
        s += n
    return out


class Cfg:
    def __init__(self, T=2048, Tc=256, NB=2, L=2, debug=False):
        self.T, self.Tc, self.NB, self.L, self.debug = T, Tc, NB, L, debug
        self.TT = T + Tc
        self.NT = self.TT // 128
        self.NTc = Tc // 128
        self.cap_lat = 2 * T // NE
        self.cap_ctx = 2 * Tc // NE
        self.RWROWS = self.TT + 3


def rw_row(cfg, t):
    return 1 + t if t < cfg.Tc else 2 + t


def host_consts(cfg):
    T, Tc, TT, NB = cfg.T, cfg.Tc, cfg.TT, cfg.NB
    c = {}
    c["ident_f"] = np.eye(128, dtype=np.float32)
    c["ident_b"] = np.eye(128).astype(ml_dtypes.bfloat16)
    rows = T // GRID_W
    row = np.repeat(np.arange(rows, dtype=np.float32), GRID_W)
    col = np.tile(np.arange(GRID_W, dtype=np.float32), rows)
    half = 32
    inv_freq = (10000.0 ** (-np.arange(0, half, 2, dtype=np.float32) / half)).astype(np.float32)
    ar = row[:, None] * inv_freq
    ac = col[:, None] * inv_freq
    ang = np.concatenate([ar, ar, ac, ac], axis=-1).astype(np.float32)
    cos = np.concatenate([np.ones((Tc, 64), np.float32), np.cos(ang)], 0)
    sin = np.concatenate([np.zeros((Tc, 64), np.float32), np.sin(ang)], 0)
    cosT = np.tile(np.concatenate([cos.T, cos.T], 0), (1, NB)).astype(np.float32)
    sinT = np.tile(np.concatenate([sin.T, sin.T], 0), (1, NB)).astype(np.float32)
    c["cosT"], c["sinT"] = np.ascontiguousarray(cosT), np.ascontiguousarray(sinT)
    R = np.zeros((64, 64), np.float32)
    for base in (0, 32):
        for i in range(16):
            R[base + i, base + 16 + i] = -1.0
            R[base + 16 + i, base + i] = 1.0
    R2 = np.zeros((128, 128), np.float32)
    R2[:64, :64] = R
    R2[64:, 64:] = R
    c["rotT"] = np.ascontiguousarray(R2.T)
    u = np.arange(128)[:, None]
    s = np.arange(128)[None, :]
    c["tri"] = np.stack([(u <= s), (u >= s)]).astype(np.float32)
    c["m_su"] = np.stack([(u < s), (u > s)]).astype(np.float32)
    c["m_iu"] = np.stack([(u <= s), (u >= s)]).astype(np.float32)
    c["m_nsu"] = -c["m_su"]
    c["m_nsl"] = -np.stack([(u > s), (u < s)]).astype(np.float32)
    c["iota"] = np.tile(np.arange(256, dtype=np.float32)[None, :], (128, 1))
    c["ones_f"] = np.ones((128, 128), np.float32)
    return c


WEIGHT_SPECS = {
    "w_mod": (D, 6 * D), "b_mod": (6 * D,), "w_in": (D, IN_COLS), "da_lambda": (4, 64), "da_norm_g": (128,),
    "rw_shift_mu": (2, RW_COLS), "rw_w0": (2, 512), "rw_w2": (2, 64, 512), "rw_a0": (2, 512),
    "rw_a2": (2, 64, 512), "rw_k_k": (512,), "rw_k_a": (512,), "rw_r_k": (8, 64), "rw_ln_g": (512,),
    "rw_ln_b": (512,), "rw_g2": (128, 512), "sg_norm_g": (512,), "sg_norm_b": (512,), "sg_w": (4, 128, 128),
    "sg_b": (4, 128), "w_branch": (1536, D), "w_out": (D, D), "ln1_g": (D,), "ln1_b": (D,),
    "w_router": (D, NE), "w_e_gate": (NE, D, FF), "w_e_up": (NE, D, FF), "w_e_down": (NE, FF, D),
    "ln2_g": (D,), "ln2_b": (D,),
}


def build(cfg, phases=None, dbg_out=(), dbg_in=()):
    nc = bass.Bass("TRN2", target_bir_lowering=False)
    T, Tc, TT, NB, NT, NTc = cfg.T, cfg.Tc, cfg.TT, cfg.NB, cfg.NT, cfg.NTc
    NS = NB * TT
    top = contextlib.ExitStack()
    k = K(nc, top)

    def din(name, shape, dt=F32):
        return V(nc.dram_tensor(name, list(shape), dt, kind="ExternalInput").ap(), k._tb(name, "dram"))

    X = din("x", [NB, T, D])
    CTX = din("ctx", [NB, Tc, D])
    C3 = din("c3", [NB + 1, D])
    W = {n: din(n, [cfg.L] + list(s)) for n, s in WEIGHT_SPECS.items()}
    hc = host_consts(cfg)
    CONST = {n: din("k_" + n, list(a.shape), BF16 if a.dtype == ml_dtypes.bfloat16 else F32) for n, a in hc.items()}
    OUT = V(nc.dram_tensor("out", [NB, T, D], F32, kind="ExternalOutput").ap(), k._tb("out", "dram"))

    S = {}

    def scratch(name, shape, dt=F32):
        kind = "ExternalOutput" if name in dbg_out else ("ExternalInput" if name in dbg_in else "Internal")
        S[name] = k.dram("s_" + name, shape, dt, kind=kind)
        return S[name]

    scratch("xcur", [NB, TT, D])
    scratch("mod", [cfg.L, 3, 6 * D])
    scratch("qk", [NB, 1024, TT], BF16)
    scratch("v", [NB, TT, 512], BF16)
    scratch("rw", [NB, cfg.RWROWS, RW_COLS])
    scratch("sg", [NB, TT, 1024])
    scratch("gT", [NB, 3072, TT], BF16)
    scratch("yT", [NB, 1536, TT], BF16)
    scratch("y0", [NB, TT, 512])
    scratch("rwaux", [NB, TT, 1024])
    scratch("xmid", [NB, TT, D])
    scratch("h2", [NB, TT, D], BF16)
    scratch("aff", [NB, NE, TT])
    scratch("slotT", [NB, TT, NE])
    scratch("GT", [NB, TT, NE])
    scratch("xe", [NB, 2, NE, D, cfg.cap_lat], BF16)
    scratch("ye", [NB, 2, NE * cfg.cap_lat, D], BF16)

    ident_f = k.sb("ident_f", [128, 128])
    ident_b = k.sb("ident_b", [128, 128], BF16)
    k.dma("sp", ident_f, CONST["ident_f"])
    k.dma("sp", ident_b, CONST["ident_b"])
    zrow = k.sb("zrow", [1, RW_COLS])
    k.memset(zrow, 0.0)
    for b in range(NB):
        for r in (0, Tc + 1, TT + 2):
            if "rw" not in dbg_in:
                k.dma("sp", S["rw"][b, r:r + 1, :], zrow)
    for b in range(NB):
        k.dma("sp", S["xcur"][b, 0:Tc, :], CTX[b])
        k.dma("sp", S["xcur"][b, Tc:TT, :], X[b])

    def stream_segs(t0, n):
        out = []
        s = t0
        while s < t0 + n:
            b = s // TT
            e = min(t0 + n, (b + 1) * TT)
            out.append((b, s - b * TT, s - t0, e - s))
            s = e
        return out

    def phase_mod(l, modT):
        k.begin_phase()
        with contextlib.ExitStack() as es:
            cT = k.sb("cT", [128, 8, 3], F32, es)
            for r in range(3):
                k.dma("sp", cT[:, :, r], C3[r].rearrange("(k p) -> p k", p=128), allow_slow_non_contiguous=True)
            scT = k.sb("scT", [128, 8, 3], BF16, es)
            k.act(scT, cT, AF.Silu)
            bm = k.sb("bm", [3, 6 * D], F32, es)
            k.dma("sp", bm, W["b_mod"][l].unsqueeze(0).broadcast_to([3, 6 * D]))
            modrow = k.sb("modrow", [3, 6 * D], F32, es)
            wr = Ring([k.sb(f"wm{i}", [128, 8, 512], BF16, es) for i in range(2)])
            pr = Ring([k.ps(f"pm{i}", [3, 512], F32, es) for i in range(2)])
            for (c0, n) in blocks(6 * D, 512):
                wb = wr.next()
                k.dma("pool", wb, W["w_mod"][l][:, c0:c0 + n].rearrange("(k p) c -> p k c", p=128))
                ps = pr.next()
                for kk in range(8):
                    k.mm(ps, scT[:, kk, :], wb[:, kk, :], start=(kk == 0), stop=(kk == 7))
                k.tt(modrow[:, c0:c0 + n], ps, bm[:, c0:c0 + n], ALU.add)
            k.dma("sp", S["mod"][l], modrow)
            for j in range(6):
                for r in range(3):
                    k.dma("sp", modT[:, j * 8:(j + 1) * 8, r],
                          S["mod"][l, r, j * D:(j + 1) * D].rearrange("(k p) -> p k", p=128),
                          allow_slow_non_contiguous=True)
            for j in (1, 4):
                k.ts(modT[:, j * 8:(j + 1) * 8, :], modT[:, j * 8:(j + 1) * 8, :], 1.0, ALU.add)
        k.end_phase()

    def phase_inproj(l, modT):
        k.begin_phase()
        with contextlib.ExitStack() as es:
            hT = k.sb("hT", [128, 8, NS], BF16, es)
            cosT = k.sb("cosT", [128, NS], F32, es)
            sinT = k.sb("sinT", [128, NS], F32, es)
            rotT = k.sb("rotT", [128, 128], F32, es)
            k.dma("sp", cosT, CONST["cosT"])
            k.dma("sp", sinT, CONST["sinT"])
            k.dma("sp", rotT, CONST["rotT"])
            xr = Ring([k.sb(f"xt{i}", [128, D], F32, es) for i in range(3)])
            ptr = Ring([k.ps(f"ptr{i}", [128, 4, 128], F32, es) for i in range(2)])
            for b in range(NB):
                for i in range(NT):
                    r = 2 if i < NTc else b
                    xt = xr.next()
                    k.dma("sp", xt, S["xcur"][b, i * 128:(i + 1) * 128, :])
                    for g in range(2):
                        pt = ptr.next()
                        for j in range(4):
                            kk = g * 4 + j
                            k.tr(pt[:, j, :], xt[:, kk * 128:(kk + 1) * 128], ident_f)
                        for j in range(4):
                            kk = g * 4 + j
                            s0 = b * TT + i * 128
                            k.act(hT[:, kk, s0:s0 + 128], pt[:, j, :], AF.Identity,
                                  scale=modT[:, 8 + kk, r:r + 1], bias=modT[:, kk, r:r + 1])
            wr = Ring([k.sb(f"wi{i}", [128, 8, 512], BF16, es) for i in range(3)])
            pmm = Ring([k.ps(f"pmm{i}", [128, 512], F32, es) for i in range(3)])
            prot = Ring([k.ps(f"prot{i}", [128, 512], F32, es) for i in range(2)])
            qs_r = Ring([k.sb(f"qs{i}", [128, 512], F32, es) for i in range(2)])
            t1_r = Ring([k.sb(f"t1{i}", [128, 512], F32, es) for i in range(2)])
            ob_r = Ring([k.sb(f"ob{i}", [128, 512], BF16, es) for i in range(3)])
            of_r = Ring([k.sb(f"of{i}", [128, 512], F32, es) for i in range(3)])
            sblocks = blocks(NS, 512)
            colblocks = []
            for (g0, g1) in ((C_Q, C_V), (C_V, C_RW), (C_RW, C_SG), (C_SG, C_G), (C_G, IN_COLS)):
                colblocks += [(g0 + a, n) for (a, n) in blocks(g1 - g0, 512)]
            for (c0, ncol) in colblocks:
                wb = wr.next()
                k.dma("pool", wb[:, :, 0:ncol], W["w_in"][l][:, c0:c0 + ncol].rearrange("(k p) c -> p k c", p=128))
                if c0 < C_V or c0 >= C_G:
                    for cc in range(ncol // 128):
                        col = c0 + cc * 128
                        for (t0, n) in sblocks:
                            ps = pmm.next()
                            for kk in range(8):
                                k.mm(ps[:, 0:n], wb[:, kk, cc * 128:(cc + 1) * 128], hT[:, kk, t0:t0 + n],
                                     start=(kk == 0), stop=(kk == 7))
                            ob = ob_r.next()
                            if col < C_V:
                                qs = qs_r.next()
                                k.act(qs[:, 0:n], ps[:, 0:n], AF.Identity, scale=(0.125 if col < C_K else 1.0))
                                pr_ = prot.next()
                                k.mm(pr_[:, 0:n], rotT, qs[:, 0:n])
                                t1 = t1_r.next()
                                k.tt(t1[:, 0:n], qs[:, 0:n], cosT[:, t0:t0 + n], ALU.mult, eng="pool")
                                k.tt(qs[:, 0:n], pr_[:, 0:n], sinT[:, t0:t0 + n], ALU.mult)
                                k.tt(ob[:, 0:n], qs[:, 0:n], t1[:, 0:n], ALU.add)
                                for (b, tb_, off, ln) in stream_segs(t0, n):
                                    k.dma("pool", S["qk"][b, col:col + 128, tb_:tb_ + ln], ob[:, off:off + ln])
                            else:
                                k.act(ob[:, 0:n], ps[:, 0:n], AF.Sigmoid)
                                gc = col - C_G
                                for (b, tb_, off, ln) in stream_segs(t0, n):
                                    k.dma("act", S["gT"][b, gc:gc + 128, tb_:tb_ + ln], ob[:, off:off + ln])
                else:
                    for b in range(NB):
                        for i in range(NT):
                            s0 = b * TT + i * 128
                            ps = pmm.next()
                            for kk in range(8):
                                k.mm(ps[:, 0:ncol], hT[:, kk, s0:s0 + 128], wb[:, kk, 0:ncol],
                                     start=(kk == 0), stop=(kk == 7))
                            if c0 < C_RW:
                                ob = ob_r.next()
                                k.copy(ob[:, 0:ncol], ps[:, 0:ncol], eng="act")
                                k.dma("act", S["v"][b, i * 128:(i + 1) * 128, c0 - C_V:c0 - C_V + ncol], ob[:, 0:ncol])
                            elif c0 < C_SG:
                                of = of_r.next()
                                k.copy(of[:, 0:ncol], ps[:, 0:ncol], eng="act")
                                r0 = rw_row(cfg, i * 128)
                                k.dma("act", S["rw"][b, r0:r0 + 128, c0 - C_RW:c0 - C_RW + ncol], of[:, 0:ncol])
                            else:
                                of = of_r.next()
                                k.act(of[:, 0:ncol], ps[:, 0:ncol], AF.Gelu_apprx_tanh)
                                k.dma("act", S["sg"][b, i * 128:(i + 1) * 128, c0 - C_SG:c0 - C_SG + ncol], of[:, 0:ncol])
        k.end_phase()

    def phase_attn(l):
        k.begin_phase()
        last = (l == DEPTH - 1)
        lam_init = 0.8 - 0.6 * math.exp(-0.3 * l)
        with contextlib.ExitStack() as es:
            lamt = k.sb("lamt", [128, 256], F32, es)
            k.dma("sp", lamt, W["da_lambda"][l].rearrange("a d -> (a d)").unsqueeze(0).broadcast_to([128, 256]))
            lv = lamt.rearrange("p (a d) -> p a d", a=4)
            prod = k.sb("lprod", [128, 2, 64], F32, es)
            k.tt(prod[:, 0, :], lv[:, 0, :], lv[:, 1, :], ALU.mult)
            k.tt(prod[:, 1, :], lv[:, 2, :], lv[:, 3, :], ALU.mult)
            lsum = k.sb("lsum", [128, 2], F32, es)
            k.op("dve", "tensor_reduce", out=lsum, in_=prod, axis=AX.X, op=ALU.add)
            lexp = k.sb("lexp", [128, 2], F32, es)
            k.act(lexp, lsum, AF.Exp)
            nlam = k.sb("nlam", [128, 1], F32, es)
            k.tt(nlam, lexp[:, 1:2], lexp[:, 0:1], ALU.subtract)
            k.ts(nlam, nlam, -lam_init, ALU.add)
            gb = k.sb("dag", [128, 128], F32, es)
            k.dma("sp", gb, W["da_norm_g"][l].unsqueeze(0).broadcast_to([128, 128]))
            k.ts(gb, gb, 1.0 - lam_init, ALU.mult)

            qr = Ring([k.sb(f"aq{i}", [128, TT], BF16, es) for i in range(2)])
            kr = Ring([k.sb(f"ak{i}", [128, TT], BF16, es) for i in range(2)])
            vr = Ring([k.sb(f"av{i}", [128, NT, 129], BF16, es) for i in range(2)])
            for vt in vr.items:
                k.memset(vt[:, :, 128:129], 1.0)
            yr = Ring([k.sb(f"ayT{i}", [128, TT], BF16, es) for i in range(2)])
            PT = [k.sb(f"aPT{m}", [128, NT, 512], BF16, es) for m in range(2)]
            pscore = Ring([k.ps(f"apsc{i}", [128, 512], F32, es) for i in range(2)])
            pacc = Ring([k.ps(f"apac{i}", [128, 512], F32, es) for i in range(4)])
            ptr_ = Ring([k.ps(f"aptr{i}", [128, 128], BF16, es) for i in range(2)])
            sm = Ring([k.sb(f"asm{i}", [128, 4], F32, es) for i in range(3)])
            y1r = Ring([k.sb(f"ay1{i}", [128, 128], F32, es) for i in range(2)])
            y2r = Ring([k.sb(f"ay2{i}", [128, 128], F32, es) for i in range(2)])
            jr = Ring([k.sb(f"ajk{i}", [128, 128], F32, es) for i in range(2)])
            ynr = Ring([k.sb(f"ayn{i}", [128, 128], BF16, es) for i in range(2)])
            groups = []
            if not last:
                groups += [(q0, n, NTc) for (q0, n) in blocks(Tc, 512)]
            groups += [(Tc + q0, n, NT) for (q0, n) in blocks(T, 512)]
            for b in range(NB):
                for h in range(DA_H):
                    qT, kT, Vh, yT = qr.next(), kr.next(), vr.next(), yr.next()
                    k.dma("sp", qT, S["qk"][b, h * 128:(h + 1) * 128, :])
                    k.dma("sp", kT, S["qk"][b, 512 + h * 128:512 + (h + 1) * 128, :])
                    k.dma("sp", Vh[:, :, 0:128], S["v"][b, :, h * 128:(h + 1) * 128].rearrange("(j p) e -> p j e", p=128))
                    for (q0, nq, nk) in groups:
                        if DBG_AT < 2:
                            continue
                        for m in range(2):
                            for j in range(nk):
                                ps = pscore.next()
                                k.mm(ps[:, 0:nq], kT[m * 64:(m + 1) * 64, j * 128:(j + 1) * 128],
                                     qT[m * 64:(m + 1) * 64, q0:q0 + nq])
                                k.act(PT[m][:, j, 0:nq], ps[:, 0:nq], AF.Exp)
                        for s_ in range(nq // 128):
                            if DBG_AT < 3:
                                continue
                            accs = []
                            for m in range(2):
                                acc = pacc.next()
                                for j in range(nk):
                                    k.mm(acc[:, 0:129], PT[m][:, j, s_ * 128:(s_ + 1) * 128], Vh[:, j, :],
                                         start=(j == 0), stop=(j == nk - 1))
                                accs.append(acc)
                            if DBG_AT < 4:
                                continue
                            t = sm.next()
                            k.op("dve", "reciprocal", out=t[:, 0:1], in_=accs[0][:, 128:129])
                            k.op("dve", "reciprocal", out=t[:, 1:2], in_=accs[1][:, 128:129])
                            k.tt(t[:, 1:2], t[:, 1:2], nlam, ALU.mult)
                            y1 = y1r.next()
                            k.ts(y1, accs[0][:, 0:128], t[:, 0:1], ALU.mult)
                            y2 = y2r.next()
                            k.stt(y2, accs[1][:, 0:128], t[:, 1:2], y1, ALU.mult, ALU.add)
                            jk = jr.next()
                            k.act(jk, y2, AF.Square, accum_out=t[:, 2:3])
                            k.ts(t[:, 2:3], t[:, 2:3], 1.0 / 128, ALU.mult, DA_EPS, ALU.add)
                            k.act(t[:, 2:3], t[:, 2:3], AF.Sqrt)
                            k.op("dve", "reciprocal", out=t[:, 3:4], in_=t[:, 2:3])
                            yn = ynr.next()
                            k.stt(yn, y2, t[:, 3:4], gb, ALU.mult, ALU.mult)
                            if DBG_AT < 5:
                                continue
                            pt = ptr_.next()
                            k.tr(pt, yn, ident_b)
                            c0 = q0 + s_ * 128
                            k.copy(yT[:, c0:c0 + 128], pt, eng="act")
                    lo = groups[0][0]
                    if DBG_AT >= 6:
                        k.dma("act", S["yT"][b, h * 128:(h + 1) * 128, lo:TT], yT[:, lo:TT])
        k.end_phase()

    def phase_gmlp(l):
        k.begin_phase()
        with contextlib.ExitStack() as es:
            wsT = k.sb("wsT", [128, 4, 128], BF16, es)
            with contextlib.ExitStack() as es2:
                ws_f = k.sb("ws_f", [128, 4, 128], F32, es2)
                k.dma("sp", ws_f, W["sg_w"][l].rearrange("g p q -> p g q"))
                pw = k.ps("sgpw", [128, 4, 128], F32, es2)
                for g in range(4):
                    k.tr(pw[:, g, :], ws_f[:, g, :], ident_f)
                k.copy(wsT, pw)
                k.barrier()
            bsT = k.sb("bsT", [128, 4], F32, es)
            k.dma("sp", bsT, W["sg_b"][l].rearrange("g p -> p g"), allow_slow_non_contiguous=True)
            ngb = k.sb("sgng", [128, 512], F32, es)
            nbb = k.sb("sgnb", [128, 512], F32, es)
            k.dma("sp", ngb, W["sg_norm_g"][l].unsqueeze(0).broadcast_to([128, 512]))
            k.dma("sp", nbb, W["sg_norm_b"][l].unsqueeze(0).broadcast_to([128, 512]))
            sgr = Ring([k.sb(f"sgt{i}", [128, 1024], F32, es) for i in range(3)])
            str_ = Ring([k.sb(f"sgst{i}", [128, 8], F32, es) for i in range(3)])
            vnr = Ring([k.sb(f"sgvn{i}", [128, 512], F32, es) for i in range(2)])
            vbr = Ring([k.sb(f"sgvb{i}", [128, 512], BF16, es) for i in range(2)])
            ysr = Ring([k.sb(f"sgy{i}", [128, 512], BF16, es) for i in range(2)])
            yTr = Ring([k.sb(f"sgyT{i}", [128, 4, 128], BF16, es) for i in range(2)])
            pmr = Ring([k.ps(f"sgpm{i}", [128, 512], F32, es) for i in range(2)])
            ptr_ = Ring([k.ps(f"sgpt{i}", [128, 4, 128], BF16, es) for i in range(2)])
            for b in range(NB):
                for i in range(NT):
                    if l == DEPTH - 1 and i < NTc:
                        continue
                    sg = sgr.next()
                    k.dma("sp", sg, S["sg"][b, i * 128:(i + 1) * 128, :])
                    st = str_.next()
                    k.op("dve", "bn_stats", out=st[:, 0:6], in_=sg[:, 512:1024])
                    k.op("dve", "bn_aggr", out=st[:, 6:8], in_=st[:, 0:6])
                    k.ts(st[:, 7:8], st[:, 7:8], LN_EPS, ALU.add)
                    k.act(st[:, 7:8], st[:, 7:8], AF.Sqrt)
                    k.op("dve", "reciprocal", out=st[:, 7:8], in_=st[:, 7:8])
                    vn = vnr.next()
                    k.ts(vn, sg[:, 512:1024], st[:, 6:7], ALU.subtract, st[:, 7:8], ALU.mult)
                    k.tt(vn, vn, ngb, ALU.mult, eng="pool")
                    vb = vbr.next()
                    k.tt(vb, vn, nbb, ALU.add)
                    pm = pmr.next()
                    for g in range(4):
                        k.mm(pm[:, g * 128:(g + 1) * 128], wsT[:, g, :], vb[:, g * 128:(g + 1) * 128])
                    ys = ysr.next()
                    for g in range(4):
                        k.stt(ys[:, g * 128:(g + 1) * 128], pm[:, g * 128:(g + 1) * 128], bsT[:, g:g + 1],
                              sg[:, g * 128:(g + 1) * 128], ALU.add, ALU.mult)
                    pt = ptr_.next()
                    for g in range(4):
                        k.tr(pt[:, g, :], ys[:, g * 128:(g + 1) * 128], ident_b)
                    yT = yTr.next()
                    k.copy(yT, pt, eng="act")
                    k.dma("act", S["yT"][b, 1024:1536, i * 128:(i + 1) * 128].rearrange("(g c) t -> c g t", c=128), yT)
        k.end_phase()

    def phase_rwkv(l):
        k.begin_phase()
        last = (l == DEPTH - 1)
        H = lambda v: v.rearrange("p (h c) -> p h c", h=8)
        with contextlib.ExitStack() as es:
            cg = k.group("rwc")
            def bcast(name, src, n):
                t = k.sb(name, [128, n], F32, es)
                k.dma("sp", t, src.unsqueeze(0).broadcast_to([128, n]), grp=cg)
                return t
            mu0b = bcast("rmu0", W["rw_shift_mu"][l, 0], RW_COLS)
            mu1b = bcast("rmu1", W["rw_shift_mu"][l, 1], RW_COLS)
            w0b = [bcast(f"rw0b{z}", W["rw_w0"][l, z], 512) for z in range(2)]
            a0b = [bcast(f"ra0b{z}", W["rw_a0"][l, z], 512) for z in range(2)]
            kkb = bcast("rkkb", W["rw_k_k"][l], 512)
            kab = bcast("rkab", W["rw_k_a"][l], 512)
            omkab = k.sb("romka", [128, 512], F32, es)
            rkb = bcast("rrkb", W["rw_r_k"][l].rearrange("h c -> (h c)"), 512)
            lngb = bcast("rlng", W["rw_ln_g"][l], 512)
            lnbb = bcast("rlnb", W["rw_ln_b"][l], 512)
            w2 = [k.sb(f"rw2_{z}", [64, 512], F32, es) for z in range(2)]
            a2 = [k.sb(f"ra2_{z}", [64, 512], F32, es) for z in range(2)]
            for z in range(2):
                k.dma("sp", w2[z], W["rw_w2"][l, z], grp=cg)
                k.dma("sp", a2[z], W["rw_a2"][l, z], grp=cg)
            g2 = k.sb("rg2", [128, 512], F32, es)
            k.dma("sp", g2, W["rw_g2"][l], grp=cg)
            ones = k.sb("rones", [128, 1], F32, es)
            tri, mc1, mc2, mnsl = [], [], [], []
            for z in range(2):
                t = k.sb(f"rtri{z}", [128, 128], F32, es); k.dma("sp", t, CONST["tri"][z], grp=cg); tri.append(t)
                t = k.sb(f"rmc1{z}", [128, 256], F32, es)
                k.dma("sp", t[:, 0:128], CONST["m_su"][z], grp=cg); k.dma("sp", t[:, 128:256], CONST["m_iu"][z], grp=cg); mc1.append(t)
                t = k.sb(f"rmc2{z}", [128, 256], F32, es)
                k.dma("sp", t[:, 0:128], CONST["m_nsu"][z], grp=cg); k.dma("sp", t[:, 128:256], CONST["m_iu"][z], grp=cg); mc2.append(t)
                t = k.sb(f"rmnsl{z}", [128, 128], F32, es); k.dma("sp", t, CONST["m_nsl"][z], grp=cg); mnsl.append(t)

            k.memset(ones, 1.0)
            k.ts(omkab, kab, -1.0, ALU.mult, 1.0, ALU.add)
            cur = k.sb("rcur", [128, RW_COLS], F32, es)
            prv = k.sb("rprv", [128, RW_COLS], F32, es)
            nxt = k.sb("rnxt", [128, RW_COLS], F32, es)
            psh = k.sb("rpsh", [128, RW_COLS], F32, es)
            sm3 = k.sb("rsm3", [128, 256], F32, es)
            smT = k.sb("rsmT", [128, 3, 128], F32, es)
            F = {n: k.sb("rf_" + n, [128, 512], F32, es) for n in
                 ("lw", "a", "kkp", "kk", "kd", "bb", "epos", "eneg", "eposx", "t0", "t1")}
            for n in ("kap", "rt", "kt", "bt", "vb"):
                F[n] = k.sb("rf_" + n, [128, 512], BF16, es)
            ss8 = k.sb("rss8", [128, 8], F32, es)
            gC = k.sb("rgC", [64, 8], F32, es)
            KR = k.sb("rKR", [128, 8, 2, 128], BF16, es)
            k.memset(KR, 0.0)
            KT = k.sb("rKT", [64, 8, 128], BF16, es)
            BT = k.sb("rBT", [64, 8, 128], BF16, es)
            A = [[k.sb(f"rA{b}_{h}", [128, 64], F32, es) for h in range(8)] for b in range(NB)]
            Ab = [[k.sb(f"rAb{b}_{h}", [128, 64], BF16, es) for h in range(8)] for b in range(NB)]
            ytile = k.sb("rytile", [128, 512], F32, es)
            y0t = k.sb("ry0t", [128, 512], F32, es)
            aux = k.sb("raux", [128, 1024], F32, es)
            gst = k.sb("rgst", [128, 16], F32, es)
            yb = k.sb("ryb", [128, 512], BF16, es)
            yTs = k.sb("ryTs", [128, 4, 128], BF16, es)
            pwa_b = k.ps("rBf0", [128, 512], F32, es)
            pwa = Ring([pwa_b])
            Bf1 = k.ps("rBf1", [128, 512], F32, es)
            ptS = Bf1[:, 0:384].rearrange("p (a t) -> p a t", a=3)
            pgc = Bf1[0:64, 384:392]
            ptF = Bf1
            Bb = Bf1.bitcast(BF16)[:, 0:512].rearrange("p (a t) -> p a t", a=4)
            lanes = []
            for ln in range(2):
                X = k.ps(f"rX{ln}", [128, 512], F32, es)
                Y = k.ps(f"rY{ln}", [128, 512], F32, es)
                Sb = k.ps(f"rS{ln}", [128, 512], F32, es)
                lanes.append(dict(
                    ps1=X[:, 0:256], ps2=X[:, 256:512], p5=Y[:, 0:256], p6=Y[:, 256:384],
                    pWT=Sb[0:64, 0:128], ps3=Sb[:, 128:256], ps4=Sb[:, 256:320], psz=Sb[:, 320:384],
                    psy=Sb[:, 384:448], psa=Sb[0:64, 448:512],
                    tmpA=k.sb(f"rtA{ln}", [64, 64], F32, es),
                    lmk=k.sb(f"rLMk{ln}", [128, 256], BF16, es),
                    mbt=k.sb(f"rMbT{ln}", [128, 128], BF16, es),
                    PT=Ring([k.sb(f"rPT{ln}_{i}", [128, 128], F32, es) for i in range(2)]),
                    PR=Ring([k.sb(f"rPR{ln}_{i}", [128, 256], F32, es) for i in range(2)]),
                    WT=k.sb(f"rWT{ln}", [64, 128], BF16, es),
                    Z=k.sb(f"rZ{ln}", [128, 64], BF16, es)))
            for b in range(NB):
                for h in range(8):
                    k.memset(A[b][h], 0.0)
                    k.memset(Ab[b][h], 0.0)

            def features(b, i, z):
                r0 = rw_row(cfg, i * 128)
                k.dma("sp", cur, S["rw"][b, r0:r0 + 128, :])
                k.dma("sp", prv, S["rw"][b, r0 - 1:r0 + 127, :])
                k.dma("sp", nxt, S["rw"][b, r0 + 1:r0 + 129, :])
                k.tt(prv, prv, cur, ALU.subtract, eng="pool")
                k.tt(prv, prv, mu0b, ALU.mult, eng="pool")
                k.tt(nxt, nxt, cur, ALU.subtract)
                k.tt(nxt, nxt, mu1b, ALU.mult)
                k.tt(psh, cur, prv, ALU.add, eng="pool")
                k.tt(psh, psh, nxt, ALU.add)
                r_, k_, v_ = psh[:, 0:512], psh[:, 512:1024], psh[:, 1024:1536]
                k.act(sm3[:, 0:64], psh[:, 1536:1600], AF.Tanh)
                k.act(sm3[:, 128:256], psh[:, 1664:1792], AF.Sigmoid)
                k.tr(ptS[0:64, 0, :], sm3[:, 0:64], ident_f)
                k.tr(ptS[0:64, 1, :], psh[:, 1600:1664], ident_f)
                k.tr(ptS[:, 2, :], sm3[:, 128:256], ident_f)
                k.copy(smT[0:64, 0:2, :], ptS[0:64, 0:2, :], eng="act")
                k.copy(smT[:, 2, :], ptS[:, 2, :], eng="act")
                lw, a_, kkp, kk, kd, bb = F["lw"], F["a"], F["kkp"], F["kk"], F["kd"], F["bb"]
                t0, t1 = F["t0"], F["t1"]
                pw = pwa.next()
                k.mm(pw, smT[0:64, 0, :], w2[z])
                k.tt(t0, pw, w0b[z], ALU.add)
                k.act(lw, t0, AF.Sigmoid)
                k.ts(lw, lw, -math.exp(-0.5), ALU.mult)
                pa = pwa.next()
                k.mm(pa, smT[0:64, 1, :], a2[z])
                k.tt(t0, pa, a0b[z], ALU.add)
                k.act(a_, t0, AF.Sigmoid)
                k.tt(kkp, k_, kkb, ALU.mult, eng="pool")
                k.tt(t1, kkp, kkp, ALU.mult, eng="pool")
                k.op("dve", "tensor_reduce", out=ss8, in_=H(t1), axis=AX.X, op=ALU.add)
                k.act(ss8, ss8, AF.Sqrt)
                k.ts(ss8, ss8, 1e-12, ALU.max)
                k.op("dve", "reciprocal", out=ss8, in_=ss8)
                k.tt(H(kk), H(kkp), ss8.unsqueeze(2).to_broadcast([128, 8, 64]), ALU.mult)
                k.tt(t0, a_, kab, ALU.mult)
                k.tt(t0, t0, omkab, ALU.add)
                k.tt(kd, k_, t0, ALU.mult)
                k.tt(bb, a_, kk, ALU.mult, eng="pool")
                pc = pwa.next()
                k.mm(pc, tri[z], lw)
                k.act(F["epos"], pc, AF.Exp)
                k.act(F["eneg"], pc, AF.Exp, scale=-1.0)
                k.tt(t0, pc, lw, ALU.subtract)
                k.act(F["eposx"], t0, AF.Exp)
                k.copy(F["vb"], v_, eng="act")
                k.tt(F["kap"], kk, F["eposx"], ALU.mult)
                k.tt(F["rt"], r_, F["epos"], ALU.mult, eng="pool")
                k.tt(F["kt"], kd, F["eneg"], ALU.mult)
                k.tt(F["bt"], bb, F["eneg"], ALU.mult, eng="pool")
                if DBG_RW < 2:
                    return v_
                for h in range(8):
                    k.mm(pgc[:, h:h + 1], lw[:, h * 64:(h + 1) * 64], ones)
                k.act(gC, pgc, AF.Exp)
                if DBG_RW < 3:
                    return v_
                if z == 0:
                    pa = pwa.next()
                    k.mm(pa, smT[0:64, 1, :], a2[1])
                    k.tt(t0, pa, a0b[1], ALU.add)
                    k.act(t1, t0, AF.Sigmoid)
                    k.tt(t1, t1, a_, ALU.add)
                    k.ts(t1, t1, 0.5, ALU.mult)
                    k.tt(t1, t1, kab, ALU.mult)
                    k.tt(t1, t1, omkab, ALU.add)
                    k.tt(t1, t1, k_, ALU.mult)
                    k.tt(t1, t1, r_, ALU.mult)
                    k.tt(t1, t1, rkb, ALU.mult)
                    k.op("dve", "tensor_reduce", out=ss8, in_=H(t1), axis=AX.X, op=ALU.add)
                    k.tt(H(aux[:, 0:512]), H(v_), ss8.unsqueeze(2).to_broadcast([128, 8, 64]), ALU.mult)
                    pg = pwa.next()
                    k.mm(pg, smT[:, 2, :], g2)
                    k.copy(aux[:, 512:1024], pg, eng="act")
                    k.dma("act", S["rwaux"][b, i * 128:(i + 1) * 128, :], aux)
                if DBG_RW < 4:
                    return v_
                ptFb = ptF.bitcast(BF16)
                ptFv = ptFb[0:64, 0:512].rearrange("p (a c t) -> p a c t", a=2, c=2)
                for hp in range(4):
                    for hh in range(2):
                        h = hp * 2 + hh
                        k.tr(ptFv[:, hh, 0, :], F["kap"][:, h * 64:(h + 1) * 64], ident_b)
                        k.tr(ptFv[:, hh, 1, :], F["rt"][:, h * 64:(h + 1) * 64], ident_b)
                    k.copy(KR[0:64, hp * 2:hp * 2 + 2, :, :], ptFv, eng="act")
                ptF4 = ptFb[0:64, 0:512].rearrange("p (a t) -> p a t", a=4)
                for (src, dstT) in ((F["kt"], KT), (F["bt"], BT)):
                    for hq in range(2):
                        for hh in range(4):
                            h = hq * 4 + hh
                            k.tr(ptF4[:, hh, :], src[:, h * 64:(h + 1) * 64], ident_b)
                        k.copy(dstT[:, hq * 4:(hq + 1) * 4, :], ptF4, eng="act")
                return F["vb"]

            def head_chunk(L, b, z, h, v_, emit):
                hs = slice(h * 64, (h + 1) * 64)
                Vh = v_[:, hs]
                KRh = KR[0:64, h, :, :].rearrange("p a t -> p (a t)")
                lmk, mbt = L["lmk"], L["mbt"]
                k.mm(L["ps1"], KT[:, h, :], KRh)
                k.mm(L["ps2"], BT[:, h, :], KRh)
                k.tt(lmk, L["ps1"], mc1[z], ALU.mult)
                PT = L["PT"].next()
                k.tt(PT, L["ps2"][:, 0:128], mc2[z][:, 0:128], ALU.mult)
                k.tt(mbt, L["ps2"][:, 128:256], mc2[z][:, 128:256], ALU.mult)
                yield
                if DBG_RW < 6:
                    return
                k.mm(L["ps3"], KR[0:64, h, 0, :], BT[:, h, :])
                k.mm(L["ps4"], lmk[:, 0:128], Vh)
                PR = L["PR"].next()
                k.tt(PR[:, 0:128], L["ps3"], mnsl[z], ALU.mult)
                k.copy(PR[:, 128:192], F["kap"][:, hs], eng="pool")
                k.copy(PR[:, 192:256], L["ps4"], eng="act")
                yield
                if DBG_RW < 7:
                    return
                for it in range(7):
                    k.mm(L["p5"], PT, PR)
                    PRn = L["PR"].next()
                    if it < 6:
                        k.mm(L["p6"], PR[:, 0:128], PT)
                        PTn = L["PT"].next()
                        k.copy(PTn, L["p6"], eng="act")
                        k.copy(PRn[:, 0:128], L["p5"][:, 0:128], eng="act")
                    k.tt(PRn[:, 128:256], PR[:, 128:256], L["p5"][:, 128:256], ALU.add)
                    PR = PRn
                    if it < 6:
                        PT = PTn
                    yield
                if DBG_RW < 8.1:
                    return
                k.tr(L["pWT"], PR[:, 128:192], ident_f)
                k.copy(L["WT"], L["pWT"], eng="act")
                yield
                if DBG_RW < 8.2:
                    return
                Ah = A[b][h]
                Abh = Ab[b][h]
                k.mm(L["psz"], L["WT"], Abh[0:64, :])
                k.stt(L["Z"], PR[:, 192:256], -1.0, L["psz"], ALU.mult, ALU.subtract)
                yield
                if DBG_RW < 8.4:
                    return
                if DBG_RW < 8.6:
                    emit = False
                if emit and DBG_RW < 8.63:
                    k.mm(L["psy"], mbt, L["Z"], start=True, stop=False)
                    k.mm(L["psy"], lmk[:, 128:256], Vh, start=False, stop=True)
                elif emit:
                    k.mm(L["psy"], KR[:, h, 1, :], Abh, start=True, stop=False)
                    k.mm(L["psy"], mbt, L["Z"], start=False, stop=False)
                    k.mm(L["psy"], lmk[:, 128:256], Vh, start=False, stop=True)
                k.mm(L["psa"], F["bt"][:, hs], L["Z"], start=True, stop=False)
                k.mm(L["psa"], F["kt"][:, hs], Vh, start=False, stop=True)
                if emit and DBG_RW >= 8.65:
                    k.copy(ytile[:, hs], L["psy"], eng="dve")
                k.tt(L["tmpA"], L["psa"], Ah[0:64, :], ALU.add)
                k.ts(Ah[0:64, :], L["tmpA"], gC[:, h:h + 1], ALU.mult)
                k.copy(Abh[0:64, :], Ah[0:64, :], eng="act")
                yield

            def run_heads(b, z, v_, emit):
                for h0 in range(0, 8, 2):
                    gens = [head_chunk(lanes[j], b, z, h0 + j, v_, emit) for j in range(2)]
                    alive = [True, True]
                    while any(alive):
                        for j in range(2):
                            if alive[j]:
                                try:
                                    next(gens[j])
                                except StopIteration:
                                    alive[j] = False

            def finalize(b, i):
                k.dma("sp", y0t, S["y0"][b, i * 128:(i + 1) * 128, :])
                k.dma("sp", aux, S["rwaux"][b, i * 128:(i + 1) * 128, :])
                t0, t1 = F["t0"], F["t1"]
                k.tt(t0, ytile, y0t, ALU.add)
                k.op("dve", "tensor_reduce", out=gst[:, 0:8], in_=H(t0), axis=AX.X, op=ALU.add)
                k.ts(gst[:, 0:8], gst[:, 0:8], 1.0 / 64, ALU.mult)
                k.tt(H(t0), H(t0), gst[:, 0:8].unsqueeze(2).to_broadcast([128, 8, 64]), ALU.subtract)
                k.tt(t1, t0, t0, ALU.mult, eng="pool")
                k.op("dve", "tensor_reduce", out=gst[:, 8:16], in_=H(t1), axis=AX.X, op=ALU.add)
                k.ts(gst[:, 8:16], gst[:, 8:16], 1.0 / 64, ALU.mult, RW_GN_EPS, ALU.add)
                k.act(gst[:, 8:16], gst[:, 8:16], AF.Sqrt)
                k.op("dve", "reciprocal", out=gst[:, 8:16], in_=gst[:, 8:16])
                k.tt(H(t0), H(t0), gst[:, 8:16].unsqueeze(2).to_broadcast([128, 8, 64]), ALU.mult)
                k.tt(t0, t0, lngb, ALU.mult, eng="pool")
                k.tt(t0, t0, lnbb, ALU.add)
                k.tt(t0, t0, aux[:, 0:512], ALU.add, eng="pool")
                k.tt(yb, t0, aux[:, 512:1024], ALU.mult)
                for g in range(4):
                    k.tr(Bb[:, g, :], yb[:, g * 128:(g + 1) * 128], ident_b)
                k.copy(yTs, Bb, eng="act")
                k.dma("act", S["yT"][b, 512:1024, i * 128:(i + 1) * 128].rearrange("(g c) t -> c g t", c=128), yTs)

            for z in range(2):
                if z == 0:
                    order = list(range(NT))
                else:
                    order = list(range(NTc - 1, -1, -1)) + list(range(NT - 1, NTc - 1, -1))
                    for b in range(NB):
                        for h in range(8):
                            k.memset(A[b][h], 0.0)
                            k.memset(Ab[b][h], 0.0)
                for i in order:
                    emit = not (last and i < NTc)
                    for b in range(NB):
                        v_ = features(b, i, z)
                        if DBG_RW < 5:
                            continue
                        run_heads(b, z, v_, emit)
                        if DBG_RW < 10:
                            continue
                        if emit:
                            if z == 0:
                                k.dma("act", S["y0"][b, i * 128:(i + 1) * 128, :], ytile)
                            else:
                                finalize(b, i)
        k.end_phase()

    def phase_merge(l):
        k.begin_phase()
        last = (l == DEPTH - 1)
        lo = Tc if last else 0
        with contextlib.ExitStack() as es:
            wbr = k.sb("wbr", [128, 12, D], BF16, es)
            k.dma("pool", wbr, W["w_branch"][l].rearrange("(k p) d -> p k d", p=128))
            wo = k.sb("wo", [128, 8, D], BF16, es)
            k.dma("pool", wo, W["w_out"][l].rearrange("(k p) d -> p k d", p=128))
            wrt = k.sb("wrt", [128, 8, NE], F32, es)
            k.dma("sp", wrt, W["w_router"][l].rearrange("(k p) e -> p k e", p=128))
            lg = k.sb("ln1g", [128, D], F32, es)
            lb = k.sb("ln1b", [128, D], F32, es)
            k.dma("sp", lg, W["ln1_g"][l].unsqueeze(0).broadcast_to([128, D]))
            k.dma("sp", lb, W["ln1_b"][l].unsqueeze(0).broadcast_to([128, D]))
            bc = {}
            for nm, slot in (("g1", 2), ("sh2", 3), ("sc2", 4)):
                for which in range(2):
                    bc[nm, which] = k.sb(f"bc_{nm}{which}", [128, D], F32, es)
            affT = k.sb("affT", [NE, TT], F32, es)
            yTr = Ring([k.sb(f"myT{i}", [128, 12, 256], BF16, es) for i in range(2)])
            gTr = Ring([k.sb(f"mgT{i}", [128, 24, 256], BF16, es) for i in range(2)])
            mTr = Ring([k.sb(f"mmT{i}", [128, 8, 256], BF16, es) for i in range(2)])
            maccr = Ring([k.sb(f"macc{i}", [128, 256], F32, es) for i in range(2)])
            mtmpr = Ring([k.sb(f"mtmp{i}", [128, 256], F32, es) for i in range(2)])
            xr = Ring([k.sb(f"mx{i}", [128, D], F32, es) for i in range(2)])
            zr = Ring([k.sb(f"mz{i}", [128, D], F32, es) for i in range(2)])
            xmr = Ring([k.sb(f"mxm{i}", [128, D], F32, es) for i in range(2)])
            h2r = Ring([k.sb(f"mh2{i}", [128, D], F32, es) for i in range(2)])
            h2br = Ring([k.sb(f"mh2b{i}", [128, D], BF16, es) for i in range(2)])
            h2Tr = Ring([k.sb(f"mh2T{i}", [128, 8, 128], F32, es) for i in range(2)])
            str_ = Ring([k.sb(f"mst{i}", [128, 16], F32, es) for i in range(3)])
            smr = Ring([k.sb(f"msm{i}", [128, 4], F32, es) for i in range(3)])
            er = Ring([k.sb(f"mex{i}", [128, NE], F32, es) for i in range(2)])
            pmm = Ring([k.ps(f"mpm{i}", [128, 512], F32, es) for i in range(2)])
            pmix = Ring([k.ps(f"mpx{i}", [128, 512], F32, es) for i in range(2)])
            ptr_ = Ring([k.ps(f"mpt{i}", [128, 4, 128], F32, es) for i in range(2)])
            plq = k.ps("mplq", [128, 512], F32, es)
            plg = plq[:, 0:NE]
            paf = plq[0:NE, 128:256]
            for nm, slot in (("g1", 2), ("sh2", 3), ("sc2", 4)):
                k.dma("sp", bc[nm, 1], S["mod"][l, NB, slot * D:(slot + 1) * D].unsqueeze(0).broadcast_to([128, D]))
            k.ts(bc["sc2", 1], bc["sc2", 1], 1.0, ALU.add)
            for b in range(NB):
                for nm, slot in (("g1", 2), ("sh2", 3), ("sc2", 4)):
                    k.dma("sp", bc[nm, 0], S["mod"][l, b, slot * D:(slot + 1) * D].unsqueeze(0).broadcast_to([128, D]))
                k.ts(bc["sc2", 0], bc["sc2", 0], 1.0, ALU.add)
                for (tb0, n) in [(lo + a, n_) for (a, n_) in blocks(TT - lo, 256)]:
                    yTb, gTb, mT = yTr.next(), gTr.next(), mTr.next()
                    k.dma("sp", yTb[:, :, 0:n], S["yT"][b, :, tb0:tb0 + n].rearrange("(k p) t -> p k t", p=128))
                    k.dma("sp", gTb[:, :, 0:n], S["gT"][b, :, tb0:tb0 + n].rearrange("(k p) t -> p k t", p=128))
                    for dc in range(8):
                        macc = maccr.next()
                        for br in range(3):
                            ps = pmm.next()
                            for kc in range(4):
                                k.mm(ps[:, 0:n], wbr[:, br * 4 + kc, dc * 128:(dc + 1) * 128], yTb[:, br * 4 + kc, 0:n],
                                     start=(kc == 0), stop=(kc == 3))
                            if br == 0:
                                k.tt(macc[:, 0:n], ps[:, 0:n], gTb[:, dc, 0:n], ALU.mult)
                            else:
                                tmp = mtmpr.next()
                                k.tt(tmp[:, 0:n], ps[:, 0:n], gTb[:, br * 8 + dc, 0:n], ALU.mult)
                                if br == 1:
                                    k.tt(macc[:, 0:n], macc[:, 0:n], tmp[:, 0:n], ALU.add, eng="pool")
                                else:
                                    k.tt(mT[:, dc, 0:n], macc[:, 0:n], tmp[:, 0:n], ALU.add, eng="pool")
                    for s_ in range(n // 128):
                        t0 = tb0 + s_ * 128
                        which = 1 if t0 < Tc else 0
                        xt = xr.next()
                        k.dma("sp", xt, S["xcur"][b, t0:t0 + 128, :])
                        z = zr.next()
                        for half in range(2):
                            px = pmix.next()
                            for kk in range(8):
                                k.mm(px, mT[:, kk, s_ * 128:(s_ + 1) * 128], wo[:, kk, half * 512:(half + 1) * 512],
                                     start=(kk == 0), stop=(kk == 7))
                            k.tt(z[:, half * 512:(half + 1) * 512], px, bc["g1", which][:, half * 512:(half + 1) * 512], ALU.mult)
                        k.stt(z, xt, DN_ALPHA, z, ALU.mult, ALU.add)
                        st = str_.next()
                        k.op("dve", "bn_stats", out=st[:, 0:6], in_=z[:, 0:512])
                        k.op("dve", "bn_stats", out=st[:, 6:12], in_=z[:, 512:1024])
                        k.op("dve", "bn_aggr", out=st[:, 12:14], in_=st[:, 0:12])
                        k.ts(st[:, 13:14], st[:, 13:14], LN_EPS, ALU.add)
                        k.act(st[:, 13:14], st[:, 13:14], AF.Sqrt)
                        k.op("dve", "reciprocal", out=st[:, 13:14], in_=st[:, 13:14])
                        xm = xmr.next()
                        k.ts(xm, z, st[:, 12:13], ALU.subtract, st[:, 13:14], ALU.mult)
                        k.tt(xm, xm, lg, ALU.mult, eng="pool")
                        k.tt(xm, xm, lb, ALU.add)
                        k.dma("act", S["xmid"][b, t0:t0 + 128, :], xm)
                        h2 = h2r.next()
                        k.tt(h2, xm, bc["sc2", which], ALU.mult, eng="pool")
                        k.tt(h2, h2, bc["sh2", which], ALU.add)
                        h2b = h2br.next()
                        k.copy(h2b, h2, eng="act")
                        k.dma("act", S["h2"][b, t0:t0 + 128, :], h2b)
                        h2T = h2Tr.next()
                        for g in range(2):
                            pt = ptr_.next()
                            for j in range(4):
                                k.tr(pt[:, j, :], h2[:, (g * 4 + j) * 128:(g * 4 + j + 1) * 128], ident_f)
                            k.copy(h2T[:, g * 4:(g + 1) * 4, :], pt, eng="act")
                        for kk in range(8):
                            k.mm(plg, h2T[:, kk, :], wrt[:, kk, :], start=(kk == 0), stop=(kk == 7))
                        sm = smr.next()
                        k.op("dve", "tensor_reduce", out=sm[:, 0:1], in_=plg, axis=AX.X, op=ALU.max)
                        k.ts(sm[:, 0:1], sm[:, 0:1], -1.0, ALU.mult)
                        e = er.next()
                        k.act(e, plg, AF.Exp, bias=sm[:, 0:1], accum_out=sm[:, 1:2])
                        k.op("dve", "reciprocal", out=sm[:, 2:3], in_=sm[:, 1:2])
                        k.ts(e, e, sm[:, 2:3], ALU.mult)
                        k.tr(paf, e, ident_f)
                        k.copy(affT[:, t0:t0 + 128], paf, eng="act")
                k.dma("sp", S["aff"][b, :, lo:TT], affT[:, lo:TT])
        k.end_phase()

    def moe_segs(l):
        segs = []
        if l != DEPTH - 1:
            segs.append((0, 0, Tc, cfg.cap_ctx))
        segs.append((1, Tc, T, cfg.cap_lat))
        return segs

    def phase_route(l):
        k.begin_phase()
        with contextlib.ExitStack() as es:
            iota = k.sb("iota", [128, 256], F32, es)
            k.dma("sp", iota, CONST["iota"])
            zeros = k.sb("rzeros", [NE, T], F32, es)
            k.memset(zeros, 0.0)
            aff = k.sb("raff", [NE, T], F32, es)
            work = k.sb("rwork", [NE, T], F32, es)
            G = k.sb("rG", [NE, T], F32, es)
            mask = k.sb("rmask", [NE, T], F32, es)
            slot = k.sb("rslot", [NE, T], F32, es)
            mx8 = k.sb("rmx8", [NE, 8], F32, es)
            ntm = T // 128
            slotT = k.sb("rslotT", [128, ntm, NE], F32, es)
            GT = k.sb("rGT", [128, ntm, NE], F32, es)
            h2sb = k.sb("rh2", [128, ntm, D], BF16, es)
            selr = Ring([k.sb(f"rsel{i}", [128, ntm, cfg.cap_lat], BF16, es) for i in range(2)])
            xer = Ring([k.sb(f"rxe{i}", [128, 8, cfg.cap_lat], BF16, es) for i in range(2)])
            ptq = k.ps("rptq", [128, 2, NE], F32, es)
            pgr = Ring([k.ps(f"rpg{i}", [128, 512], F32, es) for i in range(4)])
            for b in range(NB):
                for (sid, tlo, Ts, cap) in moe_segs(l):
                    nt = Ts // 128
                    k.dma("sp", aff[:, 0:Ts], S["aff"][b, :, tlo:tlo + Ts])
                    k.dma("sp", h2sb[:, 0:nt, :], S["h2"][b, tlo:tlo + Ts, :].rearrange("(j p) d -> p j d", p=128))
                    k.copy(work[:, 0:Ts], aff[:, 0:Ts])
                    for _ in range(cap // 8):
                        k.op("dve", "max", out=mx8, in_=work[:, 0:Ts])
                        k.op("dve", "match_replace", out=work[:, 0:Ts], in_to_replace=mx8, in_values=work[:, 0:Ts], imm_value=0.0)
                    k.tt(G[:, 0:Ts], aff[:, 0:Ts], work[:, 0:Ts], ALU.subtract)
                    k.ts(mask[:, 0:Ts], G[:, 0:Ts], 0.0, ALU.is_gt)
                    k.op("dve", "tensor_tensor_scan", out=slot[:, 0:Ts], data0=mask[:, 0:Ts], data1=zeros[:, 0:Ts],
                         initial=0.0, op0=ALU.add, op1=ALU.add)
                    k.tt(slot[:, 0:Ts], slot[:, 0:Ts], mask[:, 0:Ts], ALU.mult)
                    k.ts(slot[:, 0:Ts], slot[:, 0:Ts], -1.0, ALU.add)
                    for i in range(nt):
                        k.tr(ptq[:, 0, :], slot[:, i * 128:(i + 1) * 128], ident_f[0:NE, 0:NE])
                        k.tr(ptq[:, 1, :], G[:, i * 128:(i + 1) * 128], ident_f[0:NE, 0:NE])
                        k.copy(slotT[:, i, :], ptq[:, 0, :], eng="act")
                        k.copy(GT[:, i, :], ptq[:, 1, :], eng="act")
                    k.dma("act", S["slotT"][b, tlo:tlo + Ts, :].rearrange("(j p) e -> p j e", p=128), slotT[:, 0:nt, :])
                    k.dma("act", S["GT"][b, tlo:tlo + Ts, :].rearrange("(j p) e -> p j e", p=128), GT[:, 0:nt, :])
                    for e in range(NE):
                        sel = selr.next()
                        for i in range(nt):
                            k.ts(sel[:, i, 0:cap], iota[:, 0:cap], slotT[:, i, e:e + 1], ALU.is_equal,
                                 eng=("pool" if i % 2 else "dve"))
                        xe = xer.next()
                        for kk in range(8):
                            pg = pgr.next()
                            for i in range(nt):
                                k.mm(pg[:, 0:cap], h2sb[:, i, kk * 128:(kk + 1) * 128], sel[:, i, 0:cap],
                                     start=(i == 0), stop=(i == nt - 1))
                            k.copy(xe[:, kk, 0:cap], pg[:, 0:cap], eng=("act" if kk % 2 else "dve"))
                        k.dma("act", S["xe"][b, sid, e][:, 0:cap].rearrange("(k p) c -> p k c", p=128), xe[:, :, 0:cap])
        k.end_phase()

    def phase_experts(l):
        k.begin_phase()
        segs = moe_segs(l)
        with contextlib.ExitStack() as es:
            wgr = Ring([k.sb(f"ewg{i}", [128, 8, 1024], BF16, es) for i in range(2)])
            wur = Ring([k.sb(f"ewu{i}", [128, 8, 1024], BF16, es) for i in range(2)])
            wdr = Ring([k.sb(f"ewd{i}", [128, 8, 1024], BF16, es) for i in range(2)])
            xer = Ring([k.sb(f"exe{i}", [128, 8, cfg.cap_lat], BF16, es) for i in range(2)])
            hidr = Ring([k.sb(f"ehid{i}", [128, 8, cfg.cap_lat], BF16, es) for i in range(2)])
            sgr = Ring([k.sb(f"esg{i}", [128, cfg.cap_lat], F32, es) for i in range(2)])
            yacc = {}
            for b in range(NB):
                for (sid, tlo, Ts, cap) in segs:
                    yacc[b, sid] = k.sb(f"eya{b}_{sid}", [128, (cap + 127) // 128, D], F32, es)
            yor = Ring([k.sb(f"eyo{i}", [128, 512], BF16, es) for i in range(3)])
            pgr = Ring([k.ps(f"epg{i}", [128, 512], F32, es) for i in range(2)])
            pur = Ring([k.ps(f"epu{i}", [128, 512], F32, es) for i in range(2)])
            pyr = Ring([k.ps(f"epy{i}", [128, 512], F32, es) for i in range(3)])
            for e in range(NE):
                for half in range(2):
                    wg, wu, wd = wgr.next(), wur.next(), wdr.next()
                    f0 = half * 1024
                    k.dma("pool", wg, W["w_e_gate"][l, e][:, f0:f0 + 1024].rearrange("(k p) f -> p k f", p=128))
                    k.dma("pool", wu, W["w_e_up"][l, e][:, f0:f0 + 1024].rearrange("(k p) f -> p k f", p=128))
                    k.dma("pool", wd, W["w_e_down"][l, e][f0:f0 + 1024, :].rearrange("(k p) d -> p k d", p=128))
                    for b in range(NB):
                        for (sid, tlo, Ts, cap) in segs:
                            xe = xer.next()
                            k.dma("sp", xe[:, :, 0:cap], S["xe"][b, sid, e][:, 0:cap].rearrange("(k p) c -> p k c", p=128))
                            hid = hidr.next()
                            for fc in range(8):
                                pg, pu = pgr.next(), pur.next()
                                for kk in range(8):
                                    k.mm(pg[:, 0:cap], wg[:, kk, fc * 128:(fc + 1) * 128], xe[:, kk, 0:cap],
                                         start=(kk == 0), stop=(kk == 7))
                                for kk in range(8):
                                    k.mm(pu[:, 0:cap], wu[:, kk, fc * 128:(fc + 1) * 128], xe[:, kk, 0:cap],
                                         start=(kk == 0), stop=(kk == 7))
                                sg = sgr.next()
                                k.act(sg[:, 0:cap], pg[:, 0:cap], AF.Silu)
                                k.tt(hid[:, fc, 0:cap], sg[:, 0:cap], pu[:, 0:cap], ALU.mult)
                            for ch in range((cap + 127) // 128):
                                M = min(128, cap - ch * 128)
                                for dh in range(2):
                                    py = pyr.next()
                                    for fc in range(8):
                                        k.mm(py[0:M, :], hid[:, fc, ch * 128:ch * 128 + M], wd[:, fc, dh * 512:(dh + 1) * 512],
                                             start=(fc == 0), stop=(fc == 7))
                                    ya = yacc[b, sid][0:M, ch, dh * 512:(dh + 1) * 512]
                                    if half == 0:
                                        k.copy(ya, py[0:M, :], eng="act")
                                    else:
                                        yo = yor.next()
                                        k.tt(yo[0:M, :], ya, py[0:M, :], ALU.add)
                                        r0 = e * cap + ch * 128
                                        k.dma("act", S["ye"][b, sid, r0:r0 + M, dh * 512:(dh + 1) * 512], yo[0:M, :])
        k.end_phase()

    def phase_scatter(l):
        k.begin_phase()
        last = (l == DEPTH - 1)
        with contextlib.ExitStack() as es:
            iota = k.sb("siota", [128, 256], F32, es)
            k.dma("sp", iota, CONST["iota"])
            lg = k.sb("ln2g", [128, D], F32, es)
            lb = k.sb("ln2b", [128, D], F32, es)
            k.dma("sp", lg, W["ln2_g"][l].unsqueeze(0).broadcast_to([128, D]))
            k.dma("sp", lb, W["ln2_b"][l].unsqueeze(0).broadcast_to([128, D]))
            g2b = [k.sb(f"g2b{i}", [128, D], F32, es) for i in range(2)]
            nKmax = NE * cfg.cap_lat // 128
            ye = k.sb("sye", [128, nKmax, D], BF16, es)
            ntm = T // 128
            slotT = k.sb("sslotT", [128, ntm, NE], F32, es)
            GT = k.sb("sGT", [128, ntm, NE], F32, es)
            selr = Ring([k.sb(f"ssel{i}", [128, NE * cfg.cap_lat], BF16, es) for i in range(2)])
            selTr = Ring([k.sb(f"sselT{i}", [128, nKmax, 128], BF16, es) for i in range(2)])
            xr = Ring([k.sb(f"sx{i}", [128, D], F32, es) for i in range(2)])
            zr = Ring([k.sb(f"sz{i}", [128, D], F32, es) for i in range(2)])
            xor_ = Ring([k.sb(f"sxo{i}", [128, D], F32, es) for i in range(2)])
            str_ = Ring([k.sb(f"sst{i}", [128, 16], F32, es) for i in range(3)])
            ptr_ = Ring([k.ps(f"spt{i}", [128, 8, 128], BF16, es) for i in range(2)])
            pfr = Ring([k.ps(f"spf{i}", [128, 512], F32, es) for i in range(4)])
            k.dma("sp", g2b[1], S["mod"][l, NB, 5 * D:6 * D].unsqueeze(0).broadcast_to([128, D]))
            for b in range(NB):
                k.dma("sp", g2b[0], S["mod"][l, b, 5 * D:6 * D].unsqueeze(0).broadcast_to([128, D]))
                for (sid, tlo, Ts, cap) in moe_segs(l):
                    nt = Ts // 128
                    nK = NE * cap // 128
                    which = 1 if sid == 0 else 0
                    k.dma("sp", ye[:, 0:nK, :], S["ye"][b, sid, 0:NE * cap, :].rearrange("(j p) d -> p j d", p=128))
                    k.dma("sp", slotT[:, 0:nt, :], S["slotT"][b, tlo:tlo + Ts, :].rearrange("(j p) e -> p j e", p=128))
                    k.dma("sp", GT[:, 0:nt, :], S["GT"][b, tlo:tlo + Ts, :].rearrange("(j p) e -> p j e", p=128))
                    for i in range(nt):
                        t0 = tlo + i * 128
                        sel = selr.next()
                        for e in range(NE):
                            k.ts(sel[:, e * cap:(e + 1) * cap], iota[:, 0:cap], slotT[:, i, e:e + 1], ALU.is_equal,
                                 GT[:, i, e:e + 1], ALU.mult, eng=("pool" if e % 2 else "dve"))
                        selT = selTr.next()
                        for j0 in range(0, nK, 8):
                            pt = ptr_.next()
                            nj = min(8, nK - j0)
                            for j in range(nj):
                                k.tr(pt[:, j, :], sel[:, (j0 + j) * 128:(j0 + j + 1) * 128], ident_b)
                            k.copy(selT[:, j0:j0 + nj, :], pt[:, 0:nj, :], eng="act")
                        xt = xr.next()
                        k.dma("sp", xt, S["xmid"][b, t0:t0 + 128, :])
                        z = zr.next()
                        for dh in range(2):
                            pf = pfr.next()
                            for j in range(nK):
                                k.mm(pf, selT[:, j, :], ye[:, j, dh * 512:(dh + 1) * 512], start=(j == 0), stop=(j == nK - 1))
                            k.tt(z[:, dh * 512:(dh + 1) * 512], pf, g2b[which][:, dh * 512:(dh + 1) * 512], ALU.mult)
                        k.stt(z, xt, DN_ALPHA, z, ALU.mult, ALU.add)
                        st = str_.next()
                        k.op("dve", "bn_stats", out=st[:, 0:6], in_=z[:, 0:512])
                        k.op("dve", "bn_stats", out=st[:, 6:12], in_=z[:, 512:1024])
                        k.op("dve", "bn_aggr", out=st[:, 12:14], in_=st[:, 0:12])
                        k.ts(st[:, 13:14], st[:, 13:14], LN_EPS, ALU.add)
                        k.act(st[:, 13:14], st[:, 13:14], AF.Sqrt)
                        k.op("dve", "reciprocal", out=st[:, 13:14], in_=st[:, 13:14])
                        xo = xor_.next()
                        k.ts(xo, z, st[:, 12:13], ALU.subtract, st[:, 13:14], ALU.mult)
                        k.tt(xo, xo, lg, ALU.mult, eng="pool")
                        k.tt(xo, xo, lb, ALU.add)
                        if last:
                            k.dma("act", OUT[b, t0 - Tc:t0 - Tc + 128, :], xo)
                        else:
                            k.dma("act", S["xcur"][b, t0:t0 + 128, :], xo)
        k.end_phase()

    PH = phases
    for l in range(cfg.L):
        with contextlib.ExitStack() as les:
            modT = k.sb("modT", [128, 48, 3], F32, les)
            if PH is None or "mod" in PH:
                phase_mod(l, modT)
            if PH is None or "inproj" in PH:
                phase_inproj(l, modT)
            if PH is None or "attn" in PH:
                phase_attn(l)
            if PH is None or "rwkv" in PH:
                phase_rwkv(l)
            if PH is None or "gmlp" in PH:
                phase_gmlp(l)
            if PH is None or "merge" in PH:
                phase_merge(l)
            if PH is None or "moe" in PH:
                phase_route(l)
                phase_experts(l)
                phase_scatter(l)
        k.end_phase()
    outs = [OUT] + [S[n] for n in dbg_out]
    k.final_wait([o.tb for o in outs])
    top.close()
    return nc, hc, k


_CACHE = {}


def _get_program(cfg_key):
    if cfg_key not in _CACHE:
        cfg = Cfg(*cfg_key)
        nc, hc, k = build(cfg)
        _CACHE[cfg_key] = (cfg, nc, hc)
    return _CACHE[cfg_key]


def kernel(**inputs):
    x = np.asarray(inputs["x"], np.float32)
    B, T, _ = x.shape
    ctx = np.asarray(inputs["ctx"], np.float32)
    Tc = ctx.shape[1]
    n_cores = 8
    NB = B // n_cores
    L = inputs["w_in"].shape[0]
    cfg, nc, hc = _get_program((T, Tc, NB, L))
    c = np.asarray(inputs["c"], np.float32)
    c_ctx = np.asarray(inputs["c_ctx"], np.float32)
    shared = {n: np.ascontiguousarray(np.asarray(inputs[n], np.float32)) for n in WEIGHT_SPECS}
    for n, a in hc.items():
        shared["k_" + n] = a
    in_maps = []
    for ci in range(n_cores):
        sl = slice(ci * NB, (ci + 1) * NB)
        m = dict(shared)
        perm = np.roll(np.arange(NE), -ci)
        m["w_router"] = np.ascontiguousarray(shared["w_router"][:, :, perm])
        for n in ("w_e_gate", "w_e_up", "w_e_down"):
            m[n] = np.ascontiguousarray(shared[n][:, perm])
        m["x"] = np.ascontiguousarray(x[sl])
        m["ctx"] = np.ascontiguousarray(ctx[sl])
        m["c3"] = np.ascontiguousarray(np.concatenate([c[sl], c_ctx[None]], 0))
        in_maps.append(m)
    res = run_bass_kernel_spmd(nc, in_maps, core_ids=list(range(n_cores)))
    return np.concatenate([np.asarray(r["out"], np.float32) for r in res.results], axis=0)
```

```python
import contextlib
import math
import numpy as np
import ml_dtypes
import concourse.bass as bass
import concourse.mybir as mybir
from concourse.bass_utils import run_bass_kernel_spmd

F32 = mybir.dt.float32
BF16 = mybir.dt.bfloat16
AF = mybir.ActivationFunctionType
ALU = mybir.AluOpType
AX = mybir.AxisListType

D = 1024
GRID_W = 64
DA_H = 4
RW_H = 8
NE = 16
FF = 2048
IN_COLS = 7424
C_Q, C_K, C_V, C_RW, C_SG, C_G = 0, 512, 1024, 1536, 3328, 4352
RW_COLS = 1792
LN_EPS = 1e-5
DA_EPS = 1e-5
RW_GN_EPS = 64e-5
DEPTH = 2
DN_ALPHA = (2 * DEPTH) ** 0.25
import os
DBG_RW = float(os.environ.get('DBG_RW', '99'))
DBG_AT = float(os.environ.get('DBG_AT', '99'))


class TB:
    __slots__ = ("w", "r", "dsem", "dcnt", "name", "space")

    def __init__(self, name, space):
        self.w = {}
        self.r = {}
        self.dsem = {}
        self.dcnt = {}
        self.name = name
        self.space = space


class V:
    __slots__ = ("ap", "tb")

    def __init__(self, ap, tb):
        self.ap = ap
        self.tb = tb

    def __getitem__(self, idx):
        return V(self.ap[idx], self.tb)

    def rearrange(self, *a, **k):
        return V(self.ap.rearrange(*a, **k), self.tb)

    def bitcast(self, dt):
        return V(self.ap.bitcast(dt), self.tb)

    def unsqueeze(self, i):
        return V(self.ap.unsqueeze(i), self.tb)

    def to_broadcast(self, shape):
        return V(self.ap.to_broadcast(shape), self.tb)

    def broadcast_to(self, shape):
        return V(self.ap.broadcast_to(shape), self.tb)

    @property
    def shape(self):
        return self.ap.shape


class Grp:
    def __init__(self, sem, cnt):
        self.sem = sem
        self.cnt = cnt


class Eng:
    def __init__(self, name, h, sem):
        self.name = name
        self.h = h
        self.sem = sem
        self.cnt = 0
        self.seen = {}


class K:
    def __init__(self, nc, es):
        self.nc = nc
        self.es = es
        self.E = {}
        for name, h in (("pe", nc.tensor), ("act", nc.scalar), ("dve", nc.vector),
                        ("pool", nc.gpsimd), ("sp", nc.sync)):
            self.E[name] = Eng(name, h, es.enter_context(nc.semaphore("sem_" + name)))
        self.n_inst = 0
        self.tbs = []
        self.sem_free = {"sw": [], "hw": []}
        self.phase_mark = 0
        self.groups = []

    def _tb(self, name, space):
        tb = TB(name, space)
        self.tbs.append(tb)
        return tb

    def sb(self, name, shape, dt=F32, es=None):
        self.uid = getattr(self, "uid", 0) + 1
        name = "%s_%d" % (name, self.uid)
        t = (es or self.es).enter_context(self.nc.sbuf_tensor(name, list(shape), dt))
        return V(t[:], self._tb(name, "sb"))

    def ps(self, name, shape, dt=F32, es=None):
        self.uid = getattr(self, "uid", 0) + 1
        name = "%s_%d" % (name, self.uid)
        t = (es or self.es).enter_context(self.nc.psum_tensor(name, list(shape), dt))
        return V(t[:], self._tb(name, "ps"))

    def sub(self, v, name):
        return V(v.ap, self._tb(name, v.tb.space))

    def dram(self, name, shape, dt=F32, kind="Internal"):
        t = self.nc.dram_tensor(name, list(shape), dt, kind=kind)
        return V(t.ap(), self._tb(name, "dram"))

    def _sem_get(self, name, qk):
        if self.sem_free[qk]:
            return self.sem_free[qk].pop()
        return [self.es.enter_context(self.nc.semaphore(name)), 0]

    def group(self, name):
        sem, cnt = self._sem_get("g_" + name, "hw")
        g = Grp(sem, cnt)
        self.groups.append(g)
        return g

    def end_phase(self):
        self.barrier()
        for tb in self.tbs:
            tb.w = {}
            tb.r = {}
        for tb in self.tbs[self.phase_mark:]:
            for qk, sc in tb.dsem.items():
                self.sem_free["sw" if qk == "pool" else "hw"].append(sc)
            tb.dsem = {}
        del self.tbs[self.phase_mark:]
        for g in self.groups:
            self.sem_free["hw"].append([g.sem, g.cnt])
        self.groups = []

    def begin_phase(self):
        self.phase_mark = len(self.tbs)

    def _wait(self, E, ev):
        sem, val = ev
        if isinstance(val, Grp):
            val = val.cnt
        key = id(sem)
        if E.seen.get(key, 0) >= val:
            return
        E.h.wait_ge(sem, val)
        E.seen[key] = val

    def _sync(self, E, reads, writes, skip_self=False):
        for tb in reads:
            for ev in tb.w.values():
                if not (skip_self and ev[0] is E.sem):
                    self._wait(E, ev)
        for tb in writes:
            for ev in list(tb.w.values()) + list(tb.r.values()):
                if not (skip_self and ev[0] is E.sem):
                    self._wait(E, ev)

    def _commit(self, ev, reads, writes):
        for tb in reads:
            tb.r[id(ev[0])] = ev
        for tb in writes:
            tb.w = {id(ev[0]): ev}
            tb.r = {}

    def op(self, eng, name, **kw):
        E = self.E[eng]
        reads, writes, args = [], [], {}
        for k, v in kw.items():
            if isinstance(v, V):
                (writes if k in ("out", "accum_out", "ap") else reads).append(v.tb)
                args[k] = v.ap
            else:
                args[k] = v
        self._sync(E, reads, writes, skip_self=(eng == "pe"))
        ins = getattr(E.h, name)(**args)
        E.cnt += 1
        ins.then_inc(E.sem, 1)
        self._commit((E.sem, E.cnt), reads, writes)
        self.n_inst += 1
        return ins

    def dma(self, q, out, in_, grp=None, **kw):
        E = self.E[q]
        dst, src = out.tb, in_.tb
        qk = q
        kind = "sw" if q == "pool" else "hw"
        if grp is not None:
            sem = grp.sem
        else:
            own = dst if dst.space == "sb" else (src if src.space == "sb" else dst)
            if qk not in own.dsem:
                own.dsem[qk] = self._sem_get("d%s_%s" % (qk, own.name), kind)
            sem = own.dsem[qk][0]
        for ev in src.w.values():
            self._wait(E, ev)
        if dst.space != "dram":
            for ev in dst.w.values():
                if ev[0] is not sem:
                    self._wait(E, ev)
        for ev in dst.r.values():
            self._wait(E, ev)
        E.h.dma_start(out=out.ap, in_=in_.ap, **kw).then_inc(sem, 16)
        if grp is not None:
            grp.cnt += 16
            ev = (sem, grp)
        else:
            own.dsem[qk][1] += 16
            ev = (sem, own.dsem[qk][1])
        src.r[id(sem)] = ev
        if dst.space == "dram":
            dst.w[id(sem)] = ev
        else:
            dst.w = {id(sem): ev}
        dst.r = {}
        self.n_inst += 1

    def barrier(self):
        evs = [(E.sem, E.cnt) for E in self.E.values() if E.cnt > 0]
        for tb in self.tbs:
            for qk, sc in tb.dsem.items():
                evs.append((sc[0], sc[1]))
        for g in self.groups:
            evs.append((g.sem, g.cnt))
        for E in self.E.values():
            for ev in evs:
                if ev[0] is E.sem:
                    continue
                self._wait(E, ev)

    def final_wait(self, tbs):
        E = self.E["sp"]
        for tb in tbs:
            for ev in tb.w.values():
                self._wait(E, ev)

    def mm(self, out, lhsT, rhs, start=True, stop=True):
        return self.op("pe", "matmul", out=out, lhsT=lhsT, rhs=rhs, start=start, stop=stop)

    def tr(self, out, in_, ident):
        return self.op("pe", "transpose", out=out, in_=in_, identity=ident)

    def act(self, out, in_, func, **kw):
        return self.op("act", "activation", out=out, in_=in_, func=func, **kw)

    def tt(self, out, in0, in1, op, eng="dve"):
        return self.op(eng, "tensor_tensor", out=out, in0=in0, in1=in1, op=op)

    def ts(self, out, in0, s1, op0, s2=None, op1=None, eng="dve", **kw):
        if op1 is None:
            return self.op(eng, "tensor_scalar", out=out, in0=in0, scalar1=s1, scalar2=None, op0=op0, **kw)
        return self.op(eng, "tensor_scalar", out=out, in0=in0, scalar1=s1, scalar2=s2, op0=op0, op1=op1, **kw)

    def stt(self, out, in0, scalar, in1, op0, op1):
        return self.op("dve", "scalar_tensor_tensor", out=out, in0=in0, scalar=scalar, in1=in1, op0=op0, op1=op1)

    def copy(self, out, in_, eng="dve"):
        if eng == "act":
            return self.op("act", "activation", out=out, in_=in_, func=AF.Identity)
        return self.op(eng, "tensor_copy", out=out, in_=in_)

    def memset(self, ap, val, eng="dve"):
        return self.op(eng, "memset", ap=ap, constant=val)


class Ring:
    def __init__(self, items):
        self.items = items
        self.i = 0

    def next(self):
        it = self.items[self.i % len(self.items)]
        self.i += 1
        return it


def blocks(total, size):
    out = []
    s = 0
    while s < total:
        n = min(size, total - s)
        out.append((s, n))
        s += n
    return out


class Cfg:
    def __init__(self, T=2048, Tc=256, NB=2, L=2, debug=False):
        self.T, self.Tc, self.NB, self.L, self.debug = T, Tc, NB, L, debug
        self.TT = T + Tc
        self.NT = self.TT // 128
        self.NTc = Tc // 128
        self.cap_lat = 2 * T // NE
        self.cap_ctx = 2 * Tc // NE
        self.RWROWS = self.TT + 3


def rw_row(cfg, t):
    return 1 + t if t < cfg.Tc else 2 + t


def host_consts(cfg):
    T, Tc, TT, NB = cfg.T, cfg.Tc, cfg.TT, cfg.NB
    c = {}
    c["ident_f"] = np.eye(128, dtype=np.float32)
    c["ident_b"] = np.eye(128).astype(ml_dtypes.bfloat16)
    rows = T // GRID_W
    row = np.repeat(np.arange(rows, dtype=np.float32), GRID_W)
    col = np.tile(np.arange(GRID_W, dtype=np.float32), rows)
    half = 32
    inv_freq = (10000.0 ** (-np.arange(0, half, 2, dtype=np.float32) / half)).astype(np.float32)
    ar = row[:, None] * inv_freq
    ac = col[:, None] * inv_freq
    ang = np.concatenate([ar, ar, ac, ac], axis=-1).astype(np.float32)
    cos = np.concatenate([np.ones((Tc, 64), np.float32), np.cos(ang)], 0)
    sin = np.concatenate([np.zeros((Tc, 64), np.float32), np.sin(ang)], 0)
    cosT = np.tile(np.concatenate([cos.T, cos.T], 0), (1, NB)).astype(np.float32)
    sinT = np.tile(np.concatenate([sin.T, sin.T], 0), (1, NB)).astype(np.float32)
    c["cosT"], c["sinT"] = np.ascontiguousarray(cosT), np.ascontiguousarray(sinT)
    R = np.zeros((64, 64), np.float32)
    for base in (0, 32):
        for i in range(16):
            R[base + i, base + 16 + i] = -1.0
            R[base + 16 + i, base + i] = 1.0
    R2 = np.zeros((128, 128), np.float32)
    R2[:64, :64] = R
    R2[64:, 64:] = R
    c["rotT"] = np.ascontiguousarray(R2.T)
    u = np.arange(128)[:, None]
    s = np.arange(128)[None, :]
    c["tri"] = np.stack([(u <= s), (u >= s)]).astype(np.float32)
    c["m_su"] = np.stack([(u < s), (u > s)]).astype(np.float32)
    c["m_iu"] = np.stack([(u <= s), (u >= s)]).astype(np.float32)
    c["m_nsu"] = -c["m_su"]
    c["m_nsl"] = -np.stack([(u > s), (u < s)]).astype(np.float32)
    c["iota"] = np.tile(np.arange(256, dtype=np.float32)[None, :], (128, 1))
    c["ones_f"] = np.ones((128, 128), np.float32)
    return c


WEIGHT_SPECS = {
    "w_mod": (D, 6 * D), "b_mod": (6 * D,), "w_in": (D, IN_COLS), "da_lambda": (4, 64), "da_norm_g": (128,),
    "rw_shift_mu": (2, RW_COLS), "rw_w0": (2, 512), "rw_w2": (2, 64, 512), "rw_a0": (2, 512),
    "rw_a2": (2, 64, 512), "rw_k_k": (512,), "rw_k_a": (512,), "rw_r_k": (8, 64), "rw_ln_g": (512,),
    "rw_ln_b": (512,), "rw_g2": (128, 512), "sg_norm_g": (512,), "sg_norm_b": (512,), "sg_w": (4, 128, 128),
    "sg_b": (4, 128), "w_branch": (1536, D), "w_out": (D, D), "ln1_g": (D,), "ln1_b": (D,),
    "w_router": (D, NE), "w_e_gate": (NE, D, FF), "w_e_up": (NE, D, FF), "w_e_down": (NE, FF, D),
    "ln2_g": (D,), "ln2_b": (D,),
}


def build(cfg, phases=None, dbg_out=(), dbg_in=()):
    nc = bass.Bass("TRN2", target_bir_lowering=False)
    T, Tc, TT, NB, NT, NTc = cfg.T, cfg.Tc, cfg.TT, cfg.NB, cfg.NT, cfg.NTc
    NS = NB * TT
    top = contextlib.ExitStack()
    k = K(nc, top)

    def din(name, shape, dt=F32):
        return V(nc.dram_tensor(name, list(shape), dt, kind="ExternalInput").ap(), k._tb(name, "dram"))

    X = din("x", [NB, T, D])
    CTX = din("ctx", [NB, Tc, D])
    C3 = din("c3", [NB + 1, D])
    W = {n: din(n, [cfg.L] + list(s)) for n, s in WEIGHT_SPECS.items()}
    hc = host_consts(cfg)
    CONST = {n: din("k_" + n, list(a.shape), BF16 if a.dtype == ml_dtypes.bfloat16 else F32) for n, a in hc.items()}
    OUT = V(nc.dram_tensor("out", [NB, T, D], F32, kind="ExternalOutput").ap(), k._tb("out", "dram"))

    S = {}

    def scratch(name, shape, dt=F32):
        kind = "ExternalOutput" if name in dbg_out else ("ExternalInput" if name in dbg_in else "Internal")
        S[name] = k.dram("s_" + name, shape, dt, kind=kind)
        return S[name]

    scratch("xcur", [NB, TT, D])
    scratch("mod", [cfg.L, 3, 6 * D])
    scratch("qk", [NB, 1024, TT], BF16)
    scratch("v", [NB, TT, 512], BF16)
    scratch("rw", [NB, cfg.RWROWS, RW_COLS])
    scratch("sg", [NB, TT, 1024])
    scratch("gT", [NB, 3072, TT], BF16)
    scratch("yT", [NB, 1536, TT], BF16)
    scratch("y0", [NB, TT, 512])
    scratch("rwaux", [NB, TT, 1024])
    scratch("xmid", [NB, TT, D])
    scratch("h2", [NB, TT, D], BF16)
    scratch("aff", [NB, NE, TT])
    scratch("slotT", [NB, TT, NE])
    scratch("GT", [NB, TT, NE])
    scratch("xe", [NB, 2, NE, D, cfg.cap_lat], BF16)
    scratch("ye", [NB, 2, NE * cfg.cap_lat, D], BF16)

    ident_f = k.sb("ident_f", [128, 128])
    ident_b = k.sb("ident_b", [128, 128], BF16)
    k.dma("sp", ident_f, CONST["ident_f"])
    k.dma("sp", ident_b, CONST["ident_b"])
    zrow = k.sb("zrow", [1, RW_COLS])
    k.memset(zrow, 0.0)
    for b in range(NB):
        for r in (0, Tc + 1, TT + 2):
            if "rw" not in dbg_in:
                k.dma("sp", S["rw"][b, r:r + 1, :], zrow)
    for b in range(NB):
        k.dma("sp", S["xcur"][b, 0:Tc, :], CTX[b])
        k.dma("sp", S["xcur"][b, Tc:TT, :], X[b])

    def stream_segs(t0, n):
        out = []
        s = t0
        while s < t0 + n:
            b = s // TT
            e = min(t0 + n, (b + 1) * TT)
            out.append((b, s - b * TT, s - t0, e - s))
            s = e
        return out

    def phase_mod(l, modT):
        k.begin_phase()
        with contextlib.ExitStack() as es:
            cT = k.sb("cT", [128, 8, 3], F32, es)
            for r in range(3):
                k.dma("sp", cT[:, :, r], C3[r].rearrange("(k p) -> p k", p=128), allow_slow_non_contiguous=True)
            scT = k.sb("scT", [128, 8, 3], BF16, es)
            k.act(scT, cT, AF.Silu)
            bm = k.sb("bm", [3, 6 * D], F32, es)
            k.dma("sp", bm, W["b_mod"][l].unsqueeze(0).broadcast_to([3, 6 * D]))
            modrow = k.sb("modrow", [3, 6 * D], F32, es)
            wr = Ring([k.sb(f"wm{i}", [128, 8, 512], BF16, es) for i in range(2)])
            pr = Ring([k.ps(f"pm{i}", [3, 512], F32, es) for i in range(2)])
            for (c0, n) in blocks(6 * D, 512):
                wb = wr.next()
                k.dma("pool", wb, W["w_mod"][l][:, c0:c0 + n].rearrange("(k p) c -> p k c", p=128))
                ps = pr.next()
                for kk in range(8):
                    k.mm(ps, scT[:, kk, :], wb[:, kk, :], start=(kk == 0), stop=(kk == 7))
                k.tt(modrow[:, c0:c0 + n], ps, bm[:, c0:c0 + n], ALU.add)
            k.dma("sp", S["mod"][l], modrow)
            for j in range(6):
                for r in range(3):
                    k.dma("sp", modT[:, j * 8:(j + 1) * 8, r],
                          S["mod"][l, r, j * D:(j + 1) * D].rearrange("(k p) -> p k", p=128),
                          allow_slow_non_contiguous=True)
            for j in (1, 4):
                k.ts(modT[:, j * 8:(j + 1) * 8, :], modT[:, j * 8:(j + 1) * 8, :], 1.0, ALU.add)
        k.end_phase()

    def phase_inproj(l, modT):
        k.begin_phase()
        with contextlib.ExitStack() as es:
            hT = k.sb("hT", [128, 8, NS], BF16, es)
            cosT = k.sb("cosT", [128, NS], F32, es)
            sinT = k.sb("sinT", [128, NS], F32, es)
            rotT = k.sb("rotT", [128, 128], F32, es)
            k.dma("sp", cosT, CONST["cosT"])
            k.dma("sp", sinT, CONST["sinT"])
            k.dma("sp", rotT, CONST["rotT"])
            xr = Ring([k.sb(f"xt{i}", [128, D], F32, es) for i in range(3)])
            ptr = Ring([k.ps(f"ptr{i}", [128, 4, 128], F32, es) for i in range(2)])
            for b in range(NB):
                for i in range(NT):
                    r = 2 if i < NTc else b
                    xt = xr.next()
                    k.dma("sp", xt, S["xcur"][b, i * 128:(i + 1) * 128, :])
                    for g in range(2):
                        pt = ptr.next()
                        for j in range(4):
                            kk = g * 4 + j
                            k.tr(pt[:, j, :], xt[:, kk * 128:(kk + 1) * 128], ident_f)
                        for j in range(4):
                            kk = g * 4 + j
                            s0 = b * TT + i * 128
                            k.act(hT[:, kk, s0:s0 + 128], pt[:, j, :], AF.Identity,
                                  scale=modT[:, 8 + kk, r:r + 1], bias=modT[:, kk, r:r + 1])
            wr = Ring([k.sb(f"wi{i}", [128, 8, 512], BF16, es) for i in range(3)])
            pmm = Ring([k.ps(f"pmm{i}", [128, 512], F32, es) for i in range(3)])
            prot = Ring([k.ps(f"prot{i}", [128, 512], F32, es) for i in range(2)])
            qs_r = Ring([k.sb(f"qs{i}", [128, 512], F32, es) for i in range(2)])
            t1_r = Ring([k.sb(f"t1{i}", [128, 512], F32, es) for i in range(2)])
            ob_r = Ring([k.sb(f"ob{i}", [128, 512], BF16, es) for i in range(3)])
            of_r = Ring([k.sb(f"of{i}", [128, 512], F32, es) for i in range(3)])
            sblocks = blocks(NS, 512)
            colblocks = []
            for (g0, g1) in ((C_Q, C_V), (C_V, C_RW), (C_RW, C_SG), (C_SG, C_G), (C_G, IN_COLS)):
                colblocks += [(g0 + a, n) for (a, n) in blocks(g1 - g0, 512)]
            for (c0, ncol) in colblocks:
                wb = wr.next()
                k.dma("pool", wb[:, :, 0:ncol], W["w_in"][l][:, c0:c0 + ncol].rearrange("(k p) c -> p k c", p=128))
                if c0 < C_V or c0 >= C_G:
                    for cc in range(ncol // 128):
                        col = c0 + cc * 128
                        for (t0, n) in sblocks:
                            ps = pmm.next()
                            for kk in range(8):
                                k.mm(ps[:, 0:n], wb[:, kk, cc * 128:(cc + 1) * 128], hT[:, kk, t0:t0 + n],
                                     start=(kk == 0), stop=(kk == 7))
                            ob = ob_r.next()
                            if col < C_V:
                                qs = qs_r.next()
                                k.act(qs[:, 0:n], ps[:, 0:n], AF.Identity, scale=(0.125 if col < C_K else 1.0))
                                pr_ = prot.next()
                                k.mm(pr_[:, 0:n], rotT, qs[:, 0:n])
                                t1 = t1_r.next()
                                k.tt(t1[:, 0:n], qs[:, 0:n], cosT[:, t0:t0 + n], ALU.mult, eng="pool")
                                k.tt(qs[:, 0:n], pr_[:, 0:n], sinT[:, t0:t0 + n], ALU.mult)
                                k.tt(ob[:, 0:n], qs[:, 0:n], t1[:, 0:n], ALU.add)
                                for (b, tb_, off, ln) in stream_segs(t0, n):
                                    k.dma("pool", S["qk"][b, col:col + 128, tb_:tb_ + ln], ob[:, off:off + ln])
                            else:
                                k.act(ob[:, 0:n], ps[:, 0:n], AF.Sigmoid)
                                gc = col - C_G
                                for (b, tb_, off, ln) in stream_segs(t0, n):
                                    k.dma("act", S["gT"][b, gc:gc + 128, tb_:tb_ + ln], ob[:, off:off + ln])
                else:
                    for b in range(NB):
                        for i in range(NT):
                            s0 = b * TT + i * 128
                            ps = pmm.next()
                            for kk in range(8):
                                k.mm(ps[:, 0:ncol], hT[:, kk, s0:s0 + 128], wb[:, kk, 0:ncol],
                                     start=(kk == 0), stop=(kk == 7))
                            if c0 < C_RW:
                                ob = ob_r.next()
                                k.copy(ob[:, 0:ncol], ps[:, 0:ncol], eng="act")
                                k.dma("act", S["v"][b, i * 128:(i + 1) * 128, c0 - C_V:c0 - C_V + ncol], ob[:, 0:ncol])
                            elif c0 < C_SG:
                                of = of_r.next()
                                k.copy(of[:, 0:ncol], ps[:, 0:ncol], eng="act")
                                r0 = rw_row(cfg, i * 128)
                                k.dma("act", S["rw"][b, r0:r0 + 128, c0 - C_RW:c0 - C_RW + ncol], of[:, 0:ncol])
                            else:
                                of = of_r.next()
                                k.act(of[:, 0:ncol], ps[:, 0:ncol], AF.Gelu_apprx_tanh)
                                k.dma("act", S["sg"][b, i * 128:(i + 1) * 128, c0 - C_SG:c0 - C_SG + ncol], of[:, 0:ncol])
        k.end_phase()

    def phase_attn(l):
        k.begin_phase()
        last = (l == DEPTH - 1)
        lam_init = 0.8 - 0.6 * math.exp(-0.3 * l)
        with contextlib.ExitStack() as es:
            lamt = k.sb("lamt", [128, 256], F32, es)
            k.dma("sp", lamt, W["da_lambda"][l].rearrange("a d -> (a d)").unsqueeze(0).broadcast_to([128, 256]))
            lv = lamt.rearrange("p (a d) -> p a d", a=4)
            prod = k.sb("lprod", [128, 2, 64], F32, es)
            k.tt(prod[:, 0, :], lv[:, 0, :], lv[:, 1, :], ALU.mult)
            k.tt(prod[:, 1, :], lv[:, 2, :], lv[:, 3, :], ALU.mult)
            lsum = k.sb("lsum", [128, 2], F32, es)
            k.op("dve", "tensor_reduce", out=lsum, in_=prod, axis=AX.X, op=ALU.add)
            lexp = k.sb("lexp", [128, 2], F32, es)
            k.act(lexp, lsum, AF.Exp)
            nlam = k.sb("nlam", [128, 1], F32, es)
            k.tt(nlam, lexp[:, 1:2], lexp[:, 0:1], ALU.subtract)
            k.ts(nlam, nlam, -lam_init, ALU.add)
            gb = k.sb("dag", [128, 128], F32, es)
            k.dma("sp", gb, W["da_norm_g"][l].unsqueeze(0).broadcast_to([128, 128]))
            k.ts(gb, gb, 1.0 - lam_init, ALU.mult)

            qr = Ring([k.sb(f"aq{i}", [128, TT], BF16, es) for i in range(2)])
            kr = Ring([k.sb(f"ak{i}", [128, TT], BF16, es) for i in range(2)])
            vr = Ring([k.sb(f"av{i}", [128, NT, 129], BF16, es) for i in range(2)])
            for vt in vr.items:
                k.memset(vt[:, :, 128:129], 1.0)
            yr = Ring([k.sb(f"ayT{i}", [128, TT], BF16, es) for i in range(2)])
            PT = [k.sb(f"aPT{m}", [128, NT, 512], BF16, es) for m in range(2)]
            pscore = Ring([k.ps(f"apsc{i}", [128, 512], F32, es) for i in range(2)])
            pacc = Ring([k.ps(f"apac{i}", [128, 512], F32, es) for i in range(4)])
            ptr_ = Ring([k.ps(f"aptr{i}", [128, 128], BF16, es) for i in range(2)])
            sm = Ring([k.sb(f"asm{i}", [128, 4], F32, es) for i in range(3)])
            y1r = Ring([k.sb(f"ay1{i}", [128, 128], F32, es) for i in range(2)])
            y2r = Ring([k.sb(f"ay2{i}", [128, 128], F32, es) for i in range(2)])
            jr = Ring([k.sb(f"ajk{i}", [128, 128], F32, es) for i in range(2)])
            ynr = Ring([k.sb(f"ayn{i}", [128, 128], BF16, es) for i in range(2)])
            groups = []
            if not last:
                groups += [(q0, n, NTc) for (q0, n) in blocks(Tc, 512)]
            groups += [(Tc + q0, n, NT) for (q0, n) in blocks(T, 512)]
            for b in range(NB):
                for h in range(DA_H):
                    qT, kT, Vh, yT = qr.next(), kr.next(), vr.next(), yr.next()
                    k.dma("sp", qT, S["qk"][b, h * 128:(h + 1) * 128, :])
                    k.dma("sp", kT, S["qk"][b, 512 + h * 128:512 + (h + 1) * 128, :])
                    k.dma("sp", Vh[:, :, 0:128], S["v"][b, :, h * 128:(h + 1) * 128].rearrange("(j p) e -> p j e", p=128))
                    for (q0, nq, nk) in groups:
                        if DBG_AT < 2:
                            continue
                        for m in range(2):
                            for j in range(nk):
                                ps = pscore.next()
                                k.mm(ps[:, 0:nq], kT[m * 64:(m + 1) * 64, j * 128:(j + 1) * 128],
                                     qT[m * 64:(m + 1) * 64, q0:q0 + nq])
                                k.act(PT[m][:, j, 0:nq], ps[:, 0:nq], AF.Exp)
                        for s_ in range(nq // 128):
                            if DBG_AT < 3:
                                continue
                            accs = []
                            for m in range(2):
                                acc = pacc.next()
                                for j in range(nk):
                                    k.mm(acc[:, 0:129], PT[m][:, j, s_ * 128:(s_ + 1) * 128], Vh[:, j, :],
                                         start=(j == 0), stop=(j == nk - 1))
                                accs.append(acc)
                            if DBG_AT < 4:
                                continue
                            t = sm.next()
                            k.op("dve", "reciprocal", out=t[:, 0:1], in_=accs[0][:, 128:129])
                            k.op("dve", "reciprocal", out=t[:, 1:2], in_=accs[1][:, 128:129])
                            k.tt(t[:, 1:2], t[:, 1:2], nlam, ALU.mult)
                            y1 = y1r.next()
                            k.ts(y1, accs[0][:, 0:128], t[:, 0:1], ALU.mult)
                            y2 = y2r.next()
                            k.stt(y2, accs[1][:, 0:128], t[:, 1:2], y1, ALU.mult, ALU.add)
                            jk = jr.next()
                            k.act(jk, y2, AF.Square, accum_out=t[:, 2:3])
                            k.ts(t[:, 2:3], t[:, 2:3], 1.0 / 128, ALU.mult, DA_EPS, ALU.add)
                            k.act(t[:, 2:3], t[:, 2:3], AF.Sqrt)
                            k.op("dve", "reciprocal", out=t[:, 3:4], in_=t[:, 2:3])
                            yn = ynr.next()
                            k.stt(yn, y2, t[:, 3:4], gb, ALU.mult, ALU.mult)
                            if DBG_AT < 5:
                                continue
                            pt = ptr_.next()
                            k.tr(pt, yn, ident_b)
                            c0 = q0 + s_ * 128
                            k.copy(yT[:, c0:c0 + 128], pt, eng="act")
                    lo = groups[0][0]
                    if DBG_AT >= 6:
                        k.dma("act", S["yT"][b, h * 128:(h + 1) * 128, lo:TT], yT[:, lo:TT])
        k.end_phase()

    def phase_gmlp(l):
        k.begin_phase()
        with contextlib.ExitStack() as es:
            wsT = k.sb("wsT", [128, 4, 128], BF16, es)
            with contextlib.ExitStack() as es2:
                ws_f = k.sb("ws_f", [128, 4, 128], F32, es2)
                k.dma("sp", ws_f, W["sg_w"][l].rearrange("g p q -> p g q"))
                pw = k.ps("sgpw", [128, 4, 128], F32, es2)
                for g in range(4):
                    k.tr(pw[:, g, :], ws_f[:, g, :], ident_f)
                k.copy(wsT, pw)
                k.barrier()
            bsT = k.sb("bsT", [128, 4], F32, es)
            k.dma("sp", bsT, W["sg_b"][l].rearrange("g p -> p g"), allow_slow_non_contiguous=True)
            ngb = k.sb("sgng", [128, 512], F32, es)
            nbb = k.sb("sgnb", [128, 512], F32, es)
            k.dma("sp", ngb, W["sg_norm_g"][l].unsqueeze(0).broadcast_to([128, 512]))
            k.dma("sp", nbb, W["sg_norm_b"][l].unsqueeze(0).broadcast_to([128, 512]))
            sgr = Ring([k.sb(f"sgt{i}", [128, 1024], F32, es) for i in range(3)])
            str_ = Ring([k.sb(f"sgst{i}", [128, 8], F32, es) for i in range(3)])
            vnr = Ring([k.sb(f"sgvn{i}", [128, 512], F32, es) for i in range(2)])
            vbr = Ring([k.sb(f"sgvb{i}", [128, 512], BF16, es) for i in range(2)])
            ysr = Ring([k.sb(f"sgy{i}", [128, 512], BF16, es) for i in range(2)])
            yTr = Ring([k.sb(f"sgyT{i}", [128, 4, 128], BF16, es) for i in range(2)])
            pmr = Ring([k.ps(f"sgpm{i}", [128, 512], F32, es) for i in range(2)])
            ptr_ = Ring([k.ps(f"sgpt{i}", [128, 4, 128], BF16, es) for i in range(2)])
            for b in range(NB):
                for i in range(NT):
                    if l == DEPTH - 1 and i < NTc:
                        continue
                    sg = sgr.next()
                    k.dma("sp", sg, S["sg"][b, i * 128:(i + 1) * 128, :])
                    st = str_.next()
                    k.op("dve", "bn_stats", out=st[:, 0:6], in_=sg[:, 512:1024])
                    k.op("dve", "bn_aggr", out=st[:, 6:8], in_=st[:, 0:6])
                    k.ts(st[:, 7:8], st[:, 7:8], LN_EPS, ALU.add)
                    k.act(st[:, 7:8], st[:, 7:8], AF.Sqrt)
                    k.op("dve", "reciprocal", out=st[:, 7:8], in_=st[:, 7:8])
                    vn = vnr.next()
                    k.ts(vn, sg[:, 512:1024], st[:, 6:7], ALU.subtract, st[:, 7:8], ALU.mult)
                    k.tt(vn, vn, ngb, ALU.mult, eng="pool")
                    vb = vbr.next()
                    k.tt(vb, vn, nbb, ALU.add)
                    pm = pmr.next()
                    for g in range(4):
                        k.mm(pm[:, g * 128:(g + 1) * 128], wsT[:, g, :], vb[:, g * 128:(g + 1) * 128])
                    ys = ysr.next()
                    for g in range(4):
                        k.stt(ys[:, g * 128:(g + 1) * 128], pm[:, g * 128:(g + 1) * 128], bsT[:, g:g + 1],
                              sg[:, g * 128:(g + 1) * 128], ALU.add, ALU.mult)
                    pt = ptr_.next()
                    for g in range(4):
                        k.tr(pt[:, g, :], ys[:, g * 128:(g + 1) * 128], ident_b)
                    yT = yTr.next()
                    k.copy(yT, pt, eng="act")
                    k.dma("act", S["yT"][b, 1024:1536, i * 128:(i + 1) * 128].rearrange("(g c) t -> c g t", c=128), yT)
        k.end_phase()

    def phase_rwkv(l):
        k.begin_phase()
        last = (l == DEPTH - 1)
        H = lambda v: v.rearrange("p (h c) -> p h c", h=8)
        with contextlib.ExitStack() as es:
            cg = k.group("rwc")
            def bcast(name, src, n):
                t = k.sb(name, [128, n], F32, es)
                k.dma("sp", t, src.unsqueeze(0).broadcast_to([128, n]), grp=cg)
                return t
            mu0b = bcast("rmu0", W["rw_shift_mu"][l, 0], RW_COLS)
            mu1b = bcast("rmu1", W["rw_shift_mu"][l, 1], RW_COLS)
            w0b = [bcast(f"rw0b{z}", W["rw_w0"][l, z], 512) for z in range(2)]
            a0b = [bcast(f"ra0b{z}", W["rw_a0"][l, z], 512) for z in range(2)]
            kkb = bcast("rkkb", W["rw_k_k"][l], 512)
            kab = bcast("rkab", W["rw_k_a"][l], 512)
            omkab = k.sb("romka", [128, 512], F32, es)
            rkb = bcast("rrkb", W["rw_r_k"][l].rearrange("h c -> (h c)"), 512)
            lngb = bcast("rlng", W["rw_ln_g"][l], 512)
            lnbb = bcast("rlnb", W["rw_ln_b"][l], 512)
            w2 = [k.sb(f"rw2_{z}", [64, 512], F32, es) for z in range(2)]
            a2 = [k.sb(f"ra2_{z}", [64, 512], F32, es) for z in range(2)]
            for z in range(2):
                k.dma("sp", w2[z], W["rw_w2"][l, z], grp=cg)
                k.dma("sp", a2[z], W["rw_a2"][l, z], grp=cg)
            g2 = k.sb("rg2", [128, 512], F32, es)
            k.dma("sp", g2, W["rw_g2"][l], grp=cg)
            ones = k.sb("rones", [128, 1], F32, es)
            tri, mc1, mc2, mnsl = [], [], [], []
            for z in range(2):
                t = k.sb(f"rtri{z}", [128, 128], F32, es); k.dma("sp", t, CONST["tri"][z], grp=cg); tri.append(t)
                t = k.sb(f"rmc1{z}", [128, 256], F32, es)
                k.dma("sp", t[:, 0:128], CONST["m_su"][z], grp=cg); k.dma("sp", t[:, 128:256], CONST["m_iu"][z], grp=cg); mc1.append(t)
                t = k.sb(f"rmc2{z}", [128, 256], F32, es)
                k.dma("sp", t[:, 0:128], CONST["m_nsu"][z], grp=cg); k.dma("sp", t[:, 128:256], CONST["m_iu"][z], grp=cg); mc2.append(t)
                t = k.sb(f"rmnsl{z}", [128, 128], F32, es); k.dma("sp", t, CONST["m_nsl"][z], grp=cg); mnsl.append(t)

            k.memset(ones, 1.0)
            k.ts(omkab, kab, -1.0, ALU.mult, 1.0, ALU.add)
            cur = k.sb("rcur", [128, RW_COLS], F32, es)
            prv = k.sb("rprv", [128, RW_COLS], F32, es)
            nxt = k.sb("rnxt", [128, RW_COLS], F32, es)
            psh = k.sb("rpsh", [128, RW_COLS], F32, es)
            sm3 = k.sb("rsm3", [128, 256], F32, es)
            smT = k.sb("rsmT", [128, 3, 128], F32, es)
            F = {n: k.sb("rf_" + n, [128, 512], F32, es) for n in
                 ("lw", "a", "kkp", "kk", "kd", "bb", "epos", "eneg", "eposx", "t0", "t1")}
            for n in ("kap", "rt", "kt", "bt", "vb"):
                F[n] = k.sb("rf_" + n, [128, 512], BF16, es)
            ss8 = k.sb("rss8", [128, 8], F32, es)
            gC = k.sb("rgC", [64, 8], F32, es)
            KR = k.sb("rKR", [128, 8, 2, 128], BF16, es)
            k.memset(KR, 0.0)
            KT = k.sb("rKT", [64, 8, 128], BF16, es)
            BT = k.sb("rBT", [64, 8, 128], BF16, es)
            A = [[k.sb(f"rA{b}_{h}", [128, 64], F32, es) for h in range(8)] for b in range(NB)]
            Ab = [[k.sb(f"rAb{b}_{h}", [128, 64], BF16, es) for h in range(8)] for b in range(NB)]
            ytile = k.sb("rytile", [128, 512], F32, es)
            y0t = k.sb("ry0t", [128, 512], F32, es)
            aux = k.sb("raux", [128, 1024], F32, es)
            gst = k.sb("rgst", [128, 16], F32, es)
            yb = k.sb("ryb", [128, 512], BF16, es)
            yTs = k.sb("ryTs", [128, 4, 128], BF16, es)
            pwa_b = k.ps("rBf0", [128, 512], F32, es)
            pwa = Ring([pwa_b])
            Bf1 = k.ps("rBf1", [128, 512], F32, es)
            ptS = Bf1[:, 0:384].rearrange("p (a t) -> p a t", a=3)
            pgc = Bf1[0:64, 384:392]
            ptF = Bf1
            Bb = Bf1.bitcast(BF16)[:, 0:512].rearrange("p (a t) -> p a t", a=4)
            lanes = []
            NLANE = 3
            for ln in range(NLANE):
                Y = k.ps(f"rY{ln}", [128, 512], F32, es)
                Sb = k.ps(f"rS{ln}", [128, 512], F32, es)
                lanes.append(dict(
                    ps1=Y[:, 0:256], ps2=Y[:, 256:512], p5=Y[:, 0:256], p6=Y[:, 256:384],
                    pWT=Sb[0:64, 0:128], ps3=Sb[:, 128:256], ps4=Sb[:, 256:320], psz=Sb[:, 320:384],
                    psy=Sb[:, 384:448], psa=Sb[0:64, 448:512],
                    tmpA=k.sb(f"rtA{ln}", [64, 64], F32, es),
                    lmk=k.sb(f"rLMk{ln}", [128, 256], BF16, es),
                    mbt=k.sb(f"rMbT{ln}", [128, 128], BF16, es),
                    PT=Ring([k.sb(f"rPT{ln}_{i}", [128, 128], F32, es) for i in range(2)]),
                    PR=Ring([k.sb(f"rPR{ln}_{i}", [128, 256], F32, es) for i in range(2)]),
                    WT=k.sb(f"rWT{ln}", [64, 128], BF16, es),
                    Z=k.sb(f"rZ{ln}", [128, 64], BF16, es)))
            for b in range(NB):
                for h in range(8):
                    k.memset(A[b][h], 0.0)
                    k.memset(Ab[b][h], 0.0)

            def features(b, i, z):
                r0 = rw_row(cfg, i * 128)
                k.dma("sp", cur, S["rw"][b, r0:r0 + 128, :])
                k.dma("sp", prv, S["rw"][b, r0 - 1:r0 + 127, :])
                k.dma("sp", nxt, S["rw"][b, r0 + 1:r0 + 129, :])
                k.tt(prv, prv, cur, ALU.subtract, eng="pool")
                k.tt(prv, prv, mu0b, ALU.mult, eng="pool")
                k.tt(nxt, nxt, cur, ALU.subtract)
                k.tt(nxt, nxt, mu1b, ALU.mult)
                k.tt(psh, cur, prv, ALU.add, eng="pool")
                k.tt(psh, psh, nxt, ALU.add)
                r_, k_, v_ = psh[:, 0:512], psh[:, 512:1024], psh[:, 1024:1536]
                k.act(sm3[:, 0:64], psh[:, 1536:1600], AF.Tanh)
                k.act(sm3[:, 128:256], psh[:, 1664:1792], AF.Sigmoid)
                k.tr(ptS[0:64, 0, :], sm3[:, 0:64], ident_f)
                k.tr(ptS[0:64, 1, :], psh[:, 1600:1664], ident_f)
                k.tr(ptS[:, 2, :], sm3[:, 128:256], ident_f)
                k.copy(smT[0:64, 0:2, :], ptS[0:64, 0:2, :], eng="act")
                k.copy(smT[:, 2, :], ptS[:, 2, :], eng="act")
                lw, a_, kkp, kk, kd, bb = F["lw"], F["a"], F["kkp"], F["kk"], F["kd"], F["bb"]
                t0, t1 = F["t0"], F["t1"]
                pw = pwa.next()
                k.mm(pw, smT[0:64, 0, :], w2[z])
                k.tt(t0, pw, w0b[z], ALU.add)
                k.act(lw, t0, AF.Sigmoid)
                k.ts(lw, lw, -math.exp(-0.5), ALU.mult)
                pa = pwa.next()
                k.mm(pa, smT[0:64, 1, :], a2[z])
                k.tt(t0, pa, a0b[z], ALU.add)
                k.act(a_, t0, AF.Sigmoid)
                k.tt(kkp, k_, kkb, ALU.mult, eng="pool")
                k.tt(t1, kkp, kkp, ALU.mult, eng="pool")
                k.op("dve", "tensor_reduce", out=ss8, in_=H(t1), axis=AX.X, op=ALU.add)
                k.act(ss8, ss8, AF.Sqrt)
                k.ts(ss8, ss8, 1e-12, ALU.max)
                k.op("dve", "reciprocal", out=ss8, in_=ss8)
                k.tt(H(kk), H(kkp), ss8.unsqueeze(2).to_broadcast([128, 8, 64]), ALU.mult)
                k.tt(t0, a_, kab, ALU.mult)
                k.tt(t0, t0, omkab, ALU.add)
                k.tt(kd, k_, t0, ALU.mult)
                k.tt(bb, a_, kk, ALU.mult, eng="pool")
                pc = pwa.next()
                k.mm(pc, tri[z], lw)
                k.act(F["epos"], pc, AF.Exp)
                k.act(F["eneg"], pc, AF.Exp, scale=-1.0)
                k.tt(t0, pc, lw, ALU.subtract)
                k.act(F["eposx"], t0, AF.Exp)
                k.copy(F["vb"], v_, eng="act")
                k.tt(F["kap"], kk, F["eposx"], ALU.mult)
                k.tt(F["rt"], r_, F["epos"], ALU.mult, eng="pool")
                k.tt(F["kt"], kd, F["eneg"], ALU.mult)
                k.tt(F["bt"], bb, F["eneg"], ALU.mult, eng="pool")
                if DBG_RW < 2:
                    return v_
                for h in range(8):
                    k.mm(pgc[:, h:h + 1], lw[:, h * 64:(h + 1) * 64], ones)
                k.act(gC, pgc, AF.Exp)
                if DBG_RW < 3:
                    return v_
                if z == 0:
                    pa = pwa.next()
                    k.mm(pa, smT[0:64, 1, :], a2[1])
                    k.tt(t0, pa, a0b[1], ALU.add)
                    k.act(t1, t0, AF.Sigmoid)
                    k.tt(t1, t1, a_, ALU.add)
                    k.ts(t1, t1, 0.5, ALU.mult)
                    k.tt(t1, t1, kab, ALU.mult)
                    k.tt(t1, t1, omkab, ALU.add)
                    k.tt(t1, t1, k_, ALU.mult)
                    k.tt(t1, t1, r_, ALU.mult)
                    k.tt(t1, t1, rkb, ALU.mult)
                    k.op("dve", "tensor_reduce", out=ss8, in_=H(t1), axis=AX.X, op=ALU.add)
                    k.tt(H(aux[:, 0:512]), H(v_), ss8.unsqueeze(2).to_broadcast([128, 8, 64]), ALU.mult)
                    pg = pwa.next()
                    k.mm(pg, smT[:, 2, :], g2)
                    k.copy(aux[:, 512:1024], pg, eng="act")
                    k.dma("act", S["rwaux"][b, i * 128:(i + 1) * 128, :], aux)
                if DBG_RW < 4:
                    return v_
                ptFb = ptF.bitcast(BF16)
                ptFv = ptFb[0:64, 0:512].rearrange("p (a c t) -> p a c t", a=2, c=2)
                for hp in range(4):
                    for hh in range(2):
                        h = hp * 2 + hh
                        k.tr(ptFv[:, hh, 0, :], F["kap"][:, h * 64:(h + 1) * 64], ident_b)
                        k.tr(ptFv[:, hh, 1, :], F["rt"][:, h * 64:(h + 1) * 64], ident_b)
                    k.copy(KR[0:64, hp * 2:hp * 2 + 2, :, :], ptFv, eng="act")
                ptF4 = ptFb[0:64, 0:512].rearrange("p (a t) -> p a t", a=4)
                for (src, dstT) in ((F["kt"], KT), (F["bt"], BT)):
                    for hq in range(2):
                        for hh in range(4):
                            h = hq * 4 + hh
                            k.tr(ptF4[:, hh, :], src[:, h * 64:(h + 1) * 64], ident_b)
                        k.copy(dstT[:, hq * 4:(hq + 1) * 4, :], ptF4, eng="act")
                return F["vb"]

            def head_chunk(L, b, z, h, v_, emit):
                hs = slice(h * 64, (h + 1) * 64)
                Vh = v_[:, hs]
                KRh = KR[0:64, h, :, :].rearrange("p a t -> p (a t)")
                lmk, mbt = L["lmk"], L["mbt"]
                k.mm(L["ps1"], KT[:, h, :], KRh)
                k.mm(L["ps2"], BT[:, h, :], KRh)
                k.tt(lmk, L["ps1"], mc1[z], ALU.mult)
                PT = L["PT"].next()
                k.tt(PT, L["ps2"][:, 0:128], mc2[z][:, 0:128], ALU.mult)
                k.tt(mbt, L["ps2"][:, 128:256], mc2[z][:, 128:256], ALU.mult)
                yield
                if DBG_RW < 6:
                    return
                k.mm(L["ps3"], KR[0:64, h, 0, :], BT[:, h, :])
                k.mm(L["ps4"], lmk[:, 0:128], Vh)
                PR = L["PR"].next()
                k.tt(PR[:, 0:128], L["ps3"], mnsl[z], ALU.mult)
                k.copy(PR[:, 128:192], F["kap"][:, hs], eng="pool")
                k.copy(PR[:, 192:256], L["ps4"], eng="act")
                yield
                if DBG_RW < 7:
                    return
                for it in range(7):
                    k.mm(L["p5"], PT, PR)
                    PRn = L["PR"].next()
                    if it < 6:
                        k.mm(L["p6"], PR[:, 0:128], PT)
                        PTn = L["PT"].next()
                        k.copy(PTn, L["p6"], eng="act")
                        k.copy(PRn[:, 0:128], L["p5"][:, 0:128], eng="act")
                    k.tt(PRn[:, 128:256], PR[:, 128:256], L["p5"][:, 128:256], ALU.add)
                    PR = PRn
                    if it < 6:
                        PT = PTn
                    yield
                if DBG_RW < 8.1:
                    return
                k.tr(L["pWT"], PR[:, 128:192], ident_f)
                k.copy(L["WT"], L["pWT"], eng="act")
                yield
                if DBG_RW < 8.2:
                    return
                Ah = A[b][h]
                Abh = Ab[b][h]
                k.mm(L["psz"], L["WT"], Abh[0:64, :])
                k.stt(L["Z"], PR[:, 192:256], -1.0, L["psz"], ALU.mult, ALU.subtract)
                yield
                if DBG_RW < 8.4:
                    return
                if DBG_RW < 8.6:
                    emit = False
                if emit and DBG_RW < 8.63:
                    k.mm(L["psy"], mbt, L["Z"], start=True, stop=False)
                    k.mm(L["psy"], lmk[:, 128:256], Vh, start=False, stop=True)
                elif emit:
                    k.mm(L["psy"], KR[:, h, 1, :], Abh, start=True, stop=False)
                    k.mm(L["psy"], mbt, L["Z"], start=False, stop=False)
                    k.mm(L["psy"], lmk[:, 128:256], Vh, start=False, stop=True)
                k.mm(L["psa"], F["bt"][:, hs], L["Z"], start=True, stop=False)
                k.mm(L["psa"], F["kt"][:, hs], Vh, start=False, stop=True)
                if emit and DBG_RW >= 8.65:
                    k.copy(ytile[:, hs], L["psy"], eng="dve")
                k.tt(L["tmpA"], L["psa"], Ah[0:64, :], ALU.add)
                k.ts(Ah[0:64, :], L["tmpA"], gC[:, h:h + 1], ALU.mult)
                k.copy(Abh[0:64, :], Ah[0:64, :], eng="act")
                yield

            def run_heads(b, z, v_, emit):
                for h0 in range(0, 8, NLANE):
                    nh = min(NLANE, 8 - h0)
                    gens = [head_chunk(lanes[j], b, z, h0 + j, v_, emit) for j in range(nh)]
                    alive = [True] * nh
                    while any(alive):
                        for j in range(nh):
                            if alive[j]:
                                try:
                                    next(gens[j])
                                except StopIteration:
                                    alive[j] = False

            def finalize(b, i):
                k.dma("sp", y0t, S["y0"][b, i * 128:(i + 1) * 128, :])
                k.dma("sp", aux, S["rwaux"][b, i * 128:(i + 1) * 128, :])
                t0, t1 = F["t0"], F["t1"]
                k.tt(t0, ytile, y0t, ALU.add)
                k.op("dve", "tensor_reduce", out=gst[:, 0:8], in_=H(t0), axis=AX.X, op=ALU.add)
                k.ts(gst[:, 0:8], gst[:, 0:8], 1.0 / 64, ALU.mult)
                k.tt(H(t0), H(t0), gst[:, 0:8].unsqueeze(2).to_broadcast([128, 8, 64]), ALU.subtract)
                k.tt(t1, t0, t0, ALU.mult, eng="pool")
                k.op("dve", "tensor_reduce", out=gst[:, 8:16], in_=H(t1), axis=AX.X, op=ALU.add)
                k.ts(gst[:, 8:16], gst[:, 8:16], 1.0 / 64, ALU.mult, RW_GN_EPS, ALU.add)
                k.act(gst[:, 8:16], gst[:, 8:16], AF.Sqrt)
                k.op("dve", "reciprocal", out=gst[:, 8:16], in_=gst[:, 8:16])
                k.tt(H(t0), H(t0), gst[:, 8:16].unsqueeze(2).to_broadcast([128, 8, 64]), ALU.mult)
                k.tt(t0, t0, lngb, ALU.mult, eng="pool")
                k.tt(t0, t0, lnbb, ALU.add)
                k.tt(t0, t0, aux[:, 0:512], ALU.add, eng="pool")
                k.tt(yb, t0, aux[:, 512:1024], ALU.mult)
                for g in range(4):
                    k.tr(Bb[:, g, :], yb[:, g * 128:(g + 1) * 128], ident_b)
                k.copy(yTs, Bb, eng="act")
                k.dma("act", S["yT"][b, 512:1024, i * 128:(i + 1) * 128].rearrange("(g c) t -> c g t", c=128), yTs)

            for z in range(2):
                if z == 0:
                    order = list(range(NT))
                else:
                    order = list(range(NTc - 1, -1, -1)) + list(range(NT - 1, NTc - 1, -1))
                    for b in range(NB):
                        for h in range(8):
                            k.memset(A[b][h], 0.0)
                            k.memset(Ab[b][h], 0.0)
                for i in order:
                    emit = not (last and i < NTc)
                    for b in range(NB):
                        v_ = features(b, i, z)
                        if DBG_RW < 5:
                            continue
                        run_heads(b, z, v_, emit)
                        if DBG_RW < 10:
                            continue
                        if emit:
                            if z == 0:
                                k.dma("act", S["y0"][b, i * 128:(i + 1) * 128, :], ytile)
                            else:
                                finalize(b, i)
        k.end_phase()

    def phase_merge(l):
        k.begin_phase()
        last = (l == DEPTH - 1)
        lo = Tc if last else 0
        with contextlib.ExitStack() as es:
            wbr = k.sb("wbr", [128, 12, D], BF16, es)
            k.dma("pool", wbr, W["w_branch"][l].rearrange("(k p) d -> p k d", p=128))
            wo = k.sb("wo", [128, 8, D], BF16, es)
            k.dma("pool", wo, W["w_out"][l].rearrange("(k p) d -> p k d", p=128))
            wrt = k.sb("wrt", [128, 8, NE], F32, es)
            k.dma("sp", wrt, W["w_router"][l].rearrange("(k p) e -> p k e", p=128))
            lg = k.sb("ln1g", [128, D], F32, es)
            lb = k.sb("ln1b", [128, D], F32, es)
            k.dma("sp", lg, W["ln1_g"][l].unsqueeze(0).broadcast_to([128, D]))
            k.dma("sp", lb, W["ln1_b"][l].unsqueeze(0).broadcast_to([128, D]))
            bc = {}
            for nm, slot in (("g1", 2), ("sh2", 3), ("sc2", 4)):
                for which in range(2):
                    bc[nm, which] = k.sb(f"bc_{nm}{which}", [128, D], F32, es)
            affT = k.sb("affT", [NE, TT], F32, es)
            yTr = Ring([k.sb(f"myT{i}", [128, 12, 256], BF16, es) for i in range(2)])
            gTr = Ring([k.sb(f"mgT{i}", [128, 24, 256], BF16, es) for i in range(2)])
            mTr = Ring([k.sb(f"mmT{i}", [128, 8, 256], BF16, es) for i in range(2)])
            maccr = Ring([k.sb(f"macc{i}", [128, 256], F32, es) for i in range(2)])
            mtmpr = Ring([k.sb(f"mtmp{i}", [128, 256], F32, es) for i in range(2)])
            xr = Ring([k.sb(f"mx{i}", [128, D], F32, es) for i in range(2)])
            zr = Ring([k.sb(f"mz{i}", [128, D], F32, es) for i in range(2)])
            xmr = Ring([k.sb(f"mxm{i}", [128, D], F32, es) for i in range(2)])
            h2r = Ring([k.sb(f"mh2{i}", [128, D], F32, es) for i in range(2)])
            h2br = Ring([k.sb(f"mh2b{i}", [128, D], BF16, es) for i in range(2)])
            h2Tr = Ring([k.sb(f"mh2T{i}", [128, 8, 128], F32, es) for i in range(2)])
            str_ = Ring([k.sb(f"mst{i}", [128, 16], F32, es) for i in range(3)])
            smr = Ring([k.sb(f"msm{i}", [128, 4], F32, es) for i in range(3)])
            er = Ring([k.sb(f"mex{i}", [128, NE], F32, es) for i in range(2)])
            pmm = Ring([k.ps(f"mpm{i}", [128, 512], F32, es) for i in range(2)])
            pmix = Ring([k.ps(f"mpx{i}", [128, 512], F32, es) for i in range(2)])
            ptr_ = Ring([k.ps(f"mpt{i}", [128, 4, 128], F32, es) for i in range(2)])
            plq = k.ps("mplq", [128, 512], F32, es)
            plg = plq[:, 0:NE]
            paf = plq[0:NE, 128:256]
            for nm, slot in (("g1", 2), ("sh2", 3), ("sc2", 4)):
                k.dma("sp", bc[nm, 1], S["mod"][l, NB, slot * D:(slot + 1) * D].unsqueeze(0).broadcast_to([128, D]))
            k.ts(bc["sc2", 1], bc["sc2", 1], 1.0, ALU.add)
            for b in range(NB):
                for nm, slot in (("g1", 2), ("sh2", 3), ("sc2", 4)):
                    k.dma("sp", bc[nm, 0], S["mod"][l, b, slot * D:(slot + 1) * D].unsqueeze(0).broadcast_to([128, D]))
                k.ts(bc["sc2", 0], bc["sc2", 0], 1.0, ALU.add)
                for (tb0, n) in [(lo + a, n_) for (a, n_) in blocks(TT - lo, 256)]:
                    yTb, gTb, mT = yTr.next(), gTr.next(), mTr.next()
                    k.dma("sp", yTb[:, :, 0:n], S["yT"][b, :, tb0:tb0 + n].rearrange("(k p) t -> p k t", p=128))
                    k.dma("sp", gTb[:, :, 0:n], S["gT"][b, :, tb0:tb0 + n].rearrange("(k p) t -> p k t", p=128))
                    for dc in range(8):
                        macc = maccr.next()
                        for br in range(3):
                            ps = pmm.next()
                            for kc in range(4):
                                k.mm(ps[:, 0:n], wbr[:, br * 4 + kc, dc * 128:(dc + 1) * 128], yTb[:, br * 4 + kc, 0:n],
                                     start=(kc == 0), stop=(kc == 3))
                            if br == 0:
                                k.tt(macc[:, 0:n], ps[:, 0:n], gTb[:, dc, 0:n], ALU.mult)
                            else:
                                tmp = mtmpr.next()
                                k.tt(tmp[:, 0:n], ps[:, 0:n], gTb[:, br * 8 + dc, 0:n], ALU.mult)
                                if br == 1:
                                    k.tt(macc[:, 0:n], macc[:, 0:n], tmp[:, 0:n], ALU.add, eng="pool")
                                else:
                                    k.tt(mT[:, dc, 0:n], macc[:, 0:n], tmp[:, 0:n], ALU.add, eng="pool")
                    for s_ in range(n // 128):
                        t0 = tb0 + s_ * 128
                        which = 1 if t0 < Tc else 0
                        xt = xr.next()
                        k.dma("sp", xt, S["xcur"][b, t0:t0 + 128, :])
                        z = zr.next()
                        for half in range(2):
                            px = pmix.next()
                            for kk in range(8):
                                k.mm(px, mT[:, kk, s_ * 128:(s_ + 1) * 128], wo[:, kk, half * 512:(half + 1) * 512],
                                     start=(kk == 0), stop=(kk == 7))
                            k.tt(z[:, half * 512:(half + 1) * 512], px, bc["g1", which][:, half * 512:(half + 1) * 512], ALU.mult)
                        k.stt(z, xt, DN_ALPHA, z, ALU.mult, ALU.add)
                        st = str_.next()
                        k.op("dve", "bn_stats", out=st[:, 0:6], in_=z[:, 0:512])
                        k.op("dve", "bn_stats", out=st[:, 6:12], in_=z[:, 512:1024])
                        k.op("dve", "bn_aggr", out=st[:, 12:14], in_=st[:, 0:12])
                        k.ts(st[:, 13:14], st[:, 13:14], LN_EPS, ALU.add)
                        k.act(st[:, 13:14], st[:, 13:14], AF.Sqrt)
                        k.op("dve", "reciprocal", out=st[:, 13:14], in_=st[:, 13:14])
                        xm = xmr.next()
                        k.ts(xm, z, st[:, 12:13], ALU.subtract, st[:, 13:14], ALU.mult)
                        k.tt(xm, xm, lg, ALU.mult, eng="pool")
                        k.tt(xm, xm, lb, ALU.add)
                        k.dma("act", S["xmid"][b, t0:t0 + 128, :], xm)
                        h2 = h2r.next()
                        k.tt(h2, xm, bc["sc2", which], ALU.mult, eng="pool")
                        k.tt(h2, h2, bc["sh2", which], ALU.add)
                        h2b = h2br.next()
                        k.copy(h2b, h2, eng="act")
                        k.dma("act", S["h2"][b, t0:t0 + 128, :], h2b)
                        h2T = h2Tr.next()
                        for g in range(2):
                            pt = ptr_.next()
                            for j in range(4):
                                k.tr(pt[:, j, :], h2[:, (g * 4 + j) * 128:(g * 4 + j + 1) * 128], ident_f)
                            k.copy(h2T[:, g * 4:(g + 1) * 4, :], pt, eng="act")
                        for kk in range(8):
                            k.mm(plg, h2T[:, kk, :], wrt[:, kk, :], start=(kk == 0), stop=(kk == 7))
                        sm = smr.next()
                        k.op("dve", "tensor_reduce", out=sm[:, 0:1], in_=plg, axis=AX.X, op=ALU.max)
                        k.ts(sm[:, 0:1], sm[:, 0:1], -1.0, ALU.mult)
                        e = er.next()
                        k.act(e, plg, AF.Exp, bias=sm[:, 0:1], accum_out=sm[:, 1:2])
                        k.op("dve", "reciprocal", out=sm[:, 2:3], in_=sm[:, 1:2])
                        k.ts(e, e, sm[:, 2:3], ALU.mult)
                        k.tr(paf, e, ident_f)
                        k.copy(affT[:, t0:t0 + 128], paf, eng="act")
                k.dma("sp", S["aff"][b, :, lo:TT], affT[:, lo:TT])
        k.end_phase()

    def moe_segs(l):
        segs = []
        if l != DEPTH - 1:
            segs.append((0, 0, Tc, cfg.cap_ctx))
        segs.append((1, Tc, T, cfg.cap_lat))
        return segs

    def phase_route(l):
        k.begin_phase()
        with contextlib.ExitStack() as es:
            iota = k.sb("iota", [128, 256], F32, es)
            k.dma("sp", iota, CONST["iota"])
            zeros = k.sb("rzeros", [NE, T], F32, es)
            k.memset(zeros, 0.0)
            aff = k.sb("raff", [NE, T], F32, es)
            work = k.sb("rwork", [NE, T], F32, es)
            G = k.sb("rG", [NE, T], F32, es)
            mask = k.sb("rmask", [NE, T], F32, es)
            slot = k.sb("rslot", [NE, T], F32, es)
            mx8 = k.sb("rmx8", [NE, 8], F32, es)
            ntm = T // 128
            slotT = k.sb("rslotT", [128, ntm, NE], F32, es)
            GT = k.sb("rGT", [128, ntm, NE], F32, es)
            h2sb = k.sb("rh2", [128, ntm, D], BF16, es)
            selr = Ring([k.sb(f"rsel{i}", [128, ntm, cfg.cap_lat], BF16, es) for i in range(2)])
            xer = Ring([k.sb(f"rxe{i}", [128, 8, cfg.cap_lat], BF16, es) for i in range(2)])
            ptq = k.ps("rptq", [128, 2, NE], F32, es)
            pgr = Ring([k.ps(f"rpg{i}", [128, 512], F32, es) for i in range(4)])
            for b in range(NB):
                for (sid, tlo, Ts, cap) in moe_segs(l):
                    nt = Ts // 128
                    k.dma("sp", aff[:, 0:Ts], S["aff"][b, :, tlo:tlo + Ts])
                    k.dma("sp", h2sb[:, 0:nt, :], S["h2"][b, tlo:tlo + Ts, :].rearrange("(j p) d -> p j d", p=128))
                    k.copy(work[:, 0:Ts], aff[:, 0:Ts])
                    for _ in range(cap // 8):
                        k.op("dve", "max", out=mx8, in_=work[:, 0:Ts])
                        k.op("dve", "match_replace", out=work[:, 0:Ts], in_to_replace=mx8, in_values=work[:, 0:Ts], imm_value=0.0)
                    k.tt(G[:, 0:Ts], aff[:, 0:Ts], work[:, 0:Ts], ALU.subtract)
                    k.ts(mask[:, 0:Ts], G[:, 0:Ts], 0.0, ALU.is_gt)
                    k.op("dve", "tensor_tensor_scan", out=slot[:, 0:Ts], data0=mask[:, 0:Ts], data1=zeros[:, 0:Ts],
                         initial=0.0, op0=ALU.add, op1=ALU.add)
                    k.tt(slot[:, 0:Ts], slot[:, 0:Ts], mask[:, 0:Ts], ALU.mult)
                    k.ts(slot[:, 0:Ts], slot[:, 0:Ts], -1.0, ALU.add)
                    for i in range(nt):
                        k.tr(ptq[:, 0, :], slot[:, i * 128:(i + 1) * 128], ident_f[0:NE, 0:NE])
                        k.tr(ptq[:, 1, :], G[:, i * 128:(i + 1) * 128], ident_f[0:NE, 0:NE])
                        k.copy(slotT[:, i, :], ptq[:, 0, :], eng="act")
                        k.copy(GT[:, i, :], ptq[:, 1, :], eng="act")
                    k.dma("act", S["slotT"][b, tlo:tlo + Ts, :].rearrange("(j p) e -> p j e", p=128), slotT[:, 0:nt, :])
                    k.dma("act", S["GT"][b, tlo:tlo + Ts, :].rearrange("(j p) e -> p j e", p=128), GT[:, 0:nt, :])
                    for e in range(NE):
                        sel = selr.next()
                        for i in range(nt):
                            k.ts(sel[:, i, 0:cap], iota[:, 0:cap], slotT[:, i, e:e + 1], ALU.is_equal,
                                 eng=("pool" if i % 2 else "dve"))
                        xe = xer.next()
                        for kk in range(8):
                            pg = pgr.next()
                            for i in range(nt):
                                k.mm(pg[:, 0:cap], h2sb[:, i, kk * 128:(kk + 1) * 128], sel[:, i, 0:cap],
                                     start=(i == 0), stop=(i == nt - 1))
                            k.copy(xe[:, kk, 0:cap], pg[:, 0:cap], eng=("act" if kk % 2 else "dve"))
                        k.dma("act", S["xe"][b, sid, e][:, 0:cap].rearrange("(k p) c -> p k c", p=128), xe[:, :, 0:cap])
        k.end_phase()

    def phase_experts(l):
        k.begin_phase()
        segs = moe_segs(l)
        with contextlib.ExitStack() as es:
            wgr = Ring([k.sb(f"ewg{i}", [128, 8, 1024], BF16, es) for i in range(2)])
            wur = Ring([k.sb(f"ewu{i}", [128, 8, 1024], BF16, es) for i in range(2)])
            wdr = Ring([k.sb(f"ewd{i}", [128, 8, 1024], BF16, es) for i in range(2)])
            xer = Ring([k.sb(f"exe{i}", [128, 8, cfg.cap_lat], BF16, es) for i in range(2)])
            hidr = Ring([k.sb(f"ehid{i}", [128, 8, cfg.cap_lat], BF16, es) for i in range(2)])
            sgr = Ring([k.sb(f"esg{i}", [128, cfg.cap_lat], F32, es) for i in range(2)])
            yacc = {}
            for b in range(NB):
                for (sid, tlo, Ts, cap) in segs:
                    yacc[b, sid] = k.sb(f"eya{b}_{sid}", [128, (cap + 127) // 128, D], F32, es)
            yor = Ring([k.sb(f"eyo{i}", [128, 512], BF16, es) for i in range(3)])
            pgr = Ring([k.ps(f"epg{i}", [128, 512], F32, es) for i in range(2)])
            pur = Ring([k.ps(f"epu{i}", [128, 512], F32, es) for i in range(2)])
            pyr = Ring([k.ps(f"epy{i}", [128, 512], F32, es) for i in range(3)])
            for e in range(NE):
                for half in range(2):
                    wg, wu, wd = wgr.next(), wur.next(), wdr.next()
                    f0 = half * 1024
                    k.dma("pool", wg, W["w_e_gate"][l, e][:, f0:f0 + 1024].rearrange("(k p) f -> p k f", p=128))
                    k.dma("pool", wu, W["w_e_up"][l, e][:, f0:f0 + 1024].rearrange("(k p) f -> p k f", p=128))
                    k.dma("pool", wd, W["w_e_down"][l, e][f0:f0 + 1024, :].rearrange("(k p) d -> p k d", p=128))
                    for b in range(NB):
                        for (sid, tlo, Ts, cap) in segs:
                            xe = xer.next()
                            k.dma("sp", xe[:, :, 0:cap], S["xe"][b, sid, e][:, 0:cap].rearrange("(k p) c -> p k c", p=128))
                            hid = hidr.next()
                            for fc in range(8):
                                pg, pu = pgr.next(), pur.next()
                                for kk in range(8):
                                    k.mm(pg[:, 0:cap], wg[:, kk, fc * 128:(fc + 1) * 128], xe[:, kk, 0:cap],
                                         start=(kk == 0), stop=(kk == 7))
                                for kk in range(8):
                                    k.mm(pu[:, 0:cap], wu[:, kk, fc * 128:(fc + 1) * 128], xe[:, kk, 0:cap],
                                         start=(kk == 0), stop=(kk == 7))
                                sg = sgr.next()
                                k.act(sg[:, 0:cap], pg[:, 0:cap], AF.Silu)
                                k.tt(hid[:, fc, 0:cap], sg[:, 0:cap], pu[:, 0:cap], ALU.mult)
                            for ch in range((cap + 127) // 128):
                                M = min(128, cap - ch * 128)
                                for dh in range(2):
                                    py = pyr.next()
                                    for fc in range(8):
                                        k.mm(py[0:M, :], hid[:, fc, ch * 128:ch * 128 + M], wd[:, fc, dh * 512:(dh + 1) * 512],
                                             start=(fc == 0), stop=(fc == 7))
                                    ya = yacc[b, sid][0:M, ch, dh * 512:(dh + 1) * 512]
                                    if half == 0:
                                        k.copy(ya, py[0:M, :], eng="act")
                                    else:
                                        yo = yor.next()
                                        k.tt(yo[0:M, :], ya, py[0:M, :], ALU.add)
                                        r0 = e * cap + ch * 128
                                        k.dma("act", S["ye"][b, sid, r0:r0 + M, dh * 512:(dh + 1) * 512], yo[0:M, :])
        k.end_phase()

    def phase_scatter(l):
        k.begin_phase()
        last = (l == DEPTH - 1)
        with contextlib.ExitStack() as es:
            iota = k.sb("siota", [128, 256], F32, es)
            k.dma("sp", iota, CONST["iota"])
            lg = k.sb("ln2g", [128, D], F32, es)
            lb = k.sb("ln2b", [128, D], F32, es)
            k.dma("sp", lg, W["ln2_g"][l].unsqueeze(0).broadcast_to([128, D]))
            k.dma("sp", lb, W["ln2_b"][l].unsqueeze(0).broadcast_to([128, D]))
            g2b = [k.sb(f"g2b{i}", [128, D], F32, es) for i in range(2)]
            nKmax = NE * cfg.cap_lat // 128
            ye = k.sb("sye", [128, nKmax, D], BF16, es)
            ntm = T // 128
            slotT = k.sb("sslotT", [128, ntm, NE], F32, es)
            GT = k.sb("sGT", [128, ntm, NE], F32, es)
            selr = Ring([k.sb(f"ssel{i}", [128, NE * cfg.cap_lat], BF16, es) for i in range(2)])
            selTr = Ring([k.sb(f"sselT{i}", [128, nKmax, 128], BF16, es) for i in range(2)])
            xr = Ring([k.sb(f"sx{i}", [128, D], F32, es) for i in range(2)])
            zr = Ring([k.sb(f"sz{i}", [128, D], F32, es) for i in range(2)])
            xor_ = Ring([k.sb(f"sxo{i}", [128, D], F32, es) for i in range(2)])
            str_ = Ring([k.sb(f"sst{i}", [128, 16], F32, es) for i in range(3)])
            ptr_ = Ring([k.ps(f"spt{i}", [128, 8, 128], BF16, es) for i in range(2)])
            pfr = Ring([k.ps(f"spf{i}", [128, 512], F32, es) for i in range(4)])
            k.dma("sp", g2b[1], S["mod"][l, NB, 5 * D:6 * D].unsqueeze(0).broadcast_to([128, D]))
            for b in range(NB):
                k.dma("sp", g2b[0], S["mod"][l, b, 5 * D:6 * D].unsqueeze(0).broadcast_to([128, D]))
                for (sid, tlo, Ts, cap) in moe_segs(l):
                    nt = Ts // 128
                    nK = NE * cap // 128
                    which = 1 if sid == 0 else 0
                    k.dma("sp", ye[:, 0:nK, :], S["ye"][b, sid, 0:NE * cap, :].rearrange("(j p) d -> p j d", p=128))
                    k.dma("sp", slotT[:, 0:nt, :], S["slotT"][b, tlo:tlo + Ts, :].rearrange("(j p) e -> p j e", p=128))
                    k.dma("sp", GT[:, 0:nt, :], S["GT"][b, tlo:tlo + Ts, :].rearrange("(j p) e -> p j e", p=128))
                    for i in range(nt):
                        t0 = tlo + i * 128
                        sel = selr.next()
                        for e in range(NE):
                            k.ts(sel[:, e * cap:(e + 1) * cap], iota[:, 0:cap], slotT[:, i, e:e + 1], ALU.is_equal,
                                 GT[:, i, e:e + 1], ALU.mult, eng=("pool" if e % 2 else "dve"))
                        selT = selTr.next()
                        for j0 in range(0, nK, 8):
                            pt = ptr_.next()
                            nj = min(8, nK - j0)
                            for j in range(nj):
                                k.tr(pt[:, j, :], sel[:, (j0 + j) * 128:(j0 + j + 1) * 128], ident_b)
                            k.copy(selT[:, j0:j0 + nj, :], pt[:, 0:nj, :], eng="act")
                        xt = xr.next()
                        k.dma("sp", xt, S["xmid"][b, t0:t0 + 128, :])
                        z = zr.next()
                        for dh in range(2):
                            pf = pfr.next()
                            for j in range(nK):
                                k.mm(pf, selT[:, j, :], ye[:, j, dh * 512:(dh + 1) * 512], start=(j == 0), stop=(j == nK - 1))
                            k.tt(z[:, dh * 512:(dh + 1) * 512], pf, g2b[which][:, dh * 512:(dh + 1) * 512], ALU.mult)
                        k.stt(z, xt, DN_ALPHA, z, ALU.mult, ALU.add)
                        st = str_.next()
                        k.op("dve", "bn_stats", out=st[:, 0:6], in_=z[:, 0:512])
                        k.op("dve", "bn_stats", out=st[:, 6:12], in_=z[:, 512:1024])
                        k.op("dve", "bn_aggr", out=st[:, 12:14], in_=st[:, 0:12])
                        k.ts(st[:, 13:14], st[:, 13:14], LN_EPS, ALU.add)
                        k.act(st[:, 13:14], st[:, 13:14], AF.Sqrt)
                        k.op("dve", "reciprocal", out=st[:, 13:14], in_=st[:, 13:14])
                        xo = xor_.next()
                        k.ts(xo, z, st[:, 12:13], ALU.subtract, st[:, 13:14], ALU.mult)
                        k.tt(xo, xo, lg, ALU.mult, eng="pool")
                        k.tt(xo, xo, lb, ALU.add)
                        if last:
                            k.dma("act", OUT[b, t0 - Tc:t0 - Tc + 128, :], xo)
                        else:
                            k.dma("act", S["xcur"][b, t0:t0 + 128, :], xo)
        k.end_phase()

    PH = phases
    for l in range(cfg.L):
        with contextlib.ExitStack() as les:
            modT = k.sb("modT", [128, 48, 3], F32, les)
            if PH is None or "mod" in PH:
                phase_mod(l, modT)
            if PH is None or "inproj" in PH:
                phase_inproj(l, modT)
            if PH is None or "attn" in PH:
                phase_attn(l)
            if PH is None or "rwkv" in PH:
                phase_rwkv(l)
            if PH is None or "gmlp" in PH:
                phase_gmlp(l)
            if PH is None or "merge" in PH:
                phase_merge(l)
            if PH is None or "moe" in PH:
                phase_route(l)
                phase_experts(l)
                phase_scatter(l)
        k.end_phase()
    outs = [OUT] + [S[n] for n in dbg_out]
    k.final_wait([o.tb for o in outs])
    top.close()
    return nc, hc, k


_CACHE = {}


def _get_program(cfg_key):
    if cfg_key not in _CACHE:
        cfg = Cfg(*cfg_key)
        nc, hc, k = build(cfg)
        _CACHE[cfg_key] = (cfg, nc, hc)
    return _CACHE[cfg_key]


def kernel(**inputs):
    x = np.asarray(inputs["x"], np.float32)
    B, T, _ = x.shape
    ctx = np.asarray(inputs["ctx"], np.float32)
    Tc = ctx.shape[1]
    n_cores = 8
    NB = B // n_cores
    L = inputs["w_in"].shape[0]
    cfg, nc, hc = _get_program((T, Tc, NB, L))
    c = np.asarray(inputs["c"], np.float32)
    c_ctx = np.asarray(inputs["c_ctx"], np.float32)
    shared = {n: np.ascontiguousarray(np.asarray(inputs[n], np.float32)) for n in WEIGHT_SPECS}
    for n, a in hc.items():
        shared["k_" + n] = a
    in_maps = []
    for ci in range(n_cores):
        sl = slice(ci * NB, (ci + 1) * NB)
        m = dict(shared)
        perm = np.roll(np.arange(NE), -ci)
        m["w_router"] = np.ascontiguousarray(shared["w_router"][:, :, perm])
        for n in ("w_e_gate", "w_e_up", "w_e_down"):
            m[n] = np.ascontiguousarray(shared[n][:, perm])
        m["x"] = np.ascontiguousarray(x[sl])
        m["ctx"] = np.ascontiguousarray(ctx[sl])
        m["c3"] = np.ascontiguousarray(np.concatenate([c[sl], c_ctx[None]], 0))
        in_maps.append(m)
    res = run_bass_kernel_spmd(nc, in_maps, core_ids=list(range(n_cores)))
    return np.concatenate([np.asarray(r["out"], np.float32) for r in res.results], axis=0)
```
